# Optimizing a Trainium2 kernel written in Bass

```python
import math
import jax
import jax.numpy as jnp
from jax import lax
import numpy as np

D_MODEL = 1024
BATCH = 1
SEQ = 16384
DEPTH = 4

N_META = 16
A_HEADS = 4
A_QK_DIM = 64
A_V_DIM = 2 * A_QK_DIM
A_QK_WIDTH = 2 * A_HEADS * A_QK_DIM
A_WIDTH = A_HEADS * A_V_DIM
Q_BLOCK = 128
REL_BUCKETS = 32
REL_MAX_DIST = 128
POOL_WINDOWS = (2, 4, 8, 16)
B_GROUPS = len(POOL_WINDOWS)
B_WIDTH = D_MODEL - A_WIDTH
B_GROUP_DIM = B_WIDTH // B_GROUPS
EVEN_IN = 2 * A_QK_WIDTH + A_WIDTH + B_WIDTH
C_HEADS = 8
C_HEAD_DIM = D_MODEL // C_HEADS
C_WIDTH = C_HEADS * C_HEAD_DIM
CONV_WIDTH = 4
CHUNK = 64
ODD_IN = 4 * C_WIDTH + 2 * C_HEADS
D_FF = 4 * D_MODEL
ALPHA = (2.0 * DEPTH) ** 0.25
BETA_INIT = (8.0 * DEPTH) ** -0.25
N_EVEN = (DEPTH + 1) // 2
N_ODD = DEPTH // 2
LN_EPS = 1e-5
RMS_EPS = 1e-6

kernel_name = 'hybrid_diffattn_pool_gdn_deepnorm'


def layer_norm(x, g, b):
    xf = x.astype(jnp.float32)
    mu = jnp.mean(xf, axis=-1, keepdims=True)
    xc = xf - mu
    var = jnp.mean(xc * xc, axis=-1, keepdims=True)
    y = xc * lax.rsqrt(var + LN_EPS) * g.astype(jnp.float32) + b.astype(jnp.float32)
    return y.astype(x.dtype)


def rms_norm(x, w):
    xf = x.astype(jnp.float32)
    return xf * lax.rsqrt(jnp.mean(xf * xf, axis=-1, keepdims=True) + RMS_EPS) * w.astype(jnp.float32)


def l2_normalize(x):
    xf = x.astype(jnp.float32)
    return xf * lax.rsqrt(jnp.sum(xf * xf, axis=-1, keepdims=True) + RMS_EPS)


def t5_causal_bucket(q_pos, k_pos):
    n = jnp.maximum(q_pos[:, None] - k_pos[None, :], 0)
    max_exact = REL_BUCKETS // 2
    nf = jnp.maximum(n, 1).astype(jnp.float32)
    large = max_exact + (jnp.log(nf / max_exact) / math.log(REL_MAX_DIST / max_exact)
                         * (REL_BUCKETS - max_exact)).astype(jnp.int32)
    large = jnp.minimum(large, REL_BUCKETS - 1)
    return jnp.where(n < max_exact, n, large)


def diff_attention(q, k, v, lam, rel_bias):
    bsz, heads, _, length, _ = q.shape
    n_blocks = -(-length // Q_BLOCK)
    padded = n_blocks * Q_BLOCK
    q_pad = jnp.pad(q, ((0, 0), (0, 0), (0, 0), (0, padded - length), (0, 0)))
    k_pos = jnp.arange(length)
    scale = A_QK_DIM ** -0.5

    def one_block(i):
        start = i * Q_BLOCK
        qb = lax.dynamic_slice_in_dim(q_pad, start, Q_BLOCK, axis=3)
        q_pos = start + jnp.arange(Q_BLOCK)
        bias = jnp.transpose(rel_bias[t5_causal_bucket(q_pos, k_pos)], (2, 0, 1)).astype(jnp.float32)
        s = jnp.einsum('bhcqd,bhckd->bhcqk', qb, k).astype(jnp.float32) * scale + bias[None, :, None]
        s = jnp.where(k_pos[None, :] <= q_pos[:, None], s, -jnp.inf)
        p = jax.nn.softmax(s, axis=-1)
        a = p[:, :, 0] - lam * p[:, :, 1]
        return jnp.einsum('bhqk,bhkd->bhqd', a.astype(v.dtype), v)

    o = lax.map(one_block, jnp.arange(n_blocks))
    o = jnp.transpose(o, (1, 0, 3, 2, 4)).reshape(bsz, padded, heads, A_V_DIM)
    return o[:, :length]


def pool_mixer(u, pool_w, pool_scale):
    bsz, length, _ = u.shape
    ug = u.astype(jnp.float32).reshape(bsz, length, B_GROUPS, B_GROUP_DIM)
    cs = jnp.cumsum(ug, axis=1)
    t = jnp.arange(length)
    outs = []
    for gi, win in enumerate(POOL_WINDOWS):
        c = cs[:, :, gi]
        prev = jnp.pad(c, ((0, 0), (win, 0), (0, 0)))[:, :length]
        cnt = jnp.minimum(t + 1, win).astype(jnp.float32)[None, :, None]
        outs.append((c - prev) / cnt - ug[:, :, gi])
    pooled = jnp.stack(outs, axis=2).astype(u.dtype)
    y = jnp.einsum('blgc,gcd->blgd', pooled, pool_w).reshape(bsz, length, B_WIDTH)
    return y * pool_scale


def even_mixer(x, w_in, lam_vecs, subln_w, pool_w, pool_scale, w_out, lambda_init, rel_bias):
    bsz, length, _ = x.shape
    h = x @ w_in
    q = h[..., :A_QK_WIDTH]
    k = h[..., A_QK_WIDTH:2 * A_QK_WIDTH]
    v = h[..., 2 * A_QK_WIDTH:2 * A_QK_WIDTH + A_WIDTH]
    u = h[..., 2 * A_QK_WIDTH + A_WIDTH:]
    q = q.reshape(bsz, length, A_HEADS, 2, A_QK_DIM).transpose(0, 2, 3, 1, 4)
    k = k.reshape(bsz, length, A_HEADS, 2, A_QK_DIM).transpose(0, 2, 3, 1, 4)
    v = v.reshape(bsz, length, A_HEADS, A_V_DIM).transpose(0, 2, 1, 3)
    lv = lam_vecs.astype(jnp.float32)
    lam = jnp.exp(jnp.sum(lv[0] * lv[1])) - jnp.exp(jnp.sum(lv[2] * lv[3])) + lambda_init
    o = diff_attention(q, k, v, lam, rel_bias)
    o = (rms_norm(o, subln_w) * (1.0 - lambda_init)).astype(x.dtype).reshape(bsz, length, A_WIDTH)
    y_pool = pool_mixer(u, pool_w, pool_scale)
    return jnp.concatenate([o, y_pool.astype(x.dtype)], axis=-1) @ w_out


def causal_depthwise_conv(x, w):
    ch = x.shape[-1]
    y = lax.conv_general_dilated(jnp.swapaxes(x, 1, 2), w[:, None, :].astype(x.dtype),
                                 window_strides=(1,), padding=[(CONV_WIDTH - 1, 0)],
                                 dimension_numbers=('NCH', 'OIH', 'NCH'), feature_group_count=ch)
    return jnp.swapaxes(y, 1, 2)


def gated_delta_chunked(q, k, v, g, beta):
    bsz, length, heads, dk = q.shape
    dv = v.shape[-1]
    lead = (-N_META) % CHUNK
    total = lead + length
    tail = (-total) % CHUNK
    n_chunks = (total + tail) // CHUNK

    def prep(t):
        t = jnp.moveaxis(t, 2, 1)
        pad = [(0, 0), (0, 0), (lead, tail)] + [(0, 0)] * (t.ndim - 3)
        t = jnp.pad(t, pad)
        return t.reshape(bsz, heads, n_chunks, CHUNK, *t.shape[3:])

    q, k, v, g, beta = (prep(t) for t in (q, k, v, g, beta))
    q = q * (dk ** -0.5)
    g = jnp.cumsum(g, axis=-1)
    idx = jnp.arange(CHUNK)
    causal = idx[:, None] >= idx[None, :]
    strict = idx[:, None] > idx[None, :]
    decay = jnp.exp(jnp.where(causal, g[..., :, None] - g[..., None, :], -jnp.inf))
    k_beta = k * beta[..., None]
    lower = jnp.where(strict, jnp.einsum('bhncd,bhnsd->bhncs', k_beta, k) * decay, 0.0)
    eye = jnp.eye(CHUNK, dtype=jnp.float32)
    rhs = jnp.concatenate([v * beta[..., None], k_beta * jnp.exp(g)[..., None]], axis=-1)
    sol = lax.linalg.triangular_solve(eye + lower, rhs, left_side=True, lower=True, unit_diagonal=True)
    u, w = sol[..., :dv], sol[..., dv:]
    qk = jnp.where(causal, jnp.einsum('bhncd,bhnsd->bhncs', q, k) * decay, 0.0)

    def step(state, inp):
        q_c, k_c, u_c, w_c, g_c, qk_c = inp
        v_new = u_c - jnp.einsum('bhck,bhkv->bhcv', w_c, state)
        o_c = (jnp.einsum('bhck,bhkv->bhcv', q_c * jnp.exp(g_c)[..., None], state)
               + jnp.einsum('bhcs,bhsv->bhcv', qk_c, v_new))
        g_last = g_c[..., -1]
        k_dec = k_c * jnp.exp(g_last[..., None] - g_c)[..., None]
        state = state * jnp.exp(g_last)[..., None, None] + jnp.einsum('bhck,bhcv->bhkv', k_dec, v_new)
        return state, o_c

    xs = tuple(jnp.moveaxis(t, 2, 0) for t in (q, k, u, w, g, qk))
    state0 = jnp.zeros((bsz, heads, dk, dv), jnp.float32)
    _, o = lax.scan(step, state0, xs)
    o = jnp.moveaxis(o, 0, 2).reshape(bsz, heads, n_chunks * CHUNK, dv)[:, :, lead:lead + length]
    return jnp.moveaxis(o, 1, 2)


def odd_mixer(x, w_in, conv_w, a_log, dt_bias, norm_w, w_out):
    bsz, length, _ = x.shape
    h = x @ w_in
    qkv = jax.nn.silu(causal_depthwise_conv(h[..., :3 * C_WIDTH], conv_w))
    z = h[..., 3 * C_WIDTH:4 * C_WIDTH]
    b_raw = h[..., 4 * C_WIDTH:4 * C_WIDTH + C_HEADS].astype(jnp.float32)
    a_raw = h[..., 4 * C_WIDTH + C_HEADS:].astype(jnp.float32)

    def heads(t):
        return t.reshape(bsz, length, C_HEADS, C_HEAD_DIM)

    q = l2_normalize(heads(qkv[..., :C_WIDTH]))
    k = l2_normalize(heads(qkv[..., C_WIDTH:2 * C_WIDTH]))
    v = heads(qkv[..., 2 * C_WIDTH:]).astype(jnp.float32)
    beta = jax.nn.sigmoid(b_raw)
    g = -jnp.exp(a_log.astype(jnp.float32)) * jax.nn.softplus(a_raw + dt_bias.astype(jnp.float32))
    o = gated_delta_chunked(q, k, v, g, beta)
    o = rms_norm(o, norm_w) * jax.nn.silu(heads(z).astype(jnp.float32))
    return o.reshape(bsz, length, C_WIDTH).astype(x.dtype) @ w_out


def sqrelu_mlp(x, w1, w2):
    return jnp.square(jax.nn.relu(x @ w1)) @ w2


def setup_inputs(seed: int = 0) -> dict:
    key = jax.random.key(seed)
    ks = jax.random.split(key, 24)
    f32 = jnp.float32

    def nrm(k, shape, scale):
        return jax.random.normal(k, shape, f32) * scale

    x = nrm(ks[0], (BATCH, SEQ, D_MODEL), 1.0)
    meta_tokens = nrm(ks[1], (N_META, D_MODEL), 1.0)
    rel_bias = nrm(ks[2], (REL_BUCKETS, A_HEADS), 0.5)
    ev_w_in = nrm(ks[3], (N_EVEN, D_MODEL, EVEN_IN), D_MODEL ** -0.5)
    ev_lambda = nrm(ks[4], (N_EVEN, 4, A_QK_DIM), 0.1)
    ev_subln_w = 1.0 + nrm(ks[5], (N_EVEN, A_V_DIM), 0.02)
    ev_pool_w = nrm(ks[6], (N_EVEN, B_GROUPS, B_GROUP_DIM, B_GROUP_DIM), B_GROUP_DIM ** -0.5)
    ev_pool_scale = 1.0 + nrm(ks[7], (N_EVEN, B_WIDTH), 0.02)
    ev_w_out = nrm(ks[8], (N_EVEN, D_MODEL, D_MODEL), BETA_INIT * D_MODEL ** -0.5)
    od_w_in = nrm(ks[9], (N_ODD, D_MODEL, ODD_IN), D_MODEL ** -0.5)
    od_conv_w = nrm(ks[10], (N_ODD, 3 * C_WIDTH, CONV_WIDTH), CONV_WIDTH ** -0.5)
    od_a_log = jnp.log(jax.random.uniform(ks[11], (N_ODD, C_HEADS), f32, 1.0, 16.0))
    dt = jnp.exp(jax.random.uniform(ks[12], (N_ODD, C_HEADS), f32, math.log(1e-3), math.log(1e-1)))
    od_dt_bias = dt + jnp.log(-jnp.expm1(-dt))
    od_norm_w = 1.0 + nrm(ks[13], (N_ODD, C_HEAD_DIM), 0.02)
    od_w_out = nrm(ks[14], (N_ODD, C_WIDTH, D_MODEL), BETA_INIT * C_WIDTH ** -0.5)
    mlp_w1 = nrm(ks[15], (DEPTH, D_MODEL, D_FF), D_MODEL ** -0.5)
    mlp_w2 = nrm(ks[16], (DEPTH, D_FF, D_MODEL), BETA_INIT * D_FF ** -0.5)
    ln_mix_g = 1.0 + nrm(ks[17], (DEPTH, D_MODEL), 0.02)
    ln_mix_b = nrm(ks[18], (DEPTH, D_MODEL), 0.02)
    ln_mlp_g = 1.0 + nrm(ks[19], (DEPTH, D_MODEL), 0.02)
    ln_mlp_b = nrm(ks[20], (DEPTH, D_MODEL), 0.02)
    return {'x': x, 'meta_tokens': meta_tokens, 'rel_bias': rel_bias,
            'ev_w_in': ev_w_in, 'ev_lambda': ev_lambda, 'ev_subln_w': ev_subln_w,
            'ev_pool_w': ev_pool_w, 'ev_pool_scale': ev_pool_scale, 'ev_w_out': ev_w_out,
            'od_w_in': od_w_in, 'od_conv_w': od_conv_w, 'od_a_log': od_a_log,
            'od_dt_bias': od_dt_bias, 'od_norm_w': od_norm_w, 'od_w_out': od_w_out,
            'mlp_w1': mlp_w1, 'mlp_w2': mlp_w2,
            'ln_mix_g': ln_mix_g, 'ln_mix_b': ln_mix_b, 'ln_mlp_g': ln_mlp_g, 'ln_mlp_b': ln_mlp_b}


def reference(x, meta_tokens, rel_bias, ev_w_in, ev_lambda, ev_subln_w, ev_pool_w, ev_pool_scale,
              ev_w_out, od_w_in, od_conv_w, od_a_log, od_dt_bias, od_norm_w, od_w_out,
              mlp_w1, mlp_w2, ln_mix_g, ln_mix_b, ln_mlp_g, ln_mlp_b):
    bsz = x.shape[0]
    meta = jnp.broadcast_to(meta_tokens[None].astype(x.dtype), (bsz, N_META, D_MODEL))
    h = jnp.concatenate([meta, x], axis=1)
    for i in range(DEPTH):
        j = i // 2
        if i % 2 == 0:
            lambda_init = 0.8 - 0.6 * math.exp(-0.3 * i)
            mix = even_mixer(h, ev_w_in[j], ev_lambda[j], ev_subln_w[j], ev_pool_w[j],
                             ev_pool_scale[j], ev_w_out[j], lambda_init, rel_bias)
        else:
            mix = odd_mixer(h, od_w_in[j], od_conv_w[j], od_a_log[j], od_dt_bias[j],
                            od_norm_w[j], od_w_out[j])
        h = layer_norm(ALPHA * h + mix, ln_mix_g[i], ln_mix_b[i])
        h = layer_norm(ALPHA * h + sqrelu_mlp(h, mlp_w1[i], mlp_w2[i]), ln_mlp_g[i], ln_mlp_b[i])
    return h[:, N_META:]
```

```python
import numpy as np
import contextlib
import concourse.bass as bass
import concourse.mybir as mybir
from concourse.bass_utils import run_bass_kernel_spmd

F32 = mybir.dt.float32
BF16 = mybir.dt.bfloat16
ALU = mybir.AluOpType
AF = mybir.ActivationFunctionType
AX = mybir.AxisListType


class Sched:
    COMPUTE = ("pe", "act", "dve", "pool")
    RING = 24

    def __init__(self, nc, same_engine_sync=True):
        self.nc = nc
        self.stack = contextlib.ExitStack()
        self.ops = {e: [] for e in ("pe", "act", "dve", "pool", "sp")}
        self.cnt = {e: 0 for e in self.COMPUTE}
        self.seen = {e: {} for e in self.ops}
        self.res = {}
        self.dma_n = 0
        self.same = same_engine_sync
        self.sem = {e: self.stack.enter_context(nc.semaphore("c_" + e)) for e in self.COMPUTE}
        self.dsem = [self.stack.enter_context(nc.semaphore("d%d" % i)) for i in range(self.RING)]
        self.out_dmas = []

    def sb(self, name, shape, dt):
        return self.stack.enter_context(self.nc.sbuf_tensor("s_" + name, list(shape), dt))

    def ps(self, name, shape, dt=F32):
        return self.stack.enter_context(self.nc.psum_tensor("p_" + name, list(shape), dt))

    def _semval(self, ident):
        if ident[0] == "dma":
            n = ident[1]
            return ("d", n % self.RING), 16 * (n // self.RING + 1)
        return ("c", ident[1]), ident[2]

    def _deps(self, eng, reads, writes, me):
        need = {}
        for k in reads:
            r = self.res.get(k)
            if r and r["w"] is not None:
                s, v = self._semval(r["w"])
                need[s] = max(need.get(s, 0), v)
        for k in writes:
            r = self.res.get(k)
            if r:
                if r["w"] is not None:
                    s, v = self._semval(r["w"])
                    need[s] = max(need.get(s, 0), v)
                for s, v in r["r"].items():
                    need[s] = max(need.get(s, 0), v)
        waits = []
        for s, v in need.items():
            if s == ("c", eng) and (eng == "pe" or not self.same):
                continue
            if self.seen[eng].get(s, 0) >= v:
                continue
            self.seen[eng][s] = v
            waits.append((s, v))
        for k in reads:
            r = self.res.setdefault(k, {"w": None, "r": {}})
            s, v = self._semval(me)
            r["r"][s] = max(r["r"].get(s, 0), v)
        for k in writes:
            self.res[k] = {"w": me, "r": {}}
        return waits

    def _sem(self, s):
        return self.dsem[s[1]] if s[0] == "d" else self.sem[s[1]]

    def op(self, eng, fn, reads=(), writes=()):
        idx = self.cnt[eng] + 1
        self.cnt[eng] = idx
        me = ("eng", eng, idx)
        waits = self._deps(eng, reads, writes, me)
        self.ops[eng].append((fn, waits, self.sem[eng], 1))

    def dma(self, q, out, in_, reads=(), writes=(), is_out=False):
        n = self.dma_n
        self.dma_n += 1
        me = ("dma", n)
        waits = self._deps(q, reads, writes, me)
        if n >= self.RING:
            s, v = self._semval(("dma", n - self.RING))
            if self.seen[q].get(s, 0) < v:
                self.seen[q][s] = v
                waits.append((s, v))
        fn = lambda e, out=out, in_=in_: e.dma_start(out=out, in_=in_)
        self.ops[q].append((fn, waits, self.dsem[n % self.RING], 16))
        if is_out:
            self.out_dmas.append(me)

    def finish(self):
        need = {}
        for ident in self.out_dmas:
            s, v = self._semval(ident)
            need[s] = max(need.get(s, 0), v)
        self.final_waits = list(need.items())

    def emit(self):
        nc = self.nc
        names = {"pe": "tensor", "act": "scalar", "dve": "vector", "pool": "gpsimd", "sp": "sync"}
        with nc.Block() as block:
            for e, bn in names.items():
                lst = self.ops[e]
                extra = self.final_waits if e == "sp" else []

                def body(engobj, lst=lst, extra=extra):
                    for fn, waits, sem, inc in lst:
                        for s, v in waits:
                            engobj.wait_ge(self._sem(s), v)
                        fn(engobj).then_inc(sem, inc)
                    for s, v in extra:
                        engobj.wait_ge(self._sem(s), v)
                if lst or extra:
                    getattr(block, bn)(body)
        self.stack.close()

import math
import numpy as np

NEG = -30000.0
POOL_WINDOWS = (2, 4, 8, 16)


def t5_bucket(n):
    n = np.maximum(n, 0)
    nf = np.maximum(n, 1).astype(np.float32)
    large = 16 + (np.log(nf / np.float32(16)) / np.float32(math.log(8.0)) * np.float32(16)).astype(np.int32)
    large = np.minimum(large, 31)
    return np.where(n < 16, n, large)


def bias_tile(rb_h, qb, kb):
    kl = np.arange(128)[:, None]
    ql = np.arange(128)[None, :]
    qp = qb * 128 + ql
    kp = kb * 128 + kl
    val = rb_h[t5_bucket(qp - kp)].astype(np.float32)
    allowed = kp <= qp
    if kb == 0:
        padk = kl < 112
        if qb == 0:
            allowed = allowed & (~padk | (ql < 112))
        else:
            allowed = allowed & ~padk
    return np.where(allowed, val, np.float32(NEG)).astype(np.float32)


def prep_E(core, hT_full, w_in, lam_vecs, subln_w, pool_w, pool_scale, rel_bias, lambda_init, nq, nb, npool):
    hd, half = core // 2, core % 2
    lp = nb * 128
    f32 = np.float32
    hTq = np.zeros((1024, nq * 128), f32)
    for i in range(nq):
        qb = 2 * i + half
        if qb < nb:
            hTq[:, i * 128:(i + 1) * 128] = hT_full[:, qb * 128:(qb + 1) * 128]
    hTp = np.zeros((1024, npool + 16), f32)
    p0 = half * npool - 16
    lo = max(p0, 0)
    hTp[:, lo - p0:] = hT_full[:, lo:half * npool + npool]
    rb = rel_bias[:, hd]
    allneg = np.full((128, 128), NEG, f32)
    if half == 0:
        near = np.stack([bias_tile(rb, 4, 3), bias_tile(rb, 4, 4), allneg], axis=1)
        near0 = np.stack([bias_tile(rb, 0, 0), allneg], axis=1)
    else:
        near = np.stack([bias_tile(rb, 5, 3), bias_tile(rb, 5, 4), bias_tile(rb, 5, 5)], axis=1)
        near0 = np.stack([bias_tile(rb, 1, 0), bias_tile(rb, 1, 1)], axis=1)
    kbias = np.empty((128, 2), f32)
    kbias[:, 0] = rb[31]
    kbias[:, 1] = np.where(np.arange(128) < 112, f32(NEG), rb[31])
    win = POOL_WINDOWS[hd]
    pcoef = np.zeros((128, 4), f32)
    pcoef[:, hd] = 1.0 / win
    invfix = np.ones((128, 128), f32)
    if half == 0:
        p = np.arange(128) - 112
        invfix[:, :] = np.where(p >= 0, win / np.minimum(p + 1, win), 1.0).astype(f32)[None, :]
    cst = np.broadcast_to(np.array([lambda_init, 1.0 - lambda_init, 1e-6, 0.0], f32)[None], (128, 4))
    c = np.ascontiguousarray
    return {
        "hT": hT_full, "hTq": hTq, "hTp": hTp,
        "wq": c(w_in[:, hd * 128:(hd + 1) * 128]), "wk": c(w_in[:, 512 + hd * 128:512 + (hd + 1) * 128]),
        "wv": c(w_in[:, 1024 + hd * 128:1024 + (hd + 1) * 128]), "wu": c(w_in[:, 1536 + hd * 128:1536 + (hd + 1) * 128]),
        "wp": c(pool_w[hd]), "near0": c(near0), "near": c(near), "kbias": kbias,
        "lamrep": c(np.broadcast_to(lam_vecs[None], (128, 4, 64))), "sublnw": c(np.broadcast_to(subln_w[None], (128, 128))),
        "cst": c(cst), "pcoef": pcoef, "pscale": c(pool_scale[hd * 128:(hd + 1) * 128, None]), "invfix": invfix,
        "ident": np.eye(128, dtype=f32),
    }


def scatter_E(catT, core, res, nq, nb, npool):
    hd, half = core // 2, core % 2
    for i in range(nq):
        qb = 2 * i + half
        if qb < nb:
            catT[hd * 128:(hd + 1) * 128, qb * 128:(qb + 1) * 128] = res["oT_out"][:, i * 128:(i + 1) * 128]
    catT[512 + hd * 128:512 + (hd + 1) * 128, half * npool:(half + 1) * npool] = res["yT_out"]


def prep_O(core, hT_full, w_in, conv_w, a_log, dt_bias, norm_w):
    hd = core
    f32 = np.float32
    c = np.ascontiguousarray
    idx = np.arange(128)
    convw = np.stack([conv_w[hd * 128:(hd + 1) * 128], conv_w[1024 + hd * 128:1024 + (hd + 1) * 128],
                      conv_w[2048 + hd * 128:2048 + (hd + 1) * 128]], axis=1)
    return {
        "hT": hT_full,
        "wq": c(w_in[:, hd * 128:(hd + 1) * 128]), "wk": c(w_in[:, 1024 + hd * 128:1024 + (hd + 1) * 128]),
        "wv": c(w_in[:, 2048 + hd * 128:2048 + (hd + 1) * 128]), "wz": c(w_in[:, 3072 + hd * 128:3072 + (hd + 1) * 128]),
        "wba": c(np.stack([w_in[:, 4096 + hd], w_in[:, 4104 + hd]], axis=1)),
        "convw": c(convw.astype(f32)),
        "avec": c(np.broadcast_to(np.array([a_log[hd], dt_bias[hd]], f32)[None], (128, 2))),
        "normw": c(np.broadcast_to(norm_w[None], (128, 128))),
        "ident": np.eye(128, dtype=f32),
        "U": (idx[:, None] <= idx[None, :]).astype(f32),
        "Ms": (idx[None, :] > idx[:, None]).astype(f32),
        "Mc": (idx[None, :] >= idx[:, None]).astype(f32),
    }


ALPHA = 8.0 ** 0.25
LN_EPS = 1e-5
NT = 17
TOK = NT * 128
FFB = 256


def layer_norm(s, tag, src, src_key, g, b, dst, dst_key, sm, par):
    st, mv, rstd, nmr, xn, eps = sm["st"][par], sm["mv"][par], sm["rstd"][par], sm["nmr"][par], sm["xn"][par], sm["eps"]
    k = lambda n: (n, par)
    src_keys = src_key if isinstance(src_key, list) else [src_key]
    s.op("dve", lambda e: e.bn_stats(st[:, 0:6], src[:, 0:512]), reads=src_keys, writes=[k("st0")])
    s.op("dve", lambda e: e.bn_stats(st[:, 6:12], src[:, 512:1024]), reads=src_keys, writes=[k("st1")])
    s.op("dve", lambda e: e.bn_aggr(mv[:, 0:2], st[:, 0:12]), reads=[k("st0"), k("st1")], writes=[k("mv")])
    s.op("act", lambda e: e.activation(rstd[:, 0:1], mv[:, 1:2], AF.Sqrt, bias=eps[:, 0:1], scale=1.0),
         reads=[k("mv"), "eps"], writes=[k("rstd")])
    s.op("dve", lambda e: e.reciprocal(rstd[:, 0:1], rstd[:, 0:1]), reads=[k("rstd")], writes=[k("rstd")])
    s.op("dve", lambda e: e.scalar_tensor_tensor(nmr[:, 0:1], mv[:, 0:1], -1.0, rstd[:, 0:1], op0=ALU.mult, op1=ALU.mult),
         reads=[k("mv"), k("rstd")], writes=[k("nmr")])
    s.op("act", lambda e: e.activation(xn[:], src, AF.Identity, bias=nmr[:, 0:1], scale=rstd[:, 0:1]),
         reads=src_keys + [k("nmr"), k("rstd")], writes=[k("xn")])
    s.op("dve", lambda e: e.tensor_tensor(xn[:], xn[:], g, op=ALU.mult), reads=[k("xn"), "lnp"], writes=[k("xn")])
    s.op("pool", lambda e: e.tensor_tensor(dst, xn[:], b, op=ALU.add), reads=[k("xn"), "lnp"], writes=[dst_key])


def build_T(nc, s, h_in, catT, w_out, w1, w2, lnp_d, ident_d, h_out, hT_out):
    wo = s.sb("wo", [128, 8, 1024], BF16)
    lnp = s.sb("lnp", [128, 4, 1024], F32)
    ident = s.sb("ident", [128, 128], F32)
    acc = s.sb("acc", [128, NT, 1024], F32)
    h1T = s.sb("h1T", [128, 8, TOK], BF16)
    w1b = [s.sb("w1b%d" % i, [128, 8, FFB], BF16) for i in range(2)]
    w2b = [s.sb("w2b%d" % i, [128, FFB // 128, 1024], BF16) for i in range(2)]
    gT = [s.sb("gT%d" % i, [128, FFB // 128, 512], BF16) for i in range(2)]
    aT = [s.sb("aT%d" % i, [128, 512], BF16) for i in range(2)]
    ht = [s.sb("ht%d" % i, [128, 1024], F32) for i in range(2)]
    ct = [s.sb("ct%d" % i, [128, 8, 128], BF16) for i in range(2)]
    rt = [s.sb("rt%d" % i, [128, 1024], F32) for i in range(2)]
    ot = [s.sb("ot%d" % i, [128, 1024], F32) for i in range(2)]
    sm = {"st": [s.sb("st%d" % i, [128, 12], F32) for i in range(2)],
          "mv": [s.sb("mv%d" % i, [128, 2], F32) for i in range(2)],
          "rstd": [s.sb("rstd%d" % i, [128, 1], F32) for i in range(2)],
          "nmr": [s.sb("nmr%d" % i, [128, 1], F32) for i in range(2)],
          "xn": [s.sb("xn%d" % i, [128, 1024], F32) for i in range(2)],
          "eps": s.sb("eps", [128, 1], F32)}
    pb = [s.ps("pb%d" % i, [128, 512]) for i in range(8)]

    s.op("dve", lambda e: e.memset(sm["eps"][:], LN_EPS), writes=["eps"])
    s.dma("sp", lnp[:], lnp_d, writes=["lnp"])
    s.dma("sp", ident[:], ident_d, writes=["ident"])
    s.dma("pool", wo[:], w_out.rearrange("(k p) f -> p k f", p=128), writes=["wo"])
    catT_v = catT.rearrange("(k p) t -> p k t", p=128)
    hT_v = hT_out.rearrange("(k p) t -> p k t", p=128)
    w1_v = w1.rearrange("(k p) f -> p k f", p=128)
    w2_v = w2.rearrange("(c p) f -> p c f", p=128)

    def transposes(src, src_key, t, par, dst_fn, dst_key_fn, evac_engs):
        for hb in range(2):
            bank = pb[2 + hb]
            bk = ("pb", 2 + hb)
            for j in range(4):
                k = hb * 4 + j
                s.op("pe", lambda e, k=k, j=j, bank=bank: e.transpose(bank[:, j * 128:(j + 1) * 128], src[:, k * 128:(k + 1) * 128], ident[:]),
                     reads=[src_key, "ident"], writes=[bk])
            dst_fn(hb, bank, bk)

    for t in range(NT):
        par = t % 2
        s.dma("sp", ht[par][:], h_in[t * 128:(t + 1) * 128, :], writes=[("ht", par)])
        s.dma("pool", ct[par][:], catT_v[:, :, t * 128:(t + 1) * 128], writes=[("ct", par)])
        for hb in range(2):
            for k in range(8):
                s.op("pe", lambda e, k=k, hb=hb, par=par: e.matmul(pb[hb][:], ct[par][:, k, :], wo[:, k, hb * 512:(hb + 1) * 512],
                                                              start=(k == 0), stop=(k == 7)),
                     reads=[("ct", par), "wo"], writes=[("pb", hb)])
            s.op("dve", lambda e, hb=hb, par=par: e.scalar_tensor_tensor(rt[par][:, hb * 512:(hb + 1) * 512], ht[par][:, hb * 512:(hb + 1) * 512],
                                                                     ALPHA, pb[hb][:], op0=ALU.mult, op1=ALU.add),
                 reads=[("ht", par), ("pb", hb)], writes=[("rt", par, hb)])
        layer_norm(s, "a", rt[par][:], [("rt", par, 0), ("rt", par, 1)], lnp[:, 0, :], lnp[:, 1, :], ot[par][:], ("ot", par), sm, par)
        s.op("act", lambda e, t=t, par=par: e.mul(acc[:, t, :], ot[par][:], ALPHA), reads=[("ot", par)], writes=[("acc", t)])

        def dst_fn(hb, bank, bk, t=t, par=par):
            eng = "act" if hb == 0 else "dve"
            if eng == "act":
                f = lambda e: e.copy(h1T[:, hb * 4:(hb + 1) * 4, t * 128:(t + 1) * 128], bank[:].rearrange("p (j t) -> p j t", j=4))
            else:
                f = lambda e: e.tensor_copy(h1T[:, hb * 4:(hb + 1) * 4, t * 128:(t + 1) * 128], bank[:].rearrange("p (j t) -> p j t", j=4))
            s.op(eng, f, reads=[bk], writes=[("h1T", t, hb)])
        transposes(ot[par], ("ot", par), t, par, dst_fn, None, None)

    groups = [(g * 512, 512) for g in range(NT // 4)] + ([(NT // 4 * 512, (NT % 4) * 128)] if NT % 4 else [])
    NFB = 4096 // FFB
    NC_ = FFB // 128
    steps = [(fb, gi) for fb in range(NFB) for gi in range(len(groups))]

    def load_w(fb):
        bi = fb % 2
        s.dma("pool", w1b[bi][:], w1_v[:, :, fb * FFB:(fb + 1) * FFB], writes=[("w1b", bi)])
        s.dma("pool", w2b[bi][:], w2_v[:, fb * NC_:(fb + 1) * NC_, :], writes=[("w2b", bi)])

    def stage_A(i):
        fb, gi = steps[i]
        t0, n = groups[gi]
        bi = fb % 2
        gp = i % 2
        for c in range(NC_):
            pa = 4 + (c % 2)
            for k in range(8):
                s.op("pe", lambda e, k=k, c=c, pa=pa, bi=bi, t0=t0, n=n: e.matmul(pb[pa][:, 0:n], w1b[bi][:, k, c * 128:(c + 1) * 128], h1T[:, k, t0:t0 + n],
                                                                           start=(k == 0), stop=(k == 7)),
                     reads=[("w1b", bi)] + [("h1T", t, hb) for t in range(t0 // 128, (t0 + n) // 128) for hb in range(2)], writes=[("pb", pa)])
            ap_ = c % 2
            s.op("act", lambda e, pa=pa, ap_=ap_, n=n: e.activation(aT[ap_][:, 0:n], pb[pa][:, 0:n], AF.Relu), reads=[("pb", pa)], writes=[("aT", ap_)])
            s.op("pool", lambda e, gp=gp, c=c, ap_=ap_, n=n: e.tensor_tensor(gT[gp][:, c, 0:n], aT[ap_][:, 0:n], aT[ap_][:, 0:n], op=ALU.mult),
                 reads=[("aT", ap_)], writes=[("gT", gp, c)])

    zrot = [0]

    def stage_Z(i):
        fb, gi = steps[i]
        t0, n = groups[gi]
        bi = fb % 2
        gp = i % 2
        for tt in range(n // 128):
            t = t0 // 128 + tt
            for hb in range(2):
                zb = [0, 1, 6, 7][zrot[0] % 4]
                zrot[0] += 1
                for c in range(NC_):
                    s.op("pe", lambda e, c=c, gp=gp, tt=tt, hb=hb, bi=bi, zb=zb: e.matmul(pb[zb][:], gT[gp][:, c, tt * 128:(tt + 1) * 128], w2b[bi][:, c, hb * 512:(hb + 1) * 512],
                                                                                   start=(c == 0), stop=(c == NC_ - 1)),
                         reads=[("gT", gp, c), ("w2b", bi)], writes=[("pb", zb)])
                s.op("dve", lambda e, t=t, hb=hb, zb=zb: e.tensor_tensor(acc[:, t, hb * 512:(hb + 1) * 512], acc[:, t, hb * 512:(hb + 1) * 512], pb[zb][:], op=ALU.add),
                     reads=[("pb", zb), ("acc", t)], writes=[("acc", t)])

    load_w(0)
    load_w(1)
    stage_A(0)
    for i in range(len(steps)):
        if i + 1 < len(steps):
            stage_A(i + 1)
        stage_Z(i)
        fb, gi = steps[i]
        if gi == len(groups) - 1 and fb + 2 < NFB:
            load_w(fb + 2)

    for t in range(NT):
        par = t % 2
        layer_norm(s, "c", acc[:, t, :], ("acc", t), lnp[:, 2, :], lnp[:, 3, :], ot[par][:], ("ot", par), sm, par)
        if t == 0:
            s.op("pool", lambda e, par=par: e.memset(ot[par][0:112, :], 0.0), reads=[("ot", par)], writes=[("ot", par)])
        s.dma("sp", h_out[t * 128:(t + 1) * 128, :], ot[par][:], reads=[("ot", par)], is_out=True)

        def dst_fn(hb, bank, bk, t=t, par=par):
            if hb == 0:
                f = lambda e: e.copy(rt[par][:, 0:512].rearrange("p (j t) -> p j t", j=4), bank[:].rearrange("p (j t) -> p j t", j=4))
                s.op("act", f, reads=[bk], writes=[("rt", par, 0)])
            else:
                f = lambda e: e.tensor_copy(rt[par][:, 512:1024].rearrange("p (j t) -> p j t", j=4), bank[:].rearrange("p (j t) -> p j t", j=4))
                s.op("dve", f, reads=[bk], writes=[("rt", par, 1)])
        transposes(ot[par], ("ot", par), t, par, dst_fn, None, None)
        s.dma("sp", hT_v[:, :, t * 128:(t + 1) * 128], rt[par][:].rearrange("p (k t) -> p k t", k=8), reads=[("rt", par, 0), ("rt", par, 1)], is_out=True)


LP = 16512
NB = LP // 128
NQ = 65
NPOOL = 8256
NPL = NPOOL + 16
NEG = -30000.0


def build_E(nc, s, hT, hTq, hTp, wq, wk, wv, wu, wp, near0, near, kbias, lamrep, sublnw, cst, pcoef, pscale, invfix, ident_d,
            oT_out, yT_out, nq=NQ, nb=NB, npool=NPOOL, phase=9):
    npl = npool + 16
    lp = nb * 128
    KT = [s.sb("KT%d" % c, [64, lp], BF16) for c in range(2)]
    QT = [s.sb("QT%d" % c, [64, nq * 128], BF16) for c in range(2)]
    Vp = s.sb("Vp", [128, nb, 136], BF16)
    wqs = s.sb("wqs", [128, 8, 128], BF16)
    wks = s.sb("wks", [128, 8, 128], BF16)
    wvs = s.sb("wvs", [128, 8, 128], BF16)
    wus = s.sb("wus", [128, 8, 128], BF16)
    wps = s.sb("wps", [128, 128], BF16)
    hb_ = [s.sb("hblk%d" % i, [128, 8, 528], BF16) for i in range(2)]
    near0s = s.sb("near0s", [128, 2, 128], F32)
    nears = s.sb("nears", [128, 3, 128], F32)
    kb = s.sb("kb", [128, 2], F32)
    lam = s.sb("lam", [128, 4, 64], F32)
    lamw = s.sb("lamw", [128, 2, 64], F32)
    lams = s.sb("lams", [128, 4], F32)
    subw = s.sb("subw", [128, 128], F32)
    cs = s.sb("cs", [128, 4], F32)
    pco = s.sb("pco", [128, 4], F32)
    psc = s.sb("psc", [128, 1], F32)
    ifx = s.sb("ifx", [128, 128], F32)
    ident = s.sb("ident", [128, 128], F32)
    PT = [[s.sb("PT%d_%d" % (i, c), [128, 512], BF16) for c in range(2)] for i in range(2)]
    tmpn = [s.sb("tmpn%d" % i, [128, 128], F32) for i in range(4)]
    ep = {n: [s.sb("%s%d" % (n, i), sh, F32) for i in range(2)] for n, sh in
          [("rl", [128, 2]), ("nl", [128, 1]), ("o0", [128, 128]), ("aa", [128, 128]), ("sq", [128, 128]), ("ss", [128, 1]), ("rs", [128, 1]), ("on", [128, 128]), ("e0", [128, 129]), ("e1", [128, 129])]}
    ostg = [s.sb("ostg%d" % i, [128, 512], F32) for i in range(2)]
    uc = s.sb("uc", [128, 528], F32)
    sA = s.sb("sA", [128, 528], F32)
    sB = s.sb("sB", [128, 528], F32)
    pacc = s.sb("pacc", [128, 512], F32)
    pl = s.sb("pl", [128, 512], BF16)
    ystg = [s.sb("ystg%d" % i, [128, 512], F32) for i in range(2)]
    pb = [s.ps("pb%d" % i, [128, 512]) for i in range(8)]

    hT_v = hT.rearrange("(k p) t -> p k t", p=128)
    hTq_v = hTq.rearrange("(k p) t -> p k t", p=128)
    hTp_v = hTp.rearrange("(k p) t -> p k t", p=128)

    for (dst, src, key) in [(near0s, near0, "near0"), (nears, near, "near"), (kb, kbias, "kb"), (lam, lamrep, "lam"), (subw, sublnw, "subw"),
                            (cs, cst, "cs"), (pco, pcoef, "pco"), (psc, pscale, "psc"), (ifx, invfix, "ifx"), (ident, ident_d, "ident")]:
        s.dma("sp", dst[:], src, writes=[key])
    for (dst, src, key) in [(wqs, wq, "wq"), (wks, wk, "wk"), (wvs, wv, "wv"), (wus, wu, "wu")]:
        s.dma("pool", dst[:], src.rearrange("(k p) f -> p k f", p=128), writes=[key])
    s.dma("pool", wps[:], wp, writes=["wp"])
    s.op("dve", lambda e: e.memset(Vp[:, :, 128:129], 1.0), writes=["Vones"])
    s.op("dve", lambda e: e.tensor_tensor(lamw[:, 0, :], lam[:, 0, :], lam[:, 1, :], op=ALU.mult), reads=["lam"], writes=["lamw0"])
    s.op("dve", lambda e: e.tensor_tensor(lamw[:, 1, :], lam[:, 2, :], lam[:, 3, :], op=ALU.mult), reads=["lam"], writes=["lamw1"])
    s.op("dve", lambda e: e.reduce_sum(lams[:, 0:2], lamw[:], axis=AX.X), reads=["lamw0", "lamw1"], writes=["lams"])
    s.op("act", lambda e: e.activation(lams[:, 0:2], lams[:, 0:2], AF.Exp), reads=["lams"], writes=["lams"])
    s.op("dve", lambda e: e.tensor_tensor(lams[:, 2:3], lams[:, 0:1], lams[:, 1:2], op=ALU.subtract), reads=["lams"], writes=["lams2"])
    s.op("dve", lambda e: e.tensor_tensor(lams[:, 3:4], lams[:, 2:3], cs[:, 0:1], op=ALU.add), reads=["lams2", "cs"], writes=["lamv"])
    s.op("dve", lambda e: e.tensor_scalar_mul(lams[:, 3:4], lams[:, 3:4], -1.0), reads=["lamv"], writes=["lamv"])
    s.op("dve", lambda e: e.tensor_scalar_mul(subw[:], subw[:], cs[:, 1:2]), reads=["subw", "cs"], writes=["subw"])

    if phase < 1:
        return
    ld = [0]

    def load_blk(view, c0, n):
        i = ld[0] % 2
        ld[0] += 1
        s.dma("pool", hb_[i][:, :, 0:n], view[:, :, c0:c0 + n], writes=[("hblk", i)])
        return i

    rot = [0]

    def bank():
        b = rot[0] % 4
        rot[0] += 1
        return b

    def projT(i, n, w, wkey, M, m0, dst_fn, eng):
        b = bank()
        for k in range(8):
            s.op("pe", lambda e, k=k, b=b: e.matmul(pb[b][0:M, 0:n], w[:, k, m0:m0 + M], hb_[i][:, k, 0:n], start=(k == 0), stop=(k == 7)),
                 reads=[wkey, ("hblk", i)], writes=[("pb", b)])
        dst, dkey = dst_fn
        if eng == "act":
            s.op("act", lambda e, b=b: e.copy(dst, pb[b][0:M, 0:n]), writes=[("pb", b), dkey])
        else:
            s.op("dve", lambda e, b=b: e.tensor_copy(dst, pb[b][0:M, 0:n]), writes=[("pb", b), dkey])

    nblk = (lp + 511) // 512
    for tb in range(nblk):
        c0 = tb * 512
        n = min(512, lp - c0)
        i = load_blk(hT_v, c0, n)
        for c in range(2):
            projT(i, n, wks, "wk", 64, c * 64, (KT[c][:, c0:c0 + n], ("KT", c, tb)), "act")
        for tt in range(n // 128):
            b = bank()
            blk = c0 // 128 + tt
            for k in range(8):
                s.op("pe", lambda e, k=k, b=b, tt=tt, i=i: e.matmul(pb[b][:, 0:128], hb_[i][:, k, tt * 128:(tt + 1) * 128], wvs[:, k, :], start=(k == 0), stop=(k == 7)),
                     reads=["wv", ("hblk", i)], writes=[("pb", b)])
            s.op("dve", lambda e, b=b, blk=blk: e.tensor_copy(Vp[:, blk, 0:128], pb[b][:, 0:128]), writes=[("pb", b), ("V", blk)])
    nqc = nq * 128
    for tb in range((nqc + 511) // 512):
        c0 = tb * 512
        n = min(512, nqc - c0)
        i = load_blk(hTq_v, c0, n)
        for c in range(2):
            projT(i, n, wqs, "wq", 64, c * 64, (QT[c][:, c0:c0 + n], ("QT", c, tb)), "act")
    if phase < 2:
        return
    for tb in range((npool + 511) // 512):
        c0 = tb * 512
        n = min(512, npool - c0)
        W = n + 16
        i = load_blk(hTp_v, c0, W)
        bA = bank()
        bB = bank()
        for k in range(8):
            s.op("pe", lambda e, k=k, bA=bA, i=i: e.matmul(pb[bA][:, 0:16], wus[:, k, :], hb_[i][:, k, 0:16], start=(k == 0), stop=(k == 7)),
                 reads=["wu", ("hblk", i)], writes=[("pb", bA)])
        for k in range(8):
            s.op("pe", lambda e, k=k, bB=bB, i=i, n=n: e.matmul(pb[bB][:, 0:n], wus[:, k, :], hb_[i][:, k, 16:16 + n], start=(k == 0), stop=(k == 7)),
                 reads=["wu", ("hblk", i)], writes=[("pb", bB)])
        s.op("act", lambda e, bA=bA: e.copy(uc[:, 0:16], pb[bA][:, 0:16]), writes=[("pb", bA), "ucA"])
        s.op("dve", lambda e, bB=bB, n=n: e.tensor_copy(uc[:, 16:16 + n], pb[bB][:, 0:n]), writes=[("pb", bB), "ucB"])
        s.op("pool", lambda e, W=W: e.tensor_tensor(sA[:, 1:W], uc[:, 1:W], uc[:, 0:W - 1], op=ALU.add), reads=["ucA", "ucB"], writes=["sA"])
        s.op("dve", lambda e, W=W, n=n: e.tensor_scalar_mul(pacc[:, 0:n], sA[:, 16:W], pco[:, 0:1]), reads=["sA", "pco"], writes=["pacc"])
        s.op("pool", lambda e, W=W: e.tensor_tensor(sB[:, 3:W], sA[:, 3:W], sA[:, 1:W - 2], op=ALU.add), reads=["sA"], writes=["sB"])
        s.op("dve", lambda e, W=W, n=n: e.scalar_tensor_tensor(pacc[:, 0:n], sB[:, 16:W], pco[:, 1:2], pacc[:, 0:n], op0=ALU.mult, op1=ALU.add), reads=["sB", "pco", "pacc"], writes=["pacc"])
        s.op("pool", lambda e, W=W: e.tensor_tensor(sA[:, 7:W], sB[:, 7:W], sB[:, 3:W - 4], op=ALU.add), reads=["sB"], writes=["sA"])
        s.op("dve", lambda e, W=W, n=n: e.scalar_tensor_tensor(pacc[:, 0:n], sA[:, 16:W], pco[:, 2:3], pacc[:, 0:n], op0=ALU.mult, op1=ALU.add), reads=["sA", "pco", "pacc"], writes=["pacc"])
        s.op("pool", lambda e, W=W: e.tensor_tensor(sB[:, 15:W], sA[:, 15:W], sA[:, 7:W - 8], op=ALU.add), reads=["sA"], writes=["sB"])
        s.op("dve", lambda e, W=W, n=n: e.scalar_tensor_tensor(pacc[:, 0:n], sB[:, 16:W], pco[:, 3:4], pacc[:, 0:n], op0=ALU.mult, op1=ALU.add), reads=["sB", "pco", "pacc"], writes=["pacc"])
        if tb == 0:
            s.op("dve", lambda e: e.tensor_tensor(pacc[:, 0:128], pacc[:, 0:128], ifx[:], op=ALU.mult), reads=["pacc", "ifx"], writes=["pacc"])
        s.op("dve", lambda e, W=W, n=n: e.tensor_tensor(pl[:, 0:n], pacc[:, 0:n], uc[:, 16:W], op=ALU.subtract), reads=["pacc", "ucB"], writes=["pl"])
        b = bank()
        s.op("pe", lambda e, b=b, n=n: e.matmul(pb[b][:, 0:n], wps[:], pl[:, 0:n], start=True, stop=True), reads=["wp", "pl"], writes=[("pb", b)])
        yi = tb % 2
        s.op("act", lambda e, b=b, n=n, yi=yi: e.activation(ystg[yi][:, 0:n], pb[b][:, 0:n], AF.Identity, scale=psc[:, 0:1]), reads=["psc"], writes=[("pb", b), ("ystg", yi)])
        s.dma("sp", yT_out[:, c0:c0 + n], ystg[yi][:, 0:n], reads=[("ystg", yi)], is_out=True)

    if phase < 3:
        return
    ktb = lambda c, j: ("KT", c, (j * 128) // 512)
    qtb = lambda c, col0, col1: [("QT", c, t) for t in range(col0 // 512, (col1 - 1) // 512 + 1)]
    groups = [(g * 4, min(4, nq - g * 4)) for g in range((nq + 3) // 4)]
    steps = []
    for (i0, nbk) in groups:
        jmax = 2 * (i0 + nbk - 1) + 1
        for j in range(0, min(jmax, nb - 1) + 1):
            act = [il for il in range(nbk) if 2 * (i0 + il) + 1 >= j]
            steps.append((i0, nbk, j, act))

    def acc_ap(il, c):
        a = il * 2 + c
        return pb[4 + a // 3][:, (a % 3) * 160:(a % 3) * 160 + 129], ("pb", 4 + a // 3)

    def emit_qk(si):
        i0, nbk, j, act = steps[si]
        sp = si % 2
        lo, hi = act[0], act[-1] + 1
        for c in range(2):
            s.op("pe", lambda e, c=c, sp=sp, lo=lo, hi=hi, j=j, i0=i0: e.matmul(pb[sp * 2 + c][:, lo * 128:hi * 128], KT[c][:, j * 128:(j + 1) * 128],
                                                                      QT[c][:, (i0 + lo) * 128:(i0 + hi) * 128], start=True, stop=True),
                 reads=[ktb(c, j)] + qtb(c, (i0 + lo) * 128, (i0 + hi) * 128), writes=[("pb", sp * 2 + c)])

    tn = [0]

    def emit_sm(si):
        i0, nbk, j, act = steps[si]
        sp = si % 2
        far = [il for il in act if j <= 2 * (i0 + il) - 2]
        nearl = [il for il in act if j > 2 * (i0 + il) - 2]
        for c in range(2):
            pkeys = []
            if far:
                lo, hi = far[0], far[-1] + 1
                kcol = 1 if j == 0 else 0
                s.op("act", lambda e, c=c, sp=sp, lo=lo, hi=hi, kcol=kcol: e.activation(PT[sp][c][:, lo * 128:hi * 128], pb[sp * 2 + c][:, lo * 128:hi * 128], AF.Exp,
                                                                                  bias=kb[:, kcol:kcol + 1], scale=0.125),
                     reads=["kb"], writes=[("pb", sp * 2 + c)] + [("PT", sp, c, x) for x in far])
            for il in nearl:
                i = i0 + il
                if i == 0:
                    btile = near0s[:, j, :]
                    bkey = "near0"
                else:
                    btile = nears[:, j - (2 * i - 1), :]
                    bkey = "near"
                ti = tn[0] % 4
                tn[0] += 1
                s.op("dve", lambda e, c=c, sp=sp, il=il, ti=ti, btile=btile: e.scalar_tensor_tensor(tmpn[ti][:], pb[sp * 2 + c][:, il * 128:(il + 1) * 128], 0.125, btile,
                                                                                              op0=ALU.mult, op1=ALU.add),
                     reads=[bkey], writes=[("pb", sp * 2 + c), ("tmpn", ti)])
                s.op("act", lambda e, c=c, sp=sp, il=il, ti=ti: e.activation(PT[sp][c][:, il * 128:(il + 1) * 128], tmpn[ti][:], AF.Exp),
                     reads=[("tmpn", ti)], writes=[("PT", sp, c, il)])

    def emit_pv(si):
        i0, nbk, j, act = steps[si]
        sp = si % 2
        far = [il for il in act if j <= 2 * (i0 + il) - 2]
        for il in act:
            i = i0 + il
            last = (j == 2 * i + 1) or (j == nb - 1)
            for c in range(2):
                ap, akey = acc_ap(il, c)
                pk = ("PT", sp, c, il)
                if j == 0:
                    s.op("dve", lambda e, ap=ap: e.memset(ap, 0.0), writes=[akey])
                s.op("pe", lambda e, ap=ap, c=c, sp=sp, il=il, j=j, last=last: e.matmul(ap, PT[sp][c][:, il * 128:(il + 1) * 128], Vp[:, j, 0:129], start=False, stop=last,
                                                                                   skip_group_check=True),
                     reads=[pk, ("V", j), "Vones"], writes=[akey])
            if last and phase >= 6:
                emit_epi(i0, il)

    def emit_epi(i0, il):
        i = i0 + il
        par = i % 2
        a0, k0 = acc_ap(il, 0)
        a1, k1 = acc_ap(il, 1)
        rl, nl, o0, aa, sq, ss, rs, on = (ep[n][par] for n in ("rl", "nl", "o0", "aa", "sq", "ss", "rs", "on"))
        K = lambda n: (n, par)
        e0, e1 = ep["e0"][par], ep["e1"][par]
        s.op("dve", lambda e, a0=a0: e.tensor_copy(e0[:], a0), writes=[k0, K("e0")])
        s.op("dve", lambda e, a1=a1: e.tensor_copy(e1[:], a1), writes=[k1, K("e1")])
        a0, a1, k0, k1 = e0, e1, K("e0"), K("e1")
        s.op("dve", lambda e: e.reciprocal(rl[:, 0:1], a0[:, 128:129]), reads=[k0], writes=[K("rl0")])
        s.op("dve", lambda e: e.reciprocal(rl[:, 1:2], a1[:, 128:129]), reads=[k1], writes=[K("rl1")])
        s.op("dve", lambda e: e.tensor_tensor(nl[:], rl[:, 1:2], lams[:, 3:4], op=ALU.mult), reads=[K("rl1"), "lamv"], writes=[K("nl")])
        s.op("dve", lambda e: e.tensor_scalar_mul(o0[:], a0[:, 0:128], rl[:, 0:1]), reads=[k0, K("rl0")], writes=[K("o0")])
        s.op("dve", lambda e: e.scalar_tensor_tensor(aa[:], a1[:, 0:128], nl[:, 0:1], o0[:], op0=ALU.mult, op1=ALU.add), reads=[k1, K("nl"), K("o0")], writes=[K("aa")])
        s.op("pool", lambda e: e.tensor_tensor(sq[:], aa[:], aa[:], op=ALU.mult), reads=[K("aa")], writes=[K("sq")])
        s.op("dve", lambda e: e.reduce_sum(ss[:], sq[:], axis=AX.X), reads=[K("sq")], writes=[K("ss")])
        s.op("act", lambda e: e.activation(rs[:], ss[:], AF.Ln, bias=cs[:, 2:3], scale=1.0 / 128), reads=[K("ss"), "cs"], writes=[K("rs")])
        s.op("act", lambda e: e.activation(rs[:], rs[:], AF.Exp, scale=-0.5), reads=[K("rs")], writes=[K("rs")])
        s.op("dve", lambda e: e.scalar_tensor_tensor(on[:], aa[:], rs[:, 0:1], subw[:], op0=ALU.mult, op1=ALU.mult), reads=[K("aa"), K("rs"), "subw"], writes=[K("on")])
        s.op("pe", lambda e: e.transpose(pb[7][:, 0:128], on[:], ident[:]), reads=[K("on"), "ident"], writes=[("pb", 7)])
        g = i0 // 4
        og = g % 2
        s.op("act", lambda e: e.copy(ostg[og][:, il * 128:(il + 1) * 128], pb[7][:, 0:128]), writes=[("pb", 7), ("ostg", og, il)])
        nbk = min(4, nq - i0)
        if il == nbk - 1:
            s.dma("sp", oT_out[:, i0 * 128:(i0 + nbk) * 128], ostg[og][:, 0:nbk * 128], reads=[("ostg", og, x) for x in range(nbk)], is_out=True)

    emit_qk(0)
    for si in range(len(steps)):
        if phase >= 4:
            emit_sm(si)
        if si + 1 < len(steps):
            emit_qk(si + 1)
        if phase >= 5:
            emit_pv(si)

import math

NB = 129
RMS_EPS = 1e-6


def build_O(nc, s, hT, wq, wk, wv, wz, wba, convw, avec, normw, ident_d, U_d, Ms_d, Mc_d, oT_out, nb=NB):
    lp = nb * 128
    sb = s.sb
    wqs, wks, wvs, wzs = (sb(n, [128, 8, 128], BF16) for n in ("wqs", "wks", "wvs", "wzs"))
    wbas = sb("wbas", [128, 8, 2], BF16)
    cw = sb("cw", [128, 3, 4], F32)
    av = sb("av", [128, 2], F32)
    nw = sb("nw", [128, 128], F32)
    ident = sb("ident", [128, 128], F32)
    U = sb("U", [128, 128], F32)
    Ms = sb("Ms", [128, 128], F32)
    Mc = sb("Mc", [128, 128], F32)
    ones = sb("ones", [128, 128], F32)
    cst = sb("cst", [128, 4], F32)
    negA = sb("negA", [128, 1], F32)
    S = sb("S", [128, 128], F32)
    hb_ = [sb("hblk%d" % i, [128, 8, 512], BF16) for i in range(2)]
    X = [[sb("X%d_%d" % (p, i), [128, 515], F32) for i in range(3)] for p in range(2)]
    Y = [sb("Y%d" % i, [128, 512], F32) for i in range(3)]
    ST = [[sb("ST%d_%d" % (p, i), [128, 512], F32) for i in range(3)] for p in range(2)]
    QTb = [sb("QTb%d" % p, [128, 512], BF16) for p in range(2)]
    Qsq = [sb("Qsq%d" % p, [128, 512], F32) for p in range(2)]
    zs = [sb("zs%d" % p, [128, 4, 128], F32) for p in range(2)]
    ostg = [sb("ostg%d" % p, [128, 512], F32) for p in range(2)]

    def two(name, shape, dt=F32):
        return [sb("%s%d" % (name, p), shape, dt) for p in range(2)]
    bas, ebt, beta, gcol, gam, rq, ssk, rk, small = (two(n, [128, w]) for n, w in
                                                     [("bas", 2), ("ebt", 2), ("beta", 1), ("gcol", 1), ("gam", 2), ("rq", 1), ("ssk", 1), ("rk", 1), ("small", 8)])
    Kraw, Ksq, Kn, Vt, dg, dE, ET, ReG, Bm, t1, B32, P32, t2, qkT, Vb, Kbg, Kd, Qt, usb, wT, vn, osb, osq, og = (
        two(n, [128, 256 if n == "dg" else 128]) for n in
        ("Kraw", "Ksq", "Kn", "Vt", "dg", "dE", "ET", "ReG", "Bm", "t1", "B32", "P32", "t2", "qkT", "Vb", "Kbg", "Kd", "Qt", "usb", "wT", "vn", "osb", "osq", "og"))
    KTb = two("KTb", [128, 128], F32)
    Pb = P32
    Ab = [two("Ab%d" % i, [128, 128], F32) for i in range(2)]
    Bb = [two("Bb%d" % i, [128, 128], F32) for i in range(2)]
    sso, ro = two("sso", [128, 1]), two("ro", [128, 1])
    pb = [s.ps("pb%d" % i, [128, 512]) for i in range(8)]
    PK = lambda b: ("pb", b)

    hT_v = hT.rearrange("(k p) t -> p k t", p=128)
    for (dst, src, key) in [(cw, convw, "cw"), (av, avec, "av"), (nw, normw, "nw"), (ident, ident_d, "ident"), (U, U_d, "U"), (Ms, Ms_d, "Ms"), (Mc, Mc_d, "Mc")]:
        s.dma("sp", dst[:], src, writes=[key])
    for (dst, src, key) in [(wqs, wq, "wq"), (wks, wk, "wk"), (wvs, wv, "wv"), (wzs, wz, "wz"), (wbas, wba, "wba")]:
        s.dma("pool", dst[:], src.rearrange("(k p) f -> p k f", p=128), writes=[key])
    s.op("dve", lambda e: e.memset(ones[:], 1.0), writes=["ones"])
    s.op("dve", lambda e: e.memset(S[:], 0.0), writes=["S"])
    s.op("dve", lambda e: e.memset(cst[:, 0:1], RMS_EPS), writes=["cst0"])
    s.op("dve", lambda e: e.memset(cst[:, 1:2], 1.0), writes=["cst1"])
    s.op("dve", lambda e: e.memset(cst[:, 2:3], math.log(128.0 ** -0.5)), writes=["cst2"])
    CK = ["cst0", "cst1", "cst2"]
    s.op("act", lambda e: e.activation(negA[:], av[:, 0:1], AF.Exp), reads=["av"], writes=["negA"])
    s.op("dve", lambda e: e.tensor_scalar_mul(negA[:], negA[:], -1.0), reads=["negA"], writes=["negA"])
    for i in range(3):
        s.op("pool", lambda e, i=i: e.memset(X[1][i][:, 512:515], 0.0), writes=[("X", 1, i)])

    def mm(out, lhsT, rhs, reads, bank, start=True, stop=True):
        s.op("pe", lambda e: e.matmul(out, lhsT, rhs, start=start, stop=stop), reads=reads, writes=[PK(bank)])

    def tr(out, in_, reads, bank):
        s.op("pe", lambda e: e.transpose(out, in_, ident[:]), reads=reads + ["ident"], writes=[PK(bank)])

    nblk = (lp + 511) // 512
    wlist = [(wqs, "wq"), (wks, "wk"), (wvs, "wv")]
    def do_block(tb):
        c0 = tb * 512
        n = min(512, lp - c0)
        p = tb % 2
        s.dma("pool", hb_[p][:, :, 0:n], hT_v[:, :, c0:c0 + n], writes=[("hblk", p)])
        for i, (w, wkey) in enumerate(wlist):
            b = i % 2
            for k in range(8):
                mm(pb[b][:, 0:n], w[:, k, :], hb_[p][:, k, 0:n], [wkey, ("hblk", p)], b, start=(k == 0), stop=(k == 7))
            s.op("act", lambda e, b=b, i=i: e.copy(X[p][i][:, 3:3 + n], pb[b][:, 0:n]), writes=[PK(b), ("X", p, i)])
            s.op("dve", lambda e, i=i: e.tensor_copy(X[p][i][:, 0:3], X[1 - p][i][:, 512:515]), reads=[("X", 1 - p, i)], writes=[("Xc", p, i)])
            eng = "dve"
            xr = [("X", p, i), ("Xc", p, i), "cw"]
            s.op(eng, lambda e, i=i: e.tensor_scalar_mul(Y[i][:, 0:n], X[p][i][:, 0:n], cw[:, i, 0:1]), reads=xr, writes=[("Y", i)])
            for jj in range(1, 4):
                s.op(eng, lambda e, i=i, jj=jj: e.scalar_tensor_tensor(Y[i][:, 0:n], X[p][i][:, jj:jj + n], cw[:, i, jj:jj + 1], Y[i][:, 0:n], op0=ALU.mult, op1=ALU.add),
                     reads=xr + [("Y", i)], writes=[("Y", i)])
            s.op("act", lambda e, i=i: e.activation(ST[p][i][:, 0:n], Y[i][:, 0:n], AF.Silu), reads=[("Y", i)], writes=[("ST", p, i)])
        s.op("pool", lambda e: e.tensor_tensor(Qsq[p][:, 0:n], ST[p][0][:, 0:n], ST[p][0][:, 0:n], op=ALU.mult), reads=[("ST", p, 0)], writes=[("Qsq", p)])
        for tt in range(n // 128):
            cols = slice(tt * 128, (tt + 1) * 128)
            for k in range(8):
                mm(pb[2][:, 0:128], hb_[p][:, k, cols], wzs[:, k, :], ["wz", ("hblk", p)], 2, start=(k == 0), stop=(k == 7))
            s.op("act", lambda e, tt=tt: e.activation(zs[p][:, tt, :], pb[2][:, 0:128], AF.Silu), writes=[PK(2), ("zs", p, tt)])

        def do_chunk(tt):
            cols = slice(tt * 128, (tt + 1) * 128)
            ci = tb * 4 + tt
            q = ci % 2
            K_ = lambda name: (name, q)
            for k in range(8):
                mm(pb[2][:, 128:130], hb_[p][:, k, cols], wbas[:, k, :], ["wba", ("hblk", p)], 2, start=(k == 0), stop=(k == 7))
            s.op("dve", lambda e, q=q: e.tensor_copy(bas[q][:], pb[2][:, 128:130]), writes=[PK(2), K_("bas")])
            s.op("act", lambda e, q=q: e.activation(ebt[q][:, 0:1], bas[q][:, 0:1], AF.Exp, scale=-1.0), reads=[K_("bas")], writes=[K_("eb")])
            s.op("act", lambda e, q=q: e.activation(ebt[q][:, 1:2], bas[q][:, 1:2], AF.Exp, bias=av[:, 1:2]), reads=[K_("bas"), "av"], writes=[K_("ea")])
            s.op("act", lambda e, q=q: e.activation(ebt[q][:, 1:2], ebt[q][:, 1:2], AF.Ln, bias=cst[:, 1:2]), reads=[K_("ea")] + CK, writes=[K_("ea")])
            s.op("dve", lambda e, q=q: e.tensor_scalar_add(beta[q][:], ebt[q][:, 0:1], 1.0), reads=[K_("eb")], writes=[K_("beta")])
            s.op("dve", lambda e, q=q: e.reciprocal(beta[q][:], beta[q][:]), reads=[K_("beta")], writes=[K_("beta")])
            s.op("dve", lambda e, q=q: e.tensor_tensor(gcol[q][:], ebt[q][:, 1:2], negA[:], op=ALU.mult), reads=[K_("ea"), "negA"], writes=[K_("g")])
            mm(pb[2][:, 136:137], U[:], gcol[q][:], ["U", K_("g")], 2)
            mm(pb[2][:, 137:138], ones[:], gcol[q][:], ["ones", K_("g")], 2)
            s.op("dve", lambda e, q=q: e.tensor_copy(gam[q][:], pb[2][:, 136:138]), writes=[PK(2), K_("gam")])
            mm(pb[2][:, 132:133], Qsq[p][:, cols], ones[:, 0:1], [("Qsq", p), "ones"], 2)
            s.op("act", lambda e, q=q: e.activation(rq[q][:], pb[2][:, 132:133], AF.Ln, bias=cst[:, 0:1]), reads=CK, writes=[PK(2), K_("rq")])
            s.op("act", lambda e, q=q: e.activation(rq[q][:], rq[q][:], AF.Exp, scale=-0.5, bias=cst[:, 2:3]), reads=[K_("rq")] + CK, writes=[K_("rq")])
            tr(pb[3][:, 0:128], ST[p][1][:, cols], [("ST", p, 1)], 3)
            s.op("act", lambda e, q=q: e.copy(Kraw[q][:], pb[3][:, 0:128]), writes=[PK(3), K_("Kraw")])
            s.op("pool", lambda e, q=q: e.tensor_tensor(Ksq[q][:], Kraw[q][:], Kraw[q][:], op=ALU.mult), reads=[K_("Kraw")], writes=[K_("Ksq")])
            s.op("dve", lambda e, q=q: e.reduce_sum(ssk[q][:], Ksq[q][:], axis=AX.X), reads=[K_("Ksq")], writes=[K_("ssk")])
            s.op("act", lambda e, q=q: e.activation(rk[q][:], ssk[q][:], AF.Ln, bias=cst[:, 0:1]), reads=[K_("ssk")] + CK, writes=[K_("rk")])
            s.op("act", lambda e, q=q: e.activation(rk[q][:], rk[q][:], AF.Exp, scale=-0.5), reads=[K_("rk")], writes=[K_("rk")])
            s.op("dve", lambda e, q=q: e.tensor_scalar_mul(Kn[q][:], Kraw[q][:], rk[q][:, 0:1]), reads=[K_("Kraw"), K_("rk")], writes=[K_("Kn")])
            tr(pb[3][:, 256:384], Kn[q][:], [K_("Kn")], 3)
            s.op("act", lambda e, q=q: e.copy(KTb[q][:], pb[3][:, 256:384]), writes=[PK(3), K_("KTb")])
            tr(pb[3][:, 128:256], ST[p][2][:, cols], [("ST", p, 2)], 3)
            s.op("dve", lambda e, q=q: e.tensor_copy(Vt[q][:], pb[3][:, 128:256]), writes=[PK(3), K_("Vt")])
            s.op("pool", lambda e, q=q: e.tensor_scalar_mul(dg[q][:, 0:128], ident[:], gam[q][:, 0:1]), reads=["ident", K_("gam")], writes=[K_("dg0")])
            s.op("pool", lambda e, q=q: e.tensor_scalar_mul(dg[q][:, 128:256], ident[:], beta[q][:, 0:1]), reads=["ident", K_("beta")], writes=[K_("dg1")])
            mm(pb[4][:, 0:256], ones[:], dg[q][:], ["ones", K_("dg0"), K_("dg1")], 4)
            s.op("dve", lambda e, q=q: e.tensor_scalar(dE[q][:], pb[4][:, 0:128], gam[q][:, 0:1], 0.0, op0=ALU.subtract, op1=ALU.min), reads=[K_("gam")], writes=[PK(4), K_("dE")])
            s.op("act", lambda e, q=q: e.activation(ReG[q][:], pb[4][:, 0:128], AF.Exp), writes=[PK(4), K_("ReG")])
            s.op("dve", lambda e, q=q: e.tensor_tensor(Bm[q][:], pb[4][:, 128:256], Ms[:], op=ALU.mult), reads=["Ms"], writes=[PK(4), K_("Bm")])
            s.op("act", lambda e, q=q: e.activation(ET[q][:], dE[q][:], AF.Exp), reads=[K_("dE")], writes=[K_("ET")])
            mm(pb[4][:, 256:384], KTb[q][:], KTb[q][:], [K_("KTb")], 4)
            s.op("dve", lambda e, q=q: e.tensor_tensor(t1[q][:], pb[4][:, 256:384], ET[q][:], op=ALU.mult), reads=[K_("ET")], writes=[PK(4), K_("t1")])
            mm(pb[4][:, 384:512], KTb[q][:], ST[p][0][:, cols], [K_("KTb"), ("ST", p, 0)], 4)
            s.op("dve", lambda e, q=q: e.tensor_tensor(t2[q][:], pb[4][:, 384:512], ET[q][:], op=ALU.mult), reads=[K_("ET")], writes=[PK(4), K_("t2")])
            s.op("pool", lambda e, q=q: e.tensor_tensor(B32[q][:], t1[q][:], Bm[q][:], op=ALU.mult), reads=[K_("t1"), K_("Bm")], writes=[K_("B32")])
            s.op("pool", lambda e, q=q: e.tensor_tensor(qkT[q][:], t2[q][:], Mc[:], op=ALU.mult), reads=[K_("t2"), "Mc"], writes=[K_("qkT")])
            s.op("act", lambda e, q=q: e.copy(Bb[0][q][:], B32[q][:]), reads=[K_("B32")], writes=[K_("Bb0")])
            s.op("dve", lambda e, q=q: e.tensor_tensor(P32[q][:], ident[:], B32[q][:], op=ALU.subtract), reads=["ident", K_("B32")], writes=[K_("P32")])
            tr(pb[3][:, 384:512], B32[q][:], [K_("B32")], 3)
            s.op("act", lambda e, q=q: e.copy(Ab[0][q][:], pb[3][:, 384:512]), writes=[PK(3), K_("Ab0")])
            for lv in range(1, 7):
                a_old, a_new = (lv - 1) % 2, lv % 2
                mm(pb[5][:, 0:128], Bb[a_old][q][:], Ab[a_old][q][:], [K_("Bb%d" % a_old), K_("Ab%d" % a_old)], 5)
                if lv < 6:
                    mm(pb[6][:, 384:512], Ab[a_old][q][:], Bb[a_old][q][:], [K_("Bb%d" % a_old), K_("Ab%d" % a_old)], 6)
                s.op("act", lambda e, q=q, a_new=a_new: e.copy(Ab[a_new][q][:], pb[5][:, 0:128]), writes=[PK(5), K_("Ab%d" % a_new)])
                if lv < 6:
                    s.op("dve", lambda e, q=q, a_new=a_new: e.tensor_copy(Bb[a_new][q][:], pb[6][:, 384:512]), writes=[PK(6), K_("Bb%d" % a_new)])
                mm(pb[5][:, 256:384], Ab[a_new][q][:], Pb[q][:], [K_("Ab%d" % a_new), K_("P32")], 5)
                s.op("dve", lambda e, q=q: e.tensor_tensor(P32[q][:], P32[q][:], pb[5][:, 256:384], op=ALU.add), reads=[K_("P32")], writes=[PK(5), K_("P32")])
            s.op("act", lambda e, q=q: e.activation(small[q][:, 0:1], gam[q][:, 0:1], AF.Exp), reads=[K_("gam")], writes=[K_("eg")])
            s.op("act", lambda e, q=q: e.activation(small[q][:, 2:3], gam[q][:, 0:1], AF.Exp, scale=-1.0, bias=gam[q][:, 1:2]), reads=[K_("gam")], writes=[K_("kd")])
            s.op("act", lambda e, q=q: e.activation(small[q][:, 3:4], gam[q][:, 1:2], AF.Exp), reads=[K_("gam")], writes=[K_("dec")])
            s.op("dve", lambda e, q=q: e.tensor_tensor(small[q][:, 1:2], small[q][:, 0:1], beta[q][:], op=ALU.mult), reads=[K_("eg"), K_("beta")], writes=[K_("bg")])
            s.op("pool", lambda e, q=q: e.tensor_scalar_mul(Vb[q][:], Vt[q][:], beta[q][:, 0:1]), reads=[K_("Vt"), K_("beta")], writes=[K_("Vb")])
            s.op("pool", lambda e, q=q: e.tensor_scalar_mul(Kbg[q][:], Kn[q][:], small[q][:, 1:2]), reads=[K_("Kn"), K_("bg")], writes=[K_("Kbg")])
            s.op("pool", lambda e, q=q: e.tensor_scalar_mul(Kd[q][:], Kn[q][:], small[q][:, 2:3]), reads=[K_("Kn"), K_("kd")], writes=[K_("Kd")])
            s.op("dve", lambda e, q=q: e.tensor_tensor(Qt[q][:], ST[p][0][:, cols], ReG[q][:], op=ALU.mult), reads=[("ST", p, 0), K_("ReG")], writes=[K_("Qt")])
            mm(pb[6][:, 0:128], P32[q][:], Vb[q][:], [K_("P32"), K_("Vb")], 6)
            s.op("act", lambda e, q=q: e.copy(usb[q][:], pb[6][:, 0:128]), writes=[PK(6), K_("usb")])
            mm(pb[6][:, 128:256], Kbg[q][:], P32[q][:], [K_("P32"), K_("Kbg")], 6)
            s.op("act", lambda e, q=q: e.copy(wT[q][:], pb[6][:, 128:256]), writes=[PK(6), K_("wT")])
            mm(pb[7][:, 0:128], wT[q][:], S[:], [K_("wT"), "S"], 7)
            s.op("dve", lambda e, q=q: e.tensor_tensor(vn[q][:], usb[q][:], pb[7][:, 0:128], op=ALU.subtract), reads=[K_("usb")], writes=[PK(7), K_("vn")])
            mm(pb[7][:, 128:256], Kd[q][:], vn[q][:], [K_("Kd"), K_("vn")], 7)
            mm(pb[7][:, 256:384], Qt[q][:], S[:], [K_("Qt"), "S"], 7, start=True, stop=False)
            mm(pb[7][:, 256:384], qkT[q][:], vn[q][:], [K_("qkT"), K_("vn")], 7, start=False, stop=True)
            s.op("dve", lambda e, q=q: e.scalar_tensor_tensor(S[:], S[:], small[q][:, 3:4], pb[7][:, 128:256], op0=ALU.mult, op1=ALU.add), reads=["S", K_("dec")], writes=[PK(7), "S"])
            s.op("dve", lambda e, q=q: e.tensor_scalar_mul(osb[q][:], pb[7][:, 256:384], rq[q][:, 0:1]), reads=[K_("rq")], writes=[PK(7), K_("osb")])
            s.op("pool", lambda e, q=q: e.tensor_tensor(osq[q][:], osb[q][:], osb[q][:], op=ALU.mult), reads=[K_("osb")], writes=[K_("osq")])
            s.op("dve", lambda e, q=q: e.reduce_sum(sso[q][:], osq[q][:], axis=AX.X), reads=[K_("osq")], writes=[K_("sso")])
            s.op("act", lambda e, q=q: e.activation(ro[q][:], sso[q][:], AF.Ln, bias=cst[:, 0:1], scale=1.0 / 128), reads=[K_("sso")] + CK, writes=[K_("ro")])
            s.op("act", lambda e, q=q: e.activation(ro[q][:], ro[q][:], AF.Exp, scale=-0.5), reads=[K_("ro")], writes=[K_("ro")])
            s.op("dve", lambda e, q=q: e.scalar_tensor_tensor(og[q][:], osb[q][:], ro[q][:, 0:1], nw[:], op0=ALU.mult, op1=ALU.mult), reads=[K_("osb"), K_("ro"), "nw"], writes=[K_("og")])
            s.op("pool", lambda e, q=q, tt=tt: e.tensor_tensor(og[q][:], og[q][:], zs[p][:, tt, :], op=ALU.mult), reads=[K_("og"), ("zs", p, tt)], writes=[K_("og")])
            tr(pb[6][:, 256:384], og[q][:], [K_("og")], 6)
            s.op("act", lambda e, tt=tt: e.copy(ostg[p][:, tt * 128:(tt + 1) * 128], pb[6][:, 256:384]), writes=[PK(6), ("ostg", p, tt)])
        for tt in range(n // 128):
            do_chunk(tt)
        s.dma("sp", oT_out[:, c0:c0 + n], ostg[p][:, 0:n], reads=[("ostg", p, x) for x in range(n // 128)], is_out=True)

    for tb in range(nblk):
        do_block(tb)

_PROGS = {}


def _dram(nc, n, sh, kind="ExternalInput"):
    return nc.dram_tensor(n, list(sh), F32, kind=kind).ap()


def _prog_T():
    if "T" not in _PROGS:
        nc = bass.Bass("TRN2", target_bir_lowering=False)
        d = lambda n, sh, kind="ExternalInput": _dram(nc, n, sh, kind)
        h_in = d("h_in", [TOK, 1024]); catT = d("catT", [1024, TOK]); w_out = d("w_out", [1024, 1024])
        w1 = d("w1", [1024, 4096]); w2 = d("w2", [4096, 1024]); lnp = d("lnp", [128, 4, 1024]); ident = d("ident", [128, 128])
        h_out = d("h_out", [TOK, 1024], "ExternalOutput"); hT_out = d("hT_out", [1024, TOK], "ExternalOutput")
        s = Sched(nc)
        build_T(nc, s, h_in, catT, w_out, w1, w2, lnp, ident, h_out, hT_out)
        s.finish(); s.emit()
        _PROGS["T"] = nc
    return _PROGS["T"]


def _prog_E():
    if "E" not in _PROGS:
        nc = bass.Bass("TRN2", target_bir_lowering=False)
        d = lambda n, sh, kind="ExternalInput": _dram(nc, n, sh, kind)
        args = dict(hT=d("hT", [1024, LP]), hTq=d("hTq", [1024, NQ * 128]), hTp=d("hTp", [1024, NPOOL + 16]),
                    wq=d("wq", [1024, 128]), wk=d("wk", [1024, 128]), wv=d("wv", [1024, 128]), wu=d("wu", [1024, 128]), wp=d("wp", [128, 128]),
                    near0=d("near0", [128, 2, 128]), near=d("near", [128, 3, 128]), kbias=d("kbias", [128, 2]), lamrep=d("lamrep", [128, 4, 64]),
                    sublnw=d("sublnw", [128, 128]), cst=d("cst", [128, 4]), pcoef=d("pcoef", [128, 4]), pscale=d("pscale", [128, 1]),
                    invfix=d("invfix", [128, 128]), ident_d=d("ident", [128, 128]),
                    oT_out=d("oT_out", [128, NQ * 128], "ExternalOutput"), yT_out=d("yT_out", [128, NPOOL], "ExternalOutput"))
        s = Sched(nc)
        build_E(nc, s, nq=NQ, nb=NB, npool=NPOOL, **args)
        s.finish(); s.emit()
        _PROGS["E"] = nc
    return _PROGS["E"]


def _prog_O():
    if "O" not in _PROGS:
        nc = bass.Bass("TRN2", target_bir_lowering=False)
        d = lambda n, sh, kind="ExternalInput": _dram(nc, n, sh, kind)
        args = dict(hT=d("hT", [1024, LP]), wq=d("wq", [1024, 128]), wk=d("wk", [1024, 128]), wv=d("wv", [1024, 128]), wz=d("wz", [1024, 128]),
                    wba=d("wba", [1024, 2]), convw=d("convw", [128, 3, 4]), avec=d("avec", [128, 2]), normw=d("normw", [128, 128]),
                    ident_d=d("ident", [128, 128]), U_d=d("U", [128, 128]), Ms_d=d("Ms", [128, 128]), Mc_d=d("Mc", [128, 128]),
                    oT_out=d("oT_out", [128, LP], "ExternalOutput"))
        s = Sched(nc)
        build_O(nc, s, nb=NB, **args)
        s.finish(); s.emit()
        _PROGS["O"] = nc
    return _PROGS["O"]


def _run(nc, ims):
    res = run_bass_kernel_spmd(nc, ims, core_ids=list(range(8)))
    return res.results


def kernel(x, meta_tokens, rel_bias, ev_w_in, ev_lambda, ev_subln_w, ev_pool_w, ev_pool_scale, ev_w_out,
           od_w_in, od_conv_w, od_a_log, od_dt_bias, od_norm_w, od_w_out, mlp_w1, mlp_w2,
           ln_mix_g, ln_mix_b, ln_mlp_g, ln_mlp_b):
    f32 = np.float32
    A = lambda a: np.asarray(a, dtype=f32)
    x = A(x)[0]
    h = np.zeros((LP, 1024), f32)
    h[112:128] = A(meta_tokens)
    h[128:] = x
    hT = np.ascontiguousarray(h.T)
    eye = np.eye(128, dtype=f32)
    rows = [np.concatenate([np.arange(0, 128), np.arange(128 + 2048 * c, 128 + 2048 * (c + 1))]) for c in range(8)]
    for i in range(4):
        j = i // 2
        catT = np.zeros((1024, LP), f32)
        if i % 2 == 0:
            lambda_init = 0.8 - 0.6 * math.exp(-0.3 * i)
            ims = [prep_E(c, hT, A(ev_w_in[j]), A(ev_lambda[j]), A(ev_subln_w[j]), A(ev_pool_w[j]), A(ev_pool_scale[j]), A(rel_bias),
                          lambda_init, NQ, NB, NPOOL) for c in range(8)]
            res = _run(_prog_E(), ims)
            for c in range(8):
                scatter_E(catT, c, res[c], NQ, NB, NPOOL)
            w_out = A(ev_w_out[j])
        else:
            ims = [prep_O(c, hT, A(od_w_in[j]), A(od_conv_w[j]), A(od_a_log[j]), A(od_dt_bias[j]), A(od_norm_w[j])) for c in range(8)]
            res = _run(_prog_O(), ims)
            for c in range(8):
                catT[c * 128:(c + 1) * 128] = res[c]["oT_out"]
            w_out = A(od_w_out[j])
        lnp = np.ascontiguousarray(np.broadcast_to(np.stack([A(ln_mix_g[i]), A(ln_mix_b[i]), A(ln_mlp_g[i]), A(ln_mlp_b[i])])[None], (128, 4, 1024)))
        w1 = A(mlp_w1[i]); w2 = A(mlp_w2[i])
        ims = [{"h_in": np.ascontiguousarray(h[rows[c]]), "catT": np.ascontiguousarray(catT[:, rows[c]]), "w_out": w_out, "w1": w1, "w2": w2,
                "lnp": lnp, "ident": eye} for c in range(8)]
        res = _run(_prog_T(), ims)
        h[0:128] = res[0]["h_out"][0:128]
        hT[:, 0:128] = res[0]["hT_out"][:, 0:128]
        for c in range(8):
            h[128 + 2048 * c:128 + 2048 * (c + 1)] = res[c]["h_out"][128:]
            hT[:, 128 + 2048 * c:128 + 2048 * (c + 1)] = res[c]["hT_out"][:, 128:]
    return np.ascontiguousarray(h[128:][None])
```

```python
import numpy as np
import contextlib
import concourse.bass as bass
import concourse.mybir as mybir
from concourse.bass_utils import run_bass_kernel_spmd

F32 = mybir.dt.float32
BF16 = mybir.dt.bfloat16
ALU = mybir.AluOpType
AF = mybir.ActivationFunctionType
AX = mybir.AxisListType


class Sched:
    COMPUTE = ("pe", "act", "dve", "pool")
    RING = 24
    COLL_INC = 16

    def __init__(self, nc, same_engine_sync=True):
        self.nc = nc
        self.stack = contextlib.ExitStack()
        self.ops = {e: [] for e in ("pe", "act", "dve", "pool", "sp")}
        self.cnt = {e: 0 for e in self.COMPUTE}
        self.seen = {e: {} for e in self.ops}
        self.res = {}
        self.dma_n = 0
        self.same = same_engine_sync
        self.sem = {e: self.stack.enter_context(nc.semaphore("c_" + e)) for e in self.COMPUTE}
        self.dsem = [self.stack.enter_context(nc.semaphore("d%d" % i)) for i in range(self.RING)]
        self.out_dmas = []
        self.regs = {}
        self.phase = 0
        self.pstack = None
        self.pclose = []
        self.coll_n = 0
        self.csem = self.stack.enter_context(nc.semaphore("coll"))

    def sb(self, name, shape, dt):
        st = self.pstack if self.pstack is not None else self.stack
        return st.enter_context(self.nc.sbuf_tensor("s_%s_p%d" % (name, self.phase), list(shape), dt))

    def begin_phase(self):
        self.phase += 1
        self.pstack = contextlib.ExitStack()

    def end_phase(self):
        targets = {("c", e): self.cnt[e] for e in self.COMPUTE if self.cnt[e] > 0}
        for n in range(max(0, self.dma_n - self.RING), self.dma_n):
            sk, v = self._semval(("dma", n))
            targets[sk] = max(targets.get(sk, 0), v)
        if self.coll_n:
            targets[("k", 0)] = self.coll_n
        for e in self.ops:
            waits = []
            for sk, v in targets.items():
                if sk == ("c", e):
                    continue
                if self.seen[e].get(sk, 0) >= v:
                    continue
                self.seen[e][sk] = v
                waits.append((sk, v))
            if waits:
                self.ops[e].append((None, waits, None, 0))
        self.res = {}
        self.pstack.close()
        self.pstack = None

    def ps(self, name, shape, dt=F32):
        return self.stack.enter_context(self.nc.psum_tensor("p_" + name, list(shape), dt))

    def _semval(self, ident):
        if ident[0] == "dma":
            n = ident[1]
            return ("d", n % self.RING), 16 * (n // self.RING + 1)
        if ident[0] == "coll":
            return ("k", 0), ident[1]
        return ("c", ident[1]), ident[2]

    def _deps(self, eng, reads, writes, me):
        need = {}
        for k in reads:
            r = self.res.get(k)
            if r and r["w"] is not None:
                s, v = self._semval(r["w"])
                need[s] = max(need.get(s, 0), v)
        for k in writes:
            r = self.res.get(k)
            if r:
                if r["w"] is not None:
                    s, v = self._semval(r["w"])
                    need[s] = max(need.get(s, 0), v)
                for s, v in r["r"].items():
                    need[s] = max(need.get(s, 0), v)
        waits = []
        for s, v in need.items():
            if s == ("c", eng) and (eng == "pe" or not self.same):
                continue
            if self.seen[eng].get(s, 0) >= v:
                continue
            self.seen[eng][s] = v
            waits.append((s, v))
        for k in reads:
            r = self.res.setdefault(k, {"w": None, "r": {}})
            s, v = self._semval(me)
            r["r"][s] = max(r["r"].get(s, 0), v)
        for k in writes:
            self.res[k] = {"w": me, "r": {}}
        return waits

    def _sem(self, s):
        if s[0] == "k":
            return self.csem
        return self.dsem[s[1]] if s[0] == "d" else self.sem[s[1]]

    def op(self, eng, fn, reads=(), writes=()):
        idx = self.cnt[eng] + 1
        self.cnt[eng] = idx
        me = ("eng", eng, idx)
        waits = self._deps(eng, reads, writes, me)
        self.ops[eng].append((fn, waits, self.sem[eng], 1))

    def dma(self, q, out, in_, reads=(), writes=(), is_out=False):
        n = self.dma_n
        self.dma_n += 1
        me = ("dma", n)
        waits = self._deps(q, reads, writes, me)
        if n >= self.RING:
            s, v = self._semval(("dma", n - self.RING))
            if self.seen[q].get(s, 0) < v:
                self.seen[q][s] = v
                waits.append((s, v))
        fn = lambda e, out=out, in_=in_: e.dma_start(out=out, in_=in_)
        self.ops[q].append((fn, waits, self.dsem[n % self.RING], 16))
        if is_out:
            self.out_dmas.append(me)

    def dma_fn(self, q, fn, reads=(), writes=(), is_out=False):
        n = self.dma_n
        self.dma_n += 1
        me = ("dma", n)
        waits = self._deps(q, reads, writes, me)
        if n >= self.RING:
            s, v = self._semval(("dma", n - self.RING))
            if self.seen[q].get(s, 0) < v:
                self.seen[q][s] = v
                waits.append((s, v))
        self.ops[q].append((fn, waits, self.dsem[n % self.RING], 16))
        if is_out:
            self.out_dmas.append(me)

    def coll(self, kind, in_ap, out_ap, reads=(), writes=()):
        self.coll_n += 1
        me = ("coll", self.coll_n)
        waits = self._deps("pool", reads, writes, me)
        fn = lambda e: e.collective_compute(kind, ALU.bypass, replica_groups=[list(range(8))], ins=[in_ap.opt()], outs=[out_ap.opt()])
        self.ops["pool"].append((fn, waits, self.csem, 1))

    def reg(self, e, eng, name, ap, max_val=1 << 20):
        key = (eng, name)
        if key not in self.regs:
            r = e.alloc_register("r_%s_%s" % (eng, name))
            e.reg_load(r, ap)
            self.regs[key] = e.snap(r, min_val=0, max_val=max_val)
        return self.regs[key]

    def finish(self):
        need = {}
        for ident in self.out_dmas:
            s, v = self._semval(ident)
            need[s] = max(need.get(s, 0), v)
        self.final_waits = list(need.items())

    def emit(self):
        nc = self.nc
        names = {"pe": "tensor", "act": "scalar", "dve": "vector", "pool": "gpsimd", "sp": "sync"}
        with nc.Block() as block:
            for e, bn in names.items():
                lst = self.ops[e]
                extra = self.final_waits if e == "sp" else []

                def body(engobj, lst=lst, extra=extra):
                    for fn, waits, sem, inc in lst:
                        for s, v in waits:
                            engobj.wait_ge(self._sem(s), v)
                        if fn is not None:
                            fn(engobj).then_inc(sem, inc)
                    for s, v in extra:
                        engobj.wait_ge(self._sem(s), v)
                if lst or extra:
                    getattr(block, bn)(body)
        self.stack.close()

import math
import numpy as np

NEG = -30000.0
POOL_WINDOWS = (2, 4, 8, 16)


def t5_bucket(n):
    n = np.maximum(n, 0)
    nf = np.maximum(n, 1).astype(np.float32)
    large = 16 + (np.log(nf / np.float32(16)) / np.float32(math.log(8.0)) * np.float32(16)).astype(np.int32)
    large = np.minimum(large, 31)
    return np.where(n < 16, n, large)


def bias_tile(rb_h, qb, kb):
    kl = np.arange(128)[:, None]
    ql = np.arange(128)[None, :]
    qp = qb * 128 + ql
    kp = kb * 128 + kl
    val = rb_h[t5_bucket(qp - kp)].astype(np.float32)
    allowed = kp <= qp
    if kb == 0:
        padk = kl < 112
        if qb == 0:
            allowed = allowed & (~padk | (ql < 112))
        else:
            allowed = allowed & ~padk
    return np.where(allowed, val, np.float32(NEG)).astype(np.float32)


def prep_E(core, hT_full, w_in, lam_vecs, subln_w, pool_w, pool_scale, rel_bias, lambda_init, nq, nb, npool):
    hd, half = core // 2, core % 2
    lp = nb * 128
    f32 = np.float32
    hTq = np.zeros((1024, nq * 128), f32)
    for i in range(nq):
        qb = 2 * i + half
        if qb < nb:
            hTq[:, i * 128:(i + 1) * 128] = hT_full[:, qb * 128:(qb + 1) * 128]
    hTp = np.zeros((1024, npool + 16), f32)
    p0 = half * npool - 16
    lo = max(p0, 0)
    hTp[:, lo - p0:] = hT_full[:, lo:half * npool + npool]
    rb = rel_bias[:, hd]
    allneg = np.full((128, 128), NEG, f32)
    if half == 0:
        near = np.stack([bias_tile(rb, 4, 3), bias_tile(rb, 4, 4), allneg], axis=1)
        near0 = np.stack([bias_tile(rb, 0, 0), allneg], axis=1)
    else:
        near = np.stack([bias_tile(rb, 5, 3), bias_tile(rb, 5, 4), bias_tile(rb, 5, 5)], axis=1)
        near0 = np.stack([bias_tile(rb, 1, 0), bias_tile(rb, 1, 1)], axis=1)
    kbias = np.empty((128, 2), f32)
    kbias[:, 0] = rb[31]
    kbias[:, 1] = np.where(np.arange(128) < 112, f32(NEG), rb[31])
    win = POOL_WINDOWS[hd]
    pcoef = np.zeros((128, 4), f32)
    pcoef[:, hd] = 1.0 / win
    invfix = np.ones((128, 128), f32)
    if half == 0:
        p = np.arange(128) - 112
        invfix[:, :] = np.where(p >= 0, win / np.minimum(p + 1, win), 1.0).astype(f32)[None, :]
    cst = np.broadcast_to(np.array([lambda_init, 1.0 - lambda_init, 1e-6, 0.0], f32)[None], (128, 4))
    c = np.ascontiguousarray
    return {
        "hT": hT_full, "hTq": hTq, "hTp": hTp,
        "wq": c(w_in[:, hd * 128:(hd + 1) * 128]), "wk": c(w_in[:, 512 + hd * 128:512 + (hd + 1) * 128]),
        "wv": c(w_in[:, 1024 + hd * 128:1024 + (hd + 1) * 128]), "wu": c(w_in[:, 1536 + hd * 128:1536 + (hd + 1) * 128]),
        "wp": c(pool_w[hd]), "near0": c(near0), "near": c(near), "kbias": kbias,
        "lamrep": c(np.broadcast_to(lam_vecs[None], (128, 4, 64))), "sublnw": c(np.broadcast_to(subln_w[None], (128, 128))),
        "cst": c(cst), "pcoef": pcoef, "pscale": c(pool_scale[hd * 128:(hd + 1) * 128, None]), "invfix": invfix,
        "ident": np.eye(128, dtype=f32),
    }


def scatter_E(catT, core, res, nq, nb, npool):
    hd, half = core // 2, core % 2
    for i in range(nq):
        qb = 2 * i + half
        if qb < nb:
            catT[hd * 128:(hd + 1) * 128, qb * 128:(qb + 1) * 128] = res["oT_out"][:, i * 128:(i + 1) * 128]
    catT[512 + hd * 128:512 + (hd + 1) * 128, half * npool:(half + 1) * npool] = res["yT_out"]


def prep_O(core, hT_full, w_in, conv_w, a_log, dt_bias, norm_w):
    hd = core
    f32 = np.float32
    c = np.ascontiguousarray
    idx = np.arange(128)
    convw = np.stack([conv_w[hd * 128:(hd + 1) * 128], conv_w[1024 + hd * 128:1024 + (hd + 1) * 128],
                      conv_w[2048 + hd * 128:2048 + (hd + 1) * 128]], axis=1)
    return {
        "hT": hT_full,
        "wq": c(w_in[:, hd * 128:(hd + 1) * 128]), "wk": c(w_in[:, 1024 + hd * 128:1024 + (hd + 1) * 128]),
        "wv": c(w_in[:, 2048 + hd * 128:2048 + (hd + 1) * 128]), "wz": c(w_in[:, 3072 + hd * 128:3072 + (hd + 1) * 128]),
        "wba": c(np.stack([w_in[:, 4096 + hd], w_in[:, 4104 + hd]], axis=1)),
        "convw": c(convw.astype(f32)),
        "avec": c(np.broadcast_to(np.array([a_log[hd], dt_bias[hd]], f32)[None], (128, 2))),
        "normw": c(np.broadcast_to(norm_w[None], (128, 128))),
        "ident": np.eye(128, dtype=f32),
        "U": (idx[:, None] <= idx[None, :]).astype(f32),
        "Ms": (idx[None, :] > idx[:, None]).astype(f32),
        "Mc": (idx[None, :] >= idx[:, None]).astype(f32),
    }


ALPHA = 8.0 ** 0.25
LN_EPS = 1e-5
NT = 17
TOK = NT * 128
FFB = 256


def layer_norm(s, tag, src, src_key, g, b, dst, dst_key, sm, par):
    st, mv, rstd, nmr, xn, eps = sm["st"][par], sm["mv"][par], sm["rstd"][par], sm["nmr"][par], sm["xn"][par], sm["eps"]
    k = lambda n: (n, par)
    src_keys = src_key if isinstance(src_key, list) else [src_key]
    s.op("dve", lambda e: e.bn_stats(st[:, 0:6], src[:, 0:512]), reads=src_keys, writes=[k("st0")])
    s.op("dve", lambda e: e.bn_stats(st[:, 6:12], src[:, 512:1024]), reads=src_keys, writes=[k("st1")])
    s.op("dve", lambda e: e.bn_aggr(mv[:, 0:2], st[:, 0:12]), reads=[k("st0"), k("st1")], writes=[k("mv")])
    s.op("act", lambda e: e.activation(rstd[:, 0:1], mv[:, 1:2], AF.Sqrt, bias=eps[:, 0:1], scale=1.0),
         reads=[k("mv"), "eps"], writes=[k("rstd")])
    s.op("dve", lambda e: e.reciprocal(rstd[:, 0:1], rstd[:, 0:1]), reads=[k("rstd")], writes=[k("rstd")])
    s.op("dve", lambda e: e.scalar_tensor_tensor(nmr[:, 0:1], mv[:, 0:1], -1.0, rstd[:, 0:1], op0=ALU.mult, op1=ALU.mult),
         reads=[k("mv"), k("rstd")], writes=[k("nmr")])
    s.op("act", lambda e: e.activation(xn[:], src, AF.Identity, bias=nmr[:, 0:1], scale=rstd[:, 0:1]),
         reads=src_keys + [k("nmr"), k("rstd")], writes=[k("xn")])
    s.op("dve", lambda e: e.tensor_tensor(xn[:], xn[:], g, op=ALU.mult), reads=[k("xn"), "lnp"], writes=[k("xn")])
    s.op("pool", lambda e: e.tensor_tensor(dst, xn[:], b, op=ALU.add), reads=[k("xn"), "lnp"], writes=[dst_key])


def build_T(nc, s, h_in, catT, w_out, w1, w2, lnp_d, ident_d, h_out, hT_out):
    wo = s.sb("wo", [128, 8, 1024], BF16)
    lnp = s.sb("lnp", [128, 4, 1024], F32)
    ident = s.sb("ident", [128, 128], F32)
    acc = s.sb("acc", [128, NT, 1024], F32)
    h1T = s.sb("h1T", [128, 8, TOK], BF16)
    w1b = [s.sb("w1b%d" % i, [128, 8, FFB], BF16) for i in range(2)]
    w2b = [s.sb("w2b%d" % i, [128, FFB // 128, 1024], BF16) for i in range(2)]
    gT = [s.sb("gT%d" % i, [128, FFB // 128, 512], BF16) for i in range(2)]
    aT = [s.sb("aT%d" % i, [128, 512], BF16) for i in range(2)]
    ht = [s.sb("ht%d" % i, [128, 1024], F32) for i in range(2)]
    ct = [s.sb("ct%d" % i, [128, 8, 128], BF16) for i in range(2)]
    rt = [s.sb("rt%d" % i, [128, 1024], F32) for i in range(2)]
    ot = [s.sb("ot%d" % i, [128, 1024], F32) for i in range(2)]
    sm = {"st": [s.sb("st%d" % i, [128, 12], F32) for i in range(2)],
          "mv": [s.sb("mv%d" % i, [128, 2], F32) for i in range(2)],
          "rstd": [s.sb("rstd%d" % i, [128, 1], F32) for i in range(2)],
          "nmr": [s.sb("nmr%d" % i, [128, 1], F32) for i in range(2)],
          "xn": [s.sb("xn%d" % i, [128, 1024], F32) for i in range(2)],
          "eps": s.sb("eps", [128, 1], F32)}
    pb = [s.ps("pb%d" % i, [128, 512]) for i in range(8)]

    s.op("dve", lambda e: e.memset(sm["eps"][:], LN_EPS), writes=["eps"])
    s.dma("sp", lnp[:], lnp_d, writes=["lnp"])
    s.dma("sp", ident[:], ident_d, writes=["ident"])
    s.dma("pool", wo[:], w_out.rearrange("(k p) f -> p k f", p=128), writes=["wo"])
    catT_v = catT.rearrange("(k p) t -> p k t", p=128)
    hT_v = hT_out.rearrange("(k p) t -> p k t", p=128)
    w1_v = w1.rearrange("(k p) f -> p k f", p=128)
    w2_v = w2.rearrange("(c p) f -> p c f", p=128)

    def transposes(src, src_key, t, par, dst_fn, dst_key_fn, evac_engs):
        for hb in range(2):
            bank = pb[2 + hb]
            bk = ("pb", 2 + hb)
            for j in range(4):
                k = hb * 4 + j
                s.op("pe", lambda e, k=k, j=j, bank=bank: e.transpose(bank[:, j * 128:(j + 1) * 128], src[:, k * 128:(k + 1) * 128], ident[:]),
                     reads=[src_key, "ident"], writes=[bk])
            dst_fn(hb, bank, bk)

    for t in range(NT):
        par = t % 2
        s.dma("sp", ht[par][:], h_in[t * 128:(t + 1) * 128, :], writes=[("ht", par)])
        s.dma("pool", ct[par][:], catT_v[:, :, t * 128:(t + 1) * 128], writes=[("ct", par)])
        for hb in range(2):
            for k in range(8):
                s.op("pe", lambda e, k=k, hb=hb, par=par: e.matmul(pb[hb][:], ct[par][:, k, :], wo[:, k, hb * 512:(hb + 1) * 512],
                                                              start=(k == 0), stop=(k == 7)),
                     reads=[("ct", par), "wo"], writes=[("pb", hb)])
            s.op("dve", lambda e, hb=hb, par=par: e.scalar_tensor_tensor(rt[par][:, hb * 512:(hb + 1) * 512], ht[par][:, hb * 512:(hb + 1) * 512],
                                                                     ALPHA, pb[hb][:], op0=ALU.mult, op1=ALU.add),
                 reads=[("ht", par), ("pb", hb)], writes=[("rt", par, hb)])
        layer_norm(s, "a", rt[par][:], [("rt", par, 0), ("rt", par, 1)], lnp[:, 0, :], lnp[:, 1, :], ot[par][:], ("ot", par), sm, par)
        s.op("act", lambda e, t=t, par=par: e.mul(acc[:, t, :], ot[par][:], ALPHA), reads=[("ot", par)], writes=[("acc", t)])

        def dst_fn(hb, bank, bk, t=t, par=par):
            eng = "act" if hb == 0 else "dve"
            if eng == "act":
                f = lambda e: e.copy(h1T[:, hb * 4:(hb + 1) * 4, t * 128:(t + 1) * 128], bank[:].rearrange("p (j t) -> p j t", j=4))
            else:
                f = lambda e: e.tensor_copy(h1T[:, hb * 4:(hb + 1) * 4, t * 128:(t + 1) * 128], bank[:].rearrange("p (j t) -> p j t", j=4))
            s.op(eng, f, reads=[bk], writes=[("h1T", t, hb)])
        transposes(ot[par], ("ot", par), t, par, dst_fn, None, None)

    groups = [(g * 512, 512) for g in range(NT // 4)] + ([(NT // 4 * 512, (NT % 4) * 128)] if NT % 4 else [])
    NFB = 4096 // FFB
    NC_ = FFB // 128
    steps = [(fb, gi) for fb in range(NFB) for gi in range(len(groups))]

    def load_w(fb):
        bi = fb % 2
        s.dma("pool", w1b[bi][:], w1_v[:, :, fb * FFB:(fb + 1) * FFB], writes=[("w1b", bi)])
        s.dma("pool", w2b[bi][:], w2_v[:, fb * NC_:(fb + 1) * NC_, :], writes=[("w2b", bi)])

    def stage_A(i):
        fb, gi = steps[i]
        t0, n = groups[gi]
        bi = fb % 2
        gp = i % 2
        for c in range(NC_):
            pa = 4 + (c % 2)
            for k in range(8):
                s.op("pe", lambda e, k=k, c=c, pa=pa, bi=bi, t0=t0, n=n: e.matmul(pb[pa][:, 0:n], w1b[bi][:, k, c * 128:(c + 1) * 128], h1T[:, k, t0:t0 + n],
                                                                           start=(k == 0), stop=(k == 7)),
                     reads=[("w1b", bi)] + [("h1T", t, hb) for t in range(t0 // 128, (t0 + n) // 128) for hb in range(2)], writes=[("pb", pa)])
            ap_ = c % 2
            s.op("act", lambda e, pa=pa, ap_=ap_, n=n: e.activation(aT[ap_][:, 0:n], pb[pa][:, 0:n], AF.Relu), reads=[("pb", pa)], writes=[("aT", ap_)])
            s.op("pool", lambda e, gp=gp, c=c, ap_=ap_, n=n: e.tensor_tensor(gT[gp][:, c, 0:n], aT[ap_][:, 0:n], aT[ap_][:, 0:n], op=ALU.mult),
                 reads=[("aT", ap_)], writes=[("gT", gp, c)])

    zrot = [0]

    def stage_Z(i):
        fb, gi = steps[i]
        t0, n = groups[gi]
        bi = fb % 2
        gp = i % 2
        for tt in range(n // 128):
            t = t0 // 128 + tt
            for hb in range(2):
                zb = [0, 1, 6, 7][zrot[0] % 4]
                zrot[0] += 1
                for c in range(NC_):
                    s.op("pe", lambda e, c=c, gp=gp, tt=tt, hb=hb, bi=bi, zb=zb: e.matmul(pb[zb][:], gT[gp][:, c, tt * 128:(tt + 1) * 128], w2b[bi][:, c, hb * 512:(hb + 1) * 512],
                                                                                   start=(c == 0), stop=(c == NC_ - 1)),
                         reads=[("gT", gp, c), ("w2b", bi)], writes=[("pb", zb)])
                s.op("dve", lambda e, t=t, hb=hb, zb=zb: e.tensor_tensor(acc[:, t, hb * 512:(hb + 1) * 512], acc[:, t, hb * 512:(hb + 1) * 512], pb[zb][:], op=ALU.add),
                     reads=[("pb", zb), ("acc", t)], writes=[("acc", t)])

    load_w(0)
    load_w(1)
    stage_A(0)
    for i in range(len(steps)):
        if i + 1 < len(steps):
            stage_A(i + 1)
        stage_Z(i)
        fb, gi = steps[i]
        if gi == len(groups) - 1 and fb + 2 < NFB:
            load_w(fb + 2)

    for t in range(NT):
        par = t % 2
        layer_norm(s, "c", acc[:, t, :], ("acc", t), lnp[:, 2, :], lnp[:, 3, :], ot[par][:], ("ot", par), sm, par)
        if t == 0:
            s.op("pool", lambda e, par=par: e.memset(ot[par][0:112, :], 0.0), reads=[("ot", par)], writes=[("ot", par)])
        s.dma("sp", h_out[t * 128:(t + 1) * 128, :], ot[par][:], reads=[("ot", par)], is_out=True)

        def dst_fn(hb, bank, bk, t=t, par=par):
            if hb == 0:
                f = lambda e: e.copy(rt[par][:, 0:512].rearrange("p (j t) -> p j t", j=4), bank[:].rearrange("p (j t) -> p j t", j=4))
                s.op("act", f, reads=[bk], writes=[("rt", par, 0)])
            else:
                f = lambda e: e.tensor_copy(rt[par][:, 512:1024].rearrange("p (j t) -> p j t", j=4), bank[:].rearrange("p (j t) -> p j t", j=4))
                s.op("dve", f, reads=[bk], writes=[("rt", par, 1)])
        transposes(ot[par], ("ot", par), t, par, dst_fn, None, None)
        s.dma("sp", hT_v[:, :, t * 128:(t + 1) * 128], rt[par][:].rearrange("p (k t) -> p k t", k=8), reads=[("rt", par, 0), ("rt", par, 1)], is_out=True)


def hT_chunks(D, BPR):
    B0, G1 = D["B0"], D["G1"]
    CW = min(512, BPR * 128)
    out = [((lambda k: B0[k * 128:(k + 1) * 128, :]), 128, 0)]
    for r in range(8):
        for m in range(BPR * 128 // CW):
            out.append(((lambda k, r=r, m=m: G1[r * 1024 + k * 128:r * 1024 + (k + 1) * 128, m * CW:(m + 1) * CW]), CW, 1 + BPR * r + m * (CW // 128)))
    return out


def phase_E(s, pb, D, BPR, W):
    NXB = 8 * BPR
    nb = NXB + 1
    NQL = NXB // 2
    J = BPR // 2
    lp = nb * 128
    s.begin_phase()
    KT = [s.sb("KT%d" % c, [64, lp], BF16) for c in range(2)]
    QT = [s.sb("QT%d" % c, [64, (NQL + 1) * 128], BF16) for c in range(2)]
    Vp = s.sb("Vp", [128, nb, 136], BF16)
    wqs = s.sb("wqs", [128, 8, 128], BF16)
    wks = s.sb("wks", [128, 8, 128], BF16)
    wvs = s.sb("wvs", [128, 8, 128], BF16)
    hb_ = [s.sb("hblk%d" % i, [128, 8, 512], BF16) for i in range(2)]
    near0s = s.sb("near0s", [128, 3, 128], F32)
    nears = s.sb("nears", [128, 3, 128], F32)
    nearSs = s.sb("nearSs", [128, 128], F32)
    kb = s.sb("kb", [128, 2], F32)
    lam = s.sb("lam", [128, 4, 64], F32)
    lamw = s.sb("lamw", [128, 2, 64], F32)
    lams = s.sb("lams", [128, 4], F32)
    subw = s.sb("subw", [128, 128], F32)
    cs = s.sb("cs", [128, 4], F32)
    ident = s.sb("ident", [128, 128], F32)
    PT = [[s.sb("PT%d_%d" % (i, c), [128, 512], BF16) for c in range(2)] for i in range(2)]
    tmpn = [s.sb("tmpn%d" % i, [128, 128], F32) for i in range(4)]
    ep = {n: [s.sb("%s%d" % (n, i), sh, F32) for i in range(2)] for n, sh in
          [("rl", [128, 2]), ("nl", [128, 1]), ("o0", [128, 128]), ("aa", [128, 128]), ("sq", [128, 128]), ("ss", [128, 1]), ("rs", [128, 1]),
           ("on", [128, 128]), ("e0", [128, 129]), ("e1", [128, 129])]}
    ostg = [s.sb("ostg%d" % i, [128, 512], F32) for i in range(2)]

    for (dst, src, key) in [(near0s, W["near0"], "near0"), (nears, W["near"], "near"), (nearSs, W["nearS"], "nearS"), (kb, W["kbias"], "kb"),
                            (lam, W["lamrep"], "lam"), (subw, W["sublnw"], "subw"), (cs, W["cst"], "cs"), (ident, D["ident"], "ident")]:
        s.dma("sp", dst[:], src, writes=[key])
    for (dst, src, key) in [(wqs, W["wq"], "wq"), (wks, W["wk"], "wk"), (wvs, W["wv"], "wv")]:
        s.dma("pool", dst[:], src.rearrange("(k p) f -> p k f", p=128), writes=[key])
    s.op("dve", lambda e: e.memset(Vp[:, :, 128:129], 1.0), writes=["Vones"])
    s.op("dve", lambda e: e.tensor_tensor(lamw[:, 0, :], lam[:, 0, :], lam[:, 1, :], op=ALU.mult), reads=["lam"], writes=["lamw0"])
    s.op("dve", lambda e: e.tensor_tensor(lamw[:, 1, :], lam[:, 2, :], lam[:, 3, :], op=ALU.mult), reads=["lam"], writes=["lamw1"])
    s.op("dve", lambda e: e.reduce_sum(lams[:, 0:2], lamw[:], axis=AX.X), reads=["lamw0", "lamw1"], writes=["lams"])
    s.op("act", lambda e: e.activation(lams[:, 0:2], lams[:, 0:2], AF.Exp), reads=["lams"], writes=["lams"])
    s.op("dve", lambda e: e.tensor_tensor(lams[:, 2:3], lams[:, 0:1], lams[:, 1:2], op=ALU.subtract), reads=["lams"], writes=["lams2"])
    s.op("dve", lambda e: e.tensor_tensor(lams[:, 3:4], lams[:, 2:3], cs[:, 0:1], op=ALU.add), reads=["lams2", "cs"], writes=["lamv"])
    s.op("dve", lambda e: e.tensor_scalar_mul(lams[:, 3:4], lams[:, 3:4], -1.0), reads=["lamv"], writes=["lamv"])
    s.op("dve", lambda e: e.tensor_scalar_mul(subw[:], subw[:], cs[:, 1:2]), reads=["subw", "cs"], writes=["subw"])

    rot = [0]

    def bank():
        b = rot[0] % 4
        rot[0] += 1
        return b

    ld = [0]

    def projT(i, n, w, wkey, m0, dst, dkey):
        b = bank()
        for k in range(8):
            s.op("pe", lambda e, k=k, b=b: e.matmul(pb[b][0:64, 0:n], w[:, k, m0:m0 + 64], hb_[i][:, k, 0:n], start=(k == 0), stop=(k == 7)),
                 reads=[wkey] + [("hblk", i, kk) for kk in range(8)], writes=[("pb", b)])
        s.op("act", lambda e, b=b: e.copy(dst, pb[b][0:64, 0:n]), writes=[("pb", b), dkey])

    for (rowfn, n, blk0) in hT_chunks(D, BPR):
        i = ld[0] % 2
        ld[0] += 1
        for k in range(8):
            s.dma("pool", hb_[i][:, k, 0:n], rowfn(k), writes=[("hblk", i, k)])
        c0 = blk0 * 128
        for c in range(2):
            projT(i, n, wks, "wk", c * 64, KT[c][:, c0:c0 + n], ("KT", c, blk0))
        for tt in range(n // 128):
            b = bank()
            blk = blk0 + tt
            for k in range(8):
                s.op("pe", lambda e, k=k, b=b, tt=tt, i=i: e.matmul(pb[b][:, 0:128], hb_[i][:, k, tt * 128:(tt + 1) * 128], wvs[:, k, :], start=(k == 0), stop=(k == 7)),
                     reads=["wv"] + [("hblk", i, kk) for kk in range(8)], writes=[("pb", b)])
            s.op("dve", lambda e, b=b, blk=blk: e.tensor_copy(Vp[:, blk, 0:128], pb[b][:, 0:128]), writes=[("pb", b), ("V", blk)])
    CWB = min(512, BPR * 128) // 128
    ktb = lambda c, j: ("KT", c, 0 if j == 0 else 1 + ((j - 1) // CWB) * CWB)

    ngrp = (NQL + 3) // 4
    grp_list = [(g * 4, min(4, NQL - g * 4), False) for g in range(ngrp)] + [(NQL, 1, True)]
    oh2 = s.sb("oh2", [128, 4], F32)
    s.dma("sp", oh2[:], D["oh2"], writes=["oh2"])
    qpair = s.sb("qpair", [128, 8, 4, 256], BF16)
    for (i0, nbk, special) in grp_list:
        for il in range(nbk):
            i = i0 + il
            for k in range(8):
                if special:
                    s.dma("pool", qpair[:, k, il, 0:128], D["B0"][k * 128:(k + 1) * 128, :], writes=[("qp", k, il)])
                else:
                    r, pr = i // J, i % J
                    s.dma("pool", qpair[:, k, il, :], D["G1"][r * 1024 + k * 128:r * 1024 + (k + 1) * 128, pr * 256:(pr + 1) * 256], writes=[("qp", k, il)])
        i_ = ld[0] % 2
        ld[0] += 1
        n = nbk * 128
        qv = hb_[i_][:].rearrange("p k (b c) -> p k b c", c=128)
        for k in range(8):
            rk_ = [("qp", k, il) for il in range(nbk)] + ["oh2"]
            if special:
                s.op("dve", lambda e, k=k, nbk=nbk, qv=qv: e.tensor_scalar_mul(qv[:, k, 0:nbk, :], qpair[:, k, 0:nbk, 0:128], oh2[:, 1:2]), reads=rk_, writes=[("hblk", i_, k)])
            else:
                s.op("dve", lambda e, k=k, nbk=nbk, qv=qv: e.tensor_scalar_mul(qv[:, k, 0:nbk, :], qpair[:, k, 0:nbk, 0:128], oh2[:, 0:1]), reads=rk_, writes=[("hblk", i_, k)])
                s.op("dve", lambda e, k=k, nbk=nbk, qv=qv: e.scalar_tensor_tensor(qv[:, k, 0:nbk, :], qpair[:, k, 0:nbk, 128:256], oh2[:, 1:2], qv[:, k, 0:nbk, :], op0=ALU.mult, op1=ALU.add),
                     reads=rk_ + [("hblk", i_, k)], writes=[("hblk", i_, k)])
        for c in range(2):
            projT(i_, n, wqs, "wq", c * 64, QT[c][:, i0 * 128:i0 * 128 + n], ("QT", c, i0))
    qtb = lambda c, i0: [("QT", c, i0)]

    steps = []
    for (i0, nbk, special) in grp_list:
        if special:
            steps.append((i0, nbk, 0, [0], True))
            continue
        jmax = 2 * (i0 + nbk - 1) + 2
        for j in range(0, jmax + 1):
            act = [il for il in range(nbk) if 2 * (i0 + il) + 2 >= j]
            steps.append((i0, nbk, j, act, False))

    def acc_ap(il, c):
        a = il * 2 + c
        return pb[4 + a // 3][:, (a % 3) * 160:(a % 3) * 160 + 129], ("pb", 4 + a // 3)

    def emit_qk(si):
        i0, nbk, j, act, special = steps[si]
        sp = si % 2
        lo, hi = act[0], act[-1] + 1
        for c in range(2):
            s.op("pe", lambda e, c=c, sp=sp, lo=lo, hi=hi, j=j, i0=i0: e.matmul(pb[sp * 2 + c][:, lo * 128:hi * 128], KT[c][:, j * 128:(j + 1) * 128],
                                                                      QT[c][:, (i0 + lo) * 128:(i0 + hi) * 128], start=True, stop=True),
                 reads=[ktb(c, j)] + qtb(c, i0), writes=[("pb", sp * 2 + c)])

    tn = [0]

    def emit_sm(si):
        i0, nbk, j, act, special = steps[si]
        sp = si % 2
        far = [] if special else [il for il in act if j <= 2 * (i0 + il) - 1]
        nearl = [il for il in act if il not in far]
        for c in range(2):
            for il in nearl:
                i = i0 + il
                if special:
                    btile, bkey = nearSs[:], "nearS"
                elif i == 0:
                    btile, bkey = near0s[:, j, :], "near0"
                else:
                    btile, bkey = nears[:, j - 2 * i, :], "near"
                ti = tn[0] % 4
                tn[0] += 1
                s.op("dve", lambda e, c=c, sp=sp, il=il, ti=ti, btile=btile: e.scalar_tensor_tensor(tmpn[ti][:], pb[sp * 2 + c][:, il * 128:(il + 1) * 128], 0.125, btile,
                                                                                              op0=ALU.mult, op1=ALU.add),
                     reads=[bkey], writes=[("pb", sp * 2 + c), ("tmpn", ti)])
                s.op("act", lambda e, c=c, sp=sp, il=il, ti=ti: e.activation(PT[sp][c][:, il * 128:(il + 1) * 128], tmpn[ti][:], AF.Exp),
                     reads=[("tmpn", ti)], writes=[("PT", sp, c, il)])
            if far:
                lo, hi = far[0], far[-1] + 1
                kcol = 1 if j == 0 else 0
                s.op("act", lambda e, c=c, sp=sp, lo=lo, hi=hi, kcol=kcol: e.activation(PT[sp][c][:, lo * 128:hi * 128], pb[sp * 2 + c][:, lo * 128:hi * 128], AF.Exp,
                                                                                  bias=kb[:, kcol:kcol + 1], scale=0.125),
                     reads=["kb"], writes=[("pb", sp * 2 + c)] + [("PT", sp, c, x) for x in far])

    def emit_pv(si):
        i0, nbk, j, act, special = steps[si]
        sp = si % 2
        for il in act:
            i = i0 + il
            last = True if special else (j == 2 * i + 2)
            for c in range(2):
                ap, akey = acc_ap(il, c)
                if j == 0:
                    s.op("dve", lambda e, ap=ap: e.memset(ap, 0.0), writes=[akey])
                s.op("pe", lambda e, ap=ap, c=c, sp=sp, il=il, j=j, last=last: e.matmul(ap, PT[sp][c][:, il * 128:(il + 1) * 128], Vp[:, j, 0:129], start=False, stop=last,
                                                                                   skip_group_check=True),
                     reads=[("PT", sp, c, il), ("V", j), "Vones"], writes=[akey])
            if last:
                emit_epi(i0, il, nbk)

    gcount = [0]

    def emit_epi(i0, il, nbk):
        i = i0 + il
        par = i % 2
        a0, k0 = acc_ap(il, 0)
        a1, k1 = acc_ap(il, 1)
        rl, nl, o0, aa, sq, ss, rs, on, e0, e1 = (ep[n][par] for n in ("rl", "nl", "o0", "aa", "sq", "ss", "rs", "on", "e0", "e1"))
        K = lambda n: (n, par)
        s.op("dve", lambda e: e.tensor_copy(e0[:], a0), writes=[k0, K("e0")])
        s.op("dve", lambda e: e.tensor_copy(e1[:], a1), writes=[k1, K("e1")])
        s.op("dve", lambda e: e.reciprocal(rl[:, 0:1], e0[:, 128:129]), reads=[K("e0")], writes=[K("rl0")])
        s.op("dve", lambda e: e.reciprocal(rl[:, 1:2], e1[:, 128:129]), reads=[K("e1")], writes=[K("rl1")])
        s.op("dve", lambda e: e.tensor_tensor(nl[:], rl[:, 1:2], lams[:, 3:4], op=ALU.mult), reads=[K("rl1"), "lamv"], writes=[K("nl")])
        s.op("dve", lambda e: e.tensor_scalar_mul(o0[:], e0[:, 0:128], rl[:, 0:1]), reads=[K("e0"), K("rl0")], writes=[K("o0")])
        s.op("dve", lambda e: e.scalar_tensor_tensor(aa[:], e1[:, 0:128], nl[:, 0:1], o0[:], op0=ALU.mult, op1=ALU.add), reads=[K("e1"), K("nl"), K("o0")], writes=[K("aa")])
        s.op("pool", lambda e: e.tensor_tensor(sq[:], aa[:], aa[:], op=ALU.mult), reads=[K("aa")], writes=[K("sq")])
        s.op("dve", lambda e: e.reduce_sum(ss[:], sq[:], axis=AX.X), reads=[K("sq")], writes=[K("ss")])
        s.op("act", lambda e: e.activation(rs[:], ss[:], AF.Ln, bias=cs[:, 2:3], scale=1.0 / 128), reads=[K("ss"), "cs"], writes=[K("rs")])
        s.op("act", lambda e: e.activation(rs[:], rs[:], AF.Exp, scale=-0.5), reads=[K("rs")], writes=[K("rs")])
        s.op("dve", lambda e: e.scalar_tensor_tensor(on[:], aa[:], rs[:, 0:1], subw[:], op0=ALU.mult, op1=ALU.mult), reads=[K("aa"), K("rs"), "subw"], writes=[K("on")])
        s.op("pe", lambda e: e.transpose(pb[7][:, 0:128], on[:], ident[:]), reads=[K("on"), "ident"], writes=[("pb", 7)])
        og = (i0 // 4) % 2
        s.op("act", lambda e: e.copy(ostg[og][:, il * 128:(il + 1) * 128], pb[7][:, 0:128]), writes=[("pb", 7), ("ostg", og, il)])
        col = 8 * J * 128 if i >= NQL else ((i % J) * 8 + i // J) * 128
        s.dma("sp", D["MOe"][:, col:col + 128], ostg[og][:, il * 128:(il + 1) * 128], reads=[("ostg", og, il)], writes=[("MOe", i)])

    emit_qk(0)
    for si in range(len(steps)):
        emit_sm(si)
        if si + 1 < len(steps):
            emit_qk(si + 1)
        emit_pv(si)
    s.coll("AllGather", D["MOe"], D["GA"], reads=[("MOe", i) for i in range(NQL + 1)], writes=["GA"])
    s.end_phase()

import math

NB = 129
RMS_EPS = 1e-6


def phase_O(s, pb, D, BPR, W):
    wq, wk, wv, wz, wba, convw, avec, normw = (W[n] for n in ("wq", "wk", "wv", "wz", "wba", "convw", "avec", "normw"))
    ident_d, U_d, Ms_d, Mc_d = D["ident"], D["U"], D["Ms"], D["Mc"]
    oT_out = D["MOo"]
    s.begin_phase()
    sb = s.sb
    wqs, wks, wvs, wzs = (sb(n, [128, 8, 128], BF16) for n in ("wqs", "wks", "wvs", "wzs"))
    wbas = sb("wbas", [128, 8, 2], BF16)
    cw = sb("cw", [128, 3, 4], F32)
    av = sb("av", [128, 2], F32)
    nw = sb("nw", [128, 128], F32)
    ident = sb("ident", [128, 128], F32)
    U = sb("U", [128, 128], F32)
    Ms = sb("Ms", [128, 128], F32)
    Mc = sb("Mc", [128, 128], F32)
    ones = sb("ones", [128, 128], F32)
    cst = sb("cst", [128, 4], F32)
    negA = sb("negA", [128, 1], F32)
    S = sb("S", [128, 128], F32)
    hb_ = [sb("hblk%d" % i, [128, 8, 512], BF16) for i in range(2)]
    X = [[sb("X%d_%d" % (p, i), [128, 515], F32) for i in range(3)] for p in range(2)]
    Y = [sb("Y%d" % i, [128, 512], F32) for i in range(3)]
    ST = [[sb("ST%d_%d" % (p, i), [128, 512], F32) for i in range(3)] for p in range(2)]
    QTb = [sb("QTb%d" % p, [128, 512], BF16) for p in range(2)]
    Qsq = [sb("Qsq%d" % p, [128, 512], F32) for p in range(2)]
    zs = [sb("zs%d" % p, [128, 4, 128], F32) for p in range(2)]
    ostg = [sb("ostg%d" % p, [128, 512], F32) for p in range(2)]

    def two(name, shape, dt=F32):
        return [sb("%s%d" % (name, p), shape, dt) for p in range(2)]
    bas, ebt, beta, gcol, gam, rq, ssk, rk, small = (two(n, [128, w]) for n, w in
                                                     [("bas", 2), ("ebt", 2), ("beta", 1), ("gcol", 1), ("gam", 2), ("rq", 1), ("ssk", 1), ("rk", 1), ("small", 8)])
    Kraw, Ksq, Kn, Vt, dg, dE, ET, ReG, Bm, t1, B32, P32, t2, qkT, Vb, Kbg, Kd, Qt, usb, wT, vn, osb, osq, og = (
        two(n, [128, 256 if n == "dg" else 128]) for n in
        ("Kraw", "Ksq", "Kn", "Vt", "dg", "dE", "ET", "ReG", "Bm", "t1", "B32", "P32", "t2", "qkT", "Vb", "Kbg", "Kd", "Qt", "usb", "wT", "vn", "osb", "osq", "og"))
    KTb = two("KTb", [128, 128], F32)
    Pb = P32
    Ab = [two("Ab%d" % i, [128, 128], F32) for i in range(2)]
    Bb = [two("Bb%d" % i, [128, 128], F32) for i in range(2)]
    sso, ro = two("sso", [128, 1]), two("ro", [128, 1])
    PK = lambda b: ("pb", b)
    for (dst, src, key) in [(cw, convw, "cw"), (av, avec, "av"), (nw, normw, "nw"), (ident, ident_d, "ident"), (U, U_d, "U"), (Ms, Ms_d, "Ms"), (Mc, Mc_d, "Mc")]:
        s.dma("sp", dst[:], src, writes=[key])
    for (dst, src, key) in [(wqs, wq, "wq"), (wks, wk, "wk"), (wvs, wv, "wv"), (wzs, wz, "wz"), (wbas, wba, "wba")]:
        s.dma("pool", dst[:], src.rearrange("(k p) f -> p k f", p=128), writes=[key])
    s.op("dve", lambda e: e.memset(ones[:], 1.0), writes=["ones"])
    s.op("dve", lambda e: e.memset(S[:], 0.0), writes=["S"])
    s.op("dve", lambda e: e.memset(cst[:, 0:1], RMS_EPS), writes=["cst0"])
    s.op("dve", lambda e: e.memset(cst[:, 1:2], 1.0), writes=["cst1"])
    s.op("dve", lambda e: e.memset(cst[:, 2:3], math.log(128.0 ** -0.5)), writes=["cst2"])
    CK = ["cst0", "cst1", "cst2"]
    s.op("act", lambda e: e.activation(negA[:], av[:, 0:1], AF.Exp), reads=["av"], writes=["negA"])
    s.op("dve", lambda e: e.tensor_scalar_mul(negA[:], negA[:], -1.0), reads=["negA"], writes=["negA"])
    for i in range(3):
        s.op("pool", lambda e, i=i: e.memset(X[1][i][:, 0:515], 0.0), writes=[("X", 1, i)])

    def mm(out, lhsT, rhs, reads, bank, start=True, stop=True):
        s.op("pe", lambda e: e.matmul(out, lhsT, rhs, start=start, stop=stop), reads=reads, writes=[PK(bank)])

    def tr(out, in_, reads, bank):
        s.op("pe", lambda e: e.transpose(out, in_, ident[:]), reads=reads + ["ident"], writes=[PK(bank)])

    chunks = hT_chunks(D, BPR)
    nblk = len(chunks)
    wlist = [(wqs, "wq"), (wks, "wk"), (wvs, "wv")]
    nprev = [0]
    cbase = [0]
    def do_block(tb):
        rowfn, n, blk0 = chunks[tb]
        c0 = blk0 * 128
        p = tb % 2
        npv = nprev[0]
        nprev[0] = n
        cb = cbase[0]
        cbase[0] += n // 128
        for k in range(8):
            s.dma("pool", hb_[p][:, k, 0:n], rowfn(k), writes=[("hblk", p, k)])
        for i, (w, wkey) in enumerate(wlist):
            b = i % 2
            for k in range(8):
                mm(pb[b][:, 0:n], w[:, k, :], hb_[p][:, k, 0:n], [wkey, ("hblk", p)] + [("hblk", p, kk) for kk in range(8)], b, start=(k == 0), stop=(k == 7))
            s.op("act", lambda e, b=b, i=i: e.copy(X[p][i][:, 3:3 + n], pb[b][:, 0:n]), writes=[PK(b), ("X", p, i)])
            s.op("dve", lambda e, i=i: e.tensor_copy(X[p][i][:, 0:3], X[1 - p][i][:, npv:npv + 3]), reads=[("X", 1 - p, i)], writes=[("Xc", p, i)])
            eng = "dve"
            xr = [("X", p, i), ("Xc", p, i), "cw"]
            s.op(eng, lambda e, i=i: e.tensor_scalar_mul(Y[i][:, 0:n], X[p][i][:, 0:n], cw[:, i, 0:1]), reads=xr, writes=[("Y", i)])
            for jj in range(1, 4):
                s.op(eng, lambda e, i=i, jj=jj: e.scalar_tensor_tensor(Y[i][:, 0:n], X[p][i][:, jj:jj + n], cw[:, i, jj:jj + 1], Y[i][:, 0:n], op0=ALU.mult, op1=ALU.add),
                     reads=xr + [("Y", i)], writes=[("Y", i)])
            s.op("act", lambda e, i=i: e.activation(ST[p][i][:, 0:n], Y[i][:, 0:n], AF.Silu), reads=[("Y", i)], writes=[("ST", p, i)])
        s.op("pool", lambda e: e.tensor_tensor(Qsq[p][:, 0:n], ST[p][0][:, 0:n], ST[p][0][:, 0:n], op=ALU.mult), reads=[("ST", p, 0)], writes=[("Qsq", p)])
        for tt in range(n // 128):
            cols = slice(tt * 128, (tt + 1) * 128)
            for k in range(8):
                mm(pb[2][:, 0:128], hb_[p][:, k, cols], wzs[:, k, :], ["wz", ("hblk", p)] + [("hblk", p, kk) for kk in range(8)], 2, start=(k == 0), stop=(k == 7))
            s.op("act", lambda e, tt=tt: e.activation(zs[p][:, tt, :], pb[2][:, 0:128], AF.Silu), writes=[PK(2), ("zs", p, tt)])

        def do_chunk(tt):
            cols = slice(tt * 128, (tt + 1) * 128)
            ci = cb + tt
            q = ci % 2
            K_ = lambda name: (name, q)
            for k in range(8):
                mm(pb[2][:, 128:130], hb_[p][:, k, cols], wbas[:, k, :], ["wba", ("hblk", p)] + [("hblk", p, kk) for kk in range(8)], 2, start=(k == 0), stop=(k == 7))
                yield
            s.op("dve", lambda e, q=q: e.tensor_copy(bas[q][:], pb[2][:, 128:130]), writes=[PK(2), K_("bas")])
            yield
            s.op("act", lambda e, q=q: e.activation(ebt[q][:, 0:1], bas[q][:, 0:1], AF.Exp, scale=-1.0), reads=[K_("bas")], writes=[K_("eb")])
            yield
            s.op("act", lambda e, q=q: e.activation(ebt[q][:, 1:2], bas[q][:, 1:2], AF.Exp, bias=av[:, 1:2]), reads=[K_("bas"), "av"], writes=[K_("ea")])
            yield
            s.op("act", lambda e, q=q: e.activation(ebt[q][:, 1:2], ebt[q][:, 1:2], AF.Ln, bias=cst[:, 1:2]), reads=[K_("ea")] + CK, writes=[K_("ea")])
            yield
            s.op("dve", lambda e, q=q: e.tensor_scalar_add(beta[q][:], ebt[q][:, 0:1], 1.0), reads=[K_("eb")], writes=[K_("beta")])
            yield
            s.op("dve", lambda e, q=q: e.reciprocal(beta[q][:], beta[q][:]), reads=[K_("beta")], writes=[K_("beta")])
            yield
            s.op("dve", lambda e, q=q: e.tensor_tensor(gcol[q][:], ebt[q][:, 1:2], negA[:], op=ALU.mult), reads=[K_("ea"), "negA"], writes=[K_("g")])
            yield
            mm(pb[2][:, 136:137], U[:], gcol[q][:], ["U", K_("g")], 2)
            yield
            mm(pb[2][:, 137:138], ones[:], gcol[q][:], ["ones", K_("g")], 2)
            yield
            s.op("dve", lambda e, q=q: e.tensor_copy(gam[q][:], pb[2][:, 136:138]), writes=[PK(2), K_("gam")])
            yield
            mm(pb[2][:, 132:133], Qsq[p][:, cols], ones[:, 0:1], [("Qsq", p), "ones"], 2)
            yield
            s.op("act", lambda e, q=q: e.activation(rq[q][:], pb[2][:, 132:133], AF.Ln, bias=cst[:, 0:1]), reads=CK, writes=[PK(2), K_("rq")])
            yield
            s.op("act", lambda e, q=q: e.activation(rq[q][:], rq[q][:], AF.Exp, scale=-0.5, bias=cst[:, 2:3]), reads=[K_("rq")] + CK, writes=[K_("rq")])
            yield
            tr(pb[3][:, 0:128], ST[p][1][:, cols], [("ST", p, 1)], 3)
            yield
            s.op("act", lambda e, q=q: e.copy(Kraw[q][:], pb[3][:, 0:128]), writes=[PK(3), K_("Kraw")])
            yield
            s.op("pool", lambda e, q=q: e.tensor_tensor(Ksq[q][:], Kraw[q][:], Kraw[q][:], op=ALU.mult), reads=[K_("Kraw")], writes=[K_("Ksq")])
            yield
            s.op("dve", lambda e, q=q: e.reduce_sum(ssk[q][:], Ksq[q][:], axis=AX.X), reads=[K_("Ksq")], writes=[K_("ssk")])
            yield
            s.op("act", lambda e, q=q: e.activation(rk[q][:], ssk[q][:], AF.Ln, bias=cst[:, 0:1]), reads=[K_("ssk")] + CK, writes=[K_("rk")])
            yield
            s.op("act", lambda e, q=q: e.activation(rk[q][:], rk[q][:], AF.Exp, scale=-0.5), reads=[K_("rk")], writes=[K_("rk")])
            yield
            s.op("dve", lambda e, q=q: e.tensor_scalar_mul(Kn[q][:], Kraw[q][:], rk[q][:, 0:1]), reads=[K_("Kraw"), K_("rk")], writes=[K_("Kn")])
            yield
            tr(pb[3][:, 256:384], Kn[q][:], [K_("Kn")], 3)
            yield
            s.op("act", lambda e, q=q: e.copy(KTb[q][:], pb[3][:, 256:384]), writes=[PK(3), K_("KTb")])
            yield
            tr(pb[3][:, 128:256], ST[p][2][:, cols], [("ST", p, 2)], 3)
            yield
            s.op("dve", lambda e, q=q: e.tensor_copy(Vt[q][:], pb[3][:, 128:256]), writes=[PK(3), K_("Vt")])
            yield
            s.op("pool", lambda e, q=q: e.tensor_scalar_mul(dg[q][:, 0:128], ident[:], gam[q][:, 0:1]), reads=["ident", K_("gam")], writes=[K_("dg0")])
            yield
            s.op("pool", lambda e, q=q: e.tensor_scalar_mul(dg[q][:, 128:256], ident[:], beta[q][:, 0:1]), reads=["ident", K_("beta")], writes=[K_("dg1")])
            yield
            mm(pb[4][:, 0:256], ones[:], dg[q][:], ["ones", K_("dg0"), K_("dg1")], 4)
            yield
            s.op("dve", lambda e, q=q: e.tensor_scalar(dE[q][:], pb[4][:, 0:128], gam[q][:, 0:1], 0.0, op0=ALU.subtract, op1=ALU.min), reads=[K_("gam")], writes=[PK(4), K_("dE")])
            yield
            s.op("act", lambda e, q=q: e.activation(ReG[q][:], pb[4][:, 0:128], AF.Exp), writes=[PK(4), K_("ReG")])
            yield
            s.op("dve", lambda e, q=q: e.tensor_tensor(Bm[q][:], pb[4][:, 128:256], Ms[:], op=ALU.mult), reads=["Ms"], writes=[PK(4), K_("Bm")])
            yield
            s.op("act", lambda e, q=q: e.activation(ET[q][:], dE[q][:], AF.Exp), reads=[K_("dE")], writes=[K_("ET")])
            yield
            mm(pb[4][:, 256:384], KTb[q][:], KTb[q][:], [K_("KTb")], 4)
            yield
            s.op("dve", lambda e, q=q: e.tensor_tensor(t1[q][:], pb[4][:, 256:384], ET[q][:], op=ALU.mult), reads=[K_("ET")], writes=[PK(4), K_("t1")])
            yield
            mm(pb[4][:, 384:512], KTb[q][:], ST[p][0][:, cols], [K_("KTb"), ("ST", p, 0)], 4)
            yield
            s.op("dve", lambda e, q=q: e.tensor_tensor(t2[q][:], pb[4][:, 384:512], ET[q][:], op=ALU.mult), reads=[K_("ET")], writes=[PK(4), K_("t2")])
            yield
            s.op("pool", lambda e, q=q: e.tensor_tensor(B32[q][:], t1[q][:], Bm[q][:], op=ALU.mult), reads=[K_("t1"), K_("Bm")], writes=[K_("B32")])
            yield
            s.op("pool", lambda e, q=q: e.tensor_tensor(qkT[q][:], t2[q][:], Mc[:], op=ALU.mult), reads=[K_("t2"), "Mc"], writes=[K_("qkT")])
            yield
            s.op("act", lambda e, q=q: e.copy(Bb[0][q][:], B32[q][:]), reads=[K_("B32")], writes=[K_("Bb0")])
            yield
            s.op("dve", lambda e, q=q: e.tensor_tensor(P32[q][:], ident[:], B32[q][:], op=ALU.subtract), reads=["ident", K_("B32")], writes=[K_("P32")])
            yield
            tr(pb[3][:, 384:512], B32[q][:], [K_("B32")], 3)
            yield
            s.op("act", lambda e, q=q: e.copy(Ab[0][q][:], pb[3][:, 384:512]), writes=[PK(3), K_("Ab0")])
            yield
            yield "SPLIT"
            for lv in range(1, 7):
                a_old, a_new = (lv - 1) % 2, lv % 2
                mm(pb[5][:, 0:128], Bb[a_old][q][:], Ab[a_old][q][:], [K_("Bb%d" % a_old), K_("Ab%d" % a_old)], 5)
                yield
                if lv < 6:
                    mm(pb[6][:, 384:512], Ab[a_old][q][:], Bb[a_old][q][:], [K_("Bb%d" % a_old), K_("Ab%d" % a_old)], 6)
                s.op("act", lambda e, q=q, a_new=a_new: e.copy(Ab[a_new][q][:], pb[5][:, 0:128]), writes=[PK(5), K_("Ab%d" % a_new)])
                yield
                if lv < 6:
                    s.op("dve", lambda e, q=q, a_new=a_new: e.tensor_copy(Bb[a_new][q][:], pb[6][:, 384:512]), writes=[PK(6), K_("Bb%d" % a_new)])
                mm(pb[5][:, 256:384], Ab[a_new][q][:], Pb[q][:], [K_("Ab%d" % a_new), K_("P32")], 5)
                yield
                s.op("dve", lambda e, q=q: e.tensor_tensor(P32[q][:], P32[q][:], pb[5][:, 256:384], op=ALU.add), reads=[K_("P32")], writes=[PK(5), K_("P32")])
                yield
            s.op("act", lambda e, q=q: e.activation(small[q][:, 0:1], gam[q][:, 0:1], AF.Exp), reads=[K_("gam")], writes=[K_("eg")])
            yield
            s.op("act", lambda e, q=q: e.activation(small[q][:, 2:3], gam[q][:, 0:1], AF.Exp, scale=-1.0, bias=gam[q][:, 1:2]), reads=[K_("gam")], writes=[K_("kd")])
            yield
            s.op("act", lambda e, q=q: e.activation(small[q][:, 3:4], gam[q][:, 1:2], AF.Exp), reads=[K_("gam")], writes=[K_("dec")])
            yield
            s.op("dve", lambda e, q=q: e.tensor_tensor(small[q][:, 1:2], small[q][:, 0:1], beta[q][:], op=ALU.mult), reads=[K_("eg"), K_("beta")], writes=[K_("bg")])
            yield
            s.op("pool", lambda e, q=q: e.tensor_scalar_mul(Vb[q][:], Vt[q][:], beta[q][:, 0:1]), reads=[K_("Vt"), K_("beta")], writes=[K_("Vb")])
            yield
            s.op("pool", lambda e, q=q: e.tensor_scalar_mul(Kbg[q][:], Kn[q][:], small[q][:, 1:2]), reads=[K_("Kn"), K_("bg")], writes=[K_("Kbg")])
            yield
            s.op("pool", lambda e, q=q: e.tensor_scalar_mul(Kd[q][:], Kn[q][:], small[q][:, 2:3]), reads=[K_("Kn"), K_("kd")], writes=[K_("Kd")])
            yield
            s.op("dve", lambda e, q=q: e.tensor_tensor(Qt[q][:], ST[p][0][:, cols], ReG[q][:], op=ALU.mult), reads=[("ST", p, 0), K_("ReG")], writes=[K_("Qt")])
            yield
            mm(pb[6][:, 0:128], P32[q][:], Vb[q][:], [K_("P32"), K_("Vb")], 6)
            yield
            s.op("act", lambda e, q=q: e.copy(usb[q][:], pb[6][:, 0:128]), writes=[PK(6), K_("usb")])
            yield
            mm(pb[6][:, 128:256], Kbg[q][:], P32[q][:], [K_("P32"), K_("Kbg")], 6)
            yield
            s.op("act", lambda e, q=q: e.copy(wT[q][:], pb[6][:, 128:256]), writes=[PK(6), K_("wT")])
            yield
            mm(pb[7][:, 0:128], wT[q][:], S[:], [K_("wT"), "S"], 7)
            yield
            s.op("dve", lambda e, q=q: e.tensor_tensor(vn[q][:], usb[q][:], pb[7][:, 0:128], op=ALU.subtract), reads=[K_("usb")], writes=[PK(7), K_("vn")])
            yield
            mm(pb[7][:, 128:256], Kd[q][:], vn[q][:], [K_("Kd"), K_("vn")], 7)
            yield
            mm(pb[7][:, 256:384], Qt[q][:], S[:], [K_("Qt"), "S"], 7, start=True, stop=False)
            yield
            mm(pb[7][:, 256:384], qkT[q][:], vn[q][:], [K_("qkT"), K_("vn")], 7, start=False, stop=True)
            yield
            s.op("dve", lambda e, q=q: e.scalar_tensor_tensor(S[:], S[:], small[q][:, 3:4], pb[7][:, 128:256], op0=ALU.mult, op1=ALU.add), reads=["S", K_("dec")], writes=[PK(7), "S"])
            yield
            s.op("dve", lambda e, q=q: e.tensor_scalar_mul(osb[q][:], pb[7][:, 256:384], rq[q][:, 0:1]), reads=[K_("rq")], writes=[PK(7), K_("osb")])
            yield
            s.op("pool", lambda e, q=q: e.tensor_tensor(osq[q][:], osb[q][:], osb[q][:], op=ALU.mult), reads=[K_("osb")], writes=[K_("osq")])
            yield
            s.op("dve", lambda e, q=q: e.reduce_sum(sso[q][:], osq[q][:], axis=AX.X), reads=[K_("osq")], writes=[K_("sso")])
            yield
            s.op("act", lambda e, q=q: e.activation(ro[q][:], sso[q][:], AF.Ln, bias=cst[:, 0:1], scale=1.0 / 128), reads=[K_("sso")] + CK, writes=[K_("ro")])
            yield
            s.op("act", lambda e, q=q: e.activation(ro[q][:], ro[q][:], AF.Exp, scale=-0.5), reads=[K_("ro")], writes=[K_("ro")])
            yield
            s.op("dve", lambda e, q=q: e.scalar_tensor_tensor(og[q][:], osb[q][:], ro[q][:, 0:1], nw[:], op0=ALU.mult, op1=ALU.mult), reads=[K_("osb"), K_("ro"), "nw"], writes=[K_("og")])
            yield
            s.op("pool", lambda e, q=q, tt=tt: e.tensor_tensor(og[q][:], og[q][:], zs[p][:, tt, :], op=ALU.mult), reads=[K_("og"), ("zs", p, tt)], writes=[K_("og")])
            yield
            tr(pb[6][:, 256:384], og[q][:], [K_("og")], 6)
            yield
            s.op("act", lambda e, tt=tt: e.copy(ostg[p][:, tt * 128:(tt + 1) * 128], pb[6][:, 256:384]), writes=[PK(6), ("ostg", p, tt)])
            pbk = blk0 + tt
            col = 0 if pbk == 0 else 128 + (((pbk - 1) % BPR) * 8 + (pbk - 1) // BPR) * 128
            s.dma("sp", oT_out[:, col:col + 128], ostg[p][:, tt * 128:(tt + 1) * 128], reads=[("ostg", p, tt)], writes=[("MOo", tb, tt)])
            yield
        for tt in range(n // 128):
            drive(do_chunk(tt))

    pend = [None]

    def drive(g):
        a = pend[0]
        done_b = False
        while True:
            if a is not None:
                try:
                    next(a)
                except StopIteration:
                    a = None
            if not done_b:
                if next(g) == "SPLIT":
                    done_b = True
            if a is None and done_b:
                break
        pend[0] = g

    for tb in range(nblk):
        do_block(tb)
    for _ in pend[0]:
        pass
    s.coll("AllGather", D["MOo"], D["GO"], reads=[("MOo", tb, x) for tb in range(nblk) for x in range(4)], writes=["GO"])
    s.end_phase()


POOL_WINDOWS = (2, 4, 8, 16)


def phase_P(s, pb, D, BPR, pool):
    NT = BPR + 1
    TOK = NT * 128
    HP, B0 = D["HP"], D["B0"]
    s.begin_phase()
    ypT = s.sb("ypT", [128, 4, TOK], BF16)
    wus = s.sb("wus", [128, 8, 512], BF16)
    wps = s.sb("wps", [128, 4, 128], BF16)
    psc = s.sb("psc", [128, 4], F32)
    fix = s.sb("fix", [128, 4, 128], F32)
    hbuf = s.sb("hbuf", [128, 8, 528], BF16)
    lbf = s.sb("lbf", [128, 8, 16], F32)
    lbc = s.sb("lbc", [128, 9, 8, 16], F32)
    oh9 = s.sb("oh9", [128, 9], F32)
    s.dma("sp", oh9[:], D["oh9"], writes=["oh9"])
    uc = s.sb("uc", [128, 528], F32)
    sA = s.sb("sA", [128, 528], F32)
    sB = s.sb("sB", [128, 528], F32)
    pl = s.sb("pl", [128, 512], BF16)
    s.dma("pool", wus[:], pool["wu"].rearrange("(k p) f -> p k f", p=128), writes=["wu"])
    s.dma("pool", wps[:].rearrange("c g d -> c (g d)"), pool["wp"], writes=["wp"])
    s.dma("sp", psc[:], pool["pscale"], writes=["psc"])
    s.dma("sp", fix[:], pool["fix"], writes=["fix"])
    CW = min(512, BPR * 128)
    parts = [(0, 128, True, None)] + [(128 + m * CW, CW, False, m) for m in range(BPR * 128 // CW)]
    zb = [0]
    for (tok0, n, isb0, m) in parts:
        W = n + 16
        if isb0:
            s.op("pool", lambda e: e.memset(hbuf[:, :, 0:16], 0.0), writes=["hbufL"])
            for k in range(8):
                s.dma("pool", hbuf[:, k, 16:16 + n], B0[k * 128:(k + 1) * 128, :], writes=[("hbuf", k)])
        else:
            if m > 0:
                s.op("act", lambda e: e.copy(hbuf[:, :, 0:16], hbuf[:, :, 512:528]), reads=[("hbuf", k) for k in range(8)] + ["hbufL"], writes=["hbufL"])
            for k in range(8):
                s.dma("pool", hbuf[:, k, 16:16 + n], HP[k * 128:(k + 1) * 128, m * CW:m * CW + n], writes=[("hbuf", k)])
            if m == 0:
                for r in range(9):
                    s.dma("sp", lbc[:, r, :, :].rearrange("p k f -> p (k f)"), D["LBs"][r * 128:(r + 1) * 128, :], writes=[("lbc", r)])
                s.op("dve", lambda e: e.tensor_scalar_mul(lbf[:], lbc[:, 0, :, :], oh9[:, 0:1]), reads=[("lbc", 0), "oh9"], writes=["lbf"])
                for r in range(1, 9):
                    s.op("dve", lambda e, r=r: e.scalar_tensor_tensor(lbf[:], lbc[:, r, :, :], oh9[:, r:r + 1], lbf[:], op0=ALU.mult, op1=ALU.add),
                         reads=[("lbc", r), "oh9", "lbf"], writes=["lbf"])
                s.op("act", lambda e: e.copy(hbuf[:, :, 0:16], lbf[:]), reads=["lbf"], writes=["hbufL"])
        hk = ["hbufL"] + [("hbuf", k) for k in range(8)] + [("hbufL", k) for k in range(8)]
        for g in range(4):
            win = POOL_WINDOWS[g]
            bA, bB = zb[0] % 2, 6 + zb[0] % 2
            zb[0] += 1
            for k in range(8):
                s.op("pe", lambda e, k=k, bA=bA, g=g: e.matmul(pb[bA][:, 0:16], wus[:, k, g * 128:(g + 1) * 128], hbuf[:, k, 0:16], start=(k == 0), stop=(k == 7)),
                     reads=["wu"] + hk, writes=[("pb", bA)])
            for k in range(8):
                s.op("pe", lambda e, k=k, bB=bB, g=g, n=n: e.matmul(pb[bB][:, 0:n], wus[:, k, g * 128:(g + 1) * 128], hbuf[:, k, 16:16 + n], start=(k == 0), stop=(k == 7)),
                     reads=["wu"] + hk, writes=[("pb", bB)])
            s.op("act", lambda e, bA=bA: e.copy(uc[:, 0:16], pb[bA][:, 0:16]), writes=[("pb", bA), "ucA"])
            s.op("dve", lambda e, bB=bB, n=n: e.tensor_copy(uc[:, 16:16 + n], pb[bB][:, 0:n]), writes=[("pb", bB), "ucB"])
            src, skey = uc, ["ucA", "ucB"]
            bufs = [(sA, "sA"), (sB, "sB")]
            st = 0
            while (1 << st) < win:
                sh = 1 << st
                lo = (1 << (st + 1)) - 1
                dst, dkey = bufs[st % 2]
                s.op("dve", lambda e, dst=dst, src=src, lo=lo, sh=sh, W=W: e.tensor_tensor(dst[:, lo:W], src[:, lo:W], src[:, lo - sh:W - sh], op=ALU.add),
                     reads=skey, writes=[dkey])
                src, skey = dst, [dkey]
                st += 1
            if isb0:
                s.op("dve", lambda e, src=src, g=g: e.tensor_tensor(src[:, 16:144], src[:, 16:144], fix[:, g, :], op=ALU.mult), reads=skey + ["fix"], writes=skey)
            s.op("dve", lambda e, src=src, W=W, n=n, win=win: e.scalar_tensor_tensor(pl[:, 0:n], src[:, 16:W], 1.0 / win, uc[:, 16:W], op0=ALU.mult, op1=ALU.subtract),
                 reads=skey + ["ucB"], writes=["pl"])
            b = 4 + g % 2
            s.op("pe", lambda e, b=b, n=n, g=g: e.matmul(pb[b][:, 0:n], wps[:, g, :], pl[:, 0:n], start=True, stop=True), reads=["wp", "pl"], writes=[("pb", b)])
            s.op("act", lambda e, b=b, n=n, g=g, tok0=tok0: e.activation(ypT[:, g, tok0:tok0 + n], pb[b][:, 0:n], AF.Identity, scale=psc[:, g:g + 1]),
                 reads=["psc"], writes=[("pb", b), ("ypT", g, tok0)])
    for g in range(4):
        s.dma("sp", D["YP"][g * 128:(g + 1) * 128, :], ypT[:, g, :], reads=[("ypT", g, tok0) for (tok0, n, _, _) in parts], writes=[("YP", g)])
    s.end_phase()

def phase_T(s, pb, D, BPR, even, last, w_out, w1, w2, lnp_d, pool=None, prologue=False):
    NT = BPR + 1
    TOK = NT * 128
    s.begin_phase()
    ident = s.sb("ident", [128, 128], F32)
    s.dma("sp", ident[:], D["ident"], writes=["ident"])
    rt = [s.sb("rt%d" % i, [128, 1024], F32) for i in range(2)]
    ot = [s.sb("ot%d" % i, [128, 1024], F32) for i in range(2)]
    Hs, HP, B0 = D["Hs"], D["HP"], D["B0"]

    def transposes(src, src_key, dst_fn):
        for hb in range(2):
            bank = pb[2 + hb]
            bk = ("pb", 2 + hb)
            for j in range(4):
                k = hb * 4 + j
                s.op("pe", lambda e, k=k, j=j, bank=bank: e.transpose(bank[:, j * 128:(j + 1) * 128], src[:, k * 128:(k + 1) * 128], ident[:]),
                     reads=[src_key, "ident"], writes=[bk])
            dst_fn(hb, bank, bk)

    def emit_hT(t, par):
        def dst_fn(hb, bank, bk):
            if hb == 0:
                f = lambda e: e.copy(rt[par][:, 0:512].rearrange("p (j t) -> p j t", j=4), bank[:].rearrange("p (j t) -> p j t", j=4))
                s.op("act", f, writes=[bk, ("rt", par, 0)])
            else:
                f = lambda e: e.tensor_copy(rt[par][:, 512:1024].rearrange("p (j t) -> p j t", j=4), bank[:].rearrange("p (j t) -> p j t", j=4))
                s.op("dve", f, writes=[bk, ("rt", par, 1)])
        transposes(ot[par], ("ot", par), dst_fn)
        for k in range(8):
            if t == 0:
                dst = B0[k * 128:(k + 1) * 128, :]
                wk_ = ("B0", k)
            else:
                dst = HP[k * 128:(k + 1) * 128, (t - 1) * 128:t * 128]
                wk_ = ("HP", k, t)
            s.dma("sp", dst, rt[par][:, k * 128:(k + 1) * 128], reads=[("rt", par, k // 4)], writes=[wk_])

    if prologue:
        for t in range(NT):
            par = t % 2
            s.dma("sp", ot[par][:], D["h0"][t * 128:(t + 1) * 128, :], writes=[("ot", par)])
            s.dma("sp", Hs[t * 128:(t + 1) * 128, :], ot[par][:], reads=[("ot", par)], writes=[("Hs", t)])
            emit_hT(t, par)
        finish_T(s, D, BPR)
        s.end_phase()
        return

    wo = s.sb("wo", [128, 8, 1024], BF16)
    lnp = s.sb("lnp", [128, 4, 1024], F32)
    acc = s.sb("acc", [128, NT, 1024], F32)
    h1T = s.sb("h1T", [128, 8, TOK], BF16)
    w1b = [s.sb("w1b%d" % i, [128, 8, FFB], BF16) for i in range(2)]
    w2b = [s.sb("w2b%d" % i, [128, FFB // 128, 1024], BF16) for i in range(2)]
    gT = [s.sb("gT%d" % i, [128, FFB // 128, 512], BF16) for i in range(2)]
    aT = [s.sb("aT%d" % i, [128, 512], BF16) for i in range(2)]
    ht = [s.sb("ht0", [128, 1024], F32)] * 2
    ct = [s.sb("ct%d" % i, [128, 8, 128], BF16) for i in range(2)]
    ctf = [s.sb("ctf0", [128, 8, 128], F32)] * 2
    cand = [s.sb("cand0", [128, 8, 128], F32)] * 2
    oh8 = s.sb("oh8", [128, 8], F32)
    s.dma("sp", oh8[:], D["oh8"], writes=["oh8"])
    sm = {"st": [s.sb("st%d" % i, [128, 12], F32) for i in range(2)],
          "mv": [s.sb("mv%d" % i, [128, 2], F32) for i in range(2)],
          "rstd": [s.sb("rstd%d" % i, [128, 1], F32) for i in range(2)],
          "nmr": [s.sb("nmr%d" % i, [128, 1], F32) for i in range(2)],
          "xn": [s.sb("xn%d" % i, [128, 1024], F32) for i in range(2)],
          "eps": s.sb("eps", [128, 1], F32)}
    s.op("dve", lambda e: e.memset(sm["eps"][:], LN_EPS), writes=["eps"])
    s.dma("sp", lnp[:], lnp_d, writes=["lnp"])
    s.dma("pool", wo[:], w_out.rearrange("(k p) f -> p k f", p=128), writes=["wo"])
    w1_v = w1.rearrange("(k p) f -> p k f", p=128)
    w2_v = w2.rearrange("(c p) f -> p c f", p=128)


    nh = 4 if even else 8
    for t in range(NT):
        par = t % 2
        s.dma("sp", ht[par][:], Hs[t * 128:(t + 1) * 128, :], writes=[("ht", 0)])
        for hd in range(nh):
            if even:
                J = BPR // 2
                if t == 0:
                    rows = D["GA"][(2 * hd) * 128:(2 * hd + 1) * 128, :]
                    src1 = rows[:, 8 * J * 128:(8 * J + 1) * 128]
                else:
                    rk = 2 * hd + (t % 2)
                    jst = (t // 2 - 1) if t % 2 == 0 else (t - 1) // 2
                    srcc = D["GA"][rk * 128:(rk + 1) * 128, jst * 1024:(jst + 1) * 1024]
            else:
                go = D["GO"][hd * 128:(hd + 1) * 128, :]
                if t == 0:
                    src1 = go[:, 0:128]
                else:
                    srcc = go[:, 128 + (t - 1) * 1024:128 + t * 1024]
            if t == 0:
                s.dma("sp", ctf[par][:, hd, :], src1, writes=[("ctf", 0, hd)])
            else:
                cp = 0
                s.dma("sp", cand[cp][:].rearrange("p c q -> p (c q)"), srcc, writes=[("cand", cp)])
                s.op("dve", lambda e, cp=cp, par=par, hd=hd: e.tensor_scalar_mul(ctf[par][:, hd, :], cand[cp][:, 0, :], oh8[:, 0:1]), reads=[("cand", cp), "oh8"], writes=[("ctf", 0, hd)])
                for cc in range(1, 8):
                    s.op("dve", lambda e, cp=cp, par=par, hd=hd, cc=cc: e.scalar_tensor_tensor(ctf[par][:, hd, :], cand[cp][:, cc, :], oh8[:, cc:cc + 1], ctf[par][:, hd, :],
                                                                                        op0=ALU.mult, op1=ALU.add),
                         reads=[("cand", cp), "oh8", ("ctf", 0, hd)], writes=[("ctf", 0, hd)])
        s.op("act", lambda e, par=par: e.copy(ct[par][:, 0:nh, :], ctf[par][:, 0:nh, :]), reads=[("ctf", 0, hd) for hd in range(nh)], writes=[("ct", par)])
        ckeys = [("ct", par)]
        if even:
            for g in range(4):
                s.dma("sp", ct[par][:, 4 + g, :], D["YP"][g * 128:(g + 1) * 128, t * 128:(t + 1) * 128], writes=[("ctp", par, g)])
                ckeys.append(("ctp", par, g))
        for hb in range(2):
            for k in range(8):
                s.op("pe", lambda e, k=k, hb=hb, par=par: e.matmul(pb[hb][:], ct[par][:, k, :], wo[:, k, hb * 512:(hb + 1) * 512], start=(k == 0), stop=(k == 7)),
                     reads=ckeys + ["wo"], writes=[("pb", hb)])
            s.op("dve", lambda e, hb=hb, par=par: e.scalar_tensor_tensor(rt[par][:, hb * 512:(hb + 1) * 512], ht[par][:, hb * 512:(hb + 1) * 512],
                                                                     ALPHA, pb[hb][:], op0=ALU.mult, op1=ALU.add),
                 reads=[("ht", 0)], writes=[("pb", hb), ("rt", par, hb)])
        layer_norm(s, "a", rt[par][:], [("rt", par, 0), ("rt", par, 1)], lnp[:, 0, :], lnp[:, 1, :], ot[par][:], ("ot", par), sm, par)
        s.op("act", lambda e, t=t, par=par: e.mul(acc[:, t, :], ot[par][:], ALPHA), reads=[("ot", par)], writes=[("acc", t)])

        def dst_fn(hb, bank, bk, t=t, par=par):
            if hb == 0:
                f = lambda e: e.copy(h1T[:, 0:4, t * 128:(t + 1) * 128], bank[:].rearrange("p (j t) -> p j t", j=4))
                s.op("act", f, writes=[bk, ("h1T", t, hb)])
            else:
                f = lambda e: e.tensor_copy(h1T[:, 4:8, t * 128:(t + 1) * 128], bank[:].rearrange("p (j t) -> p j t", j=4))
                s.op("dve", f, writes=[bk, ("h1T", t, hb)])
        transposes(ot[par], ("ot", par), dst_fn)

    groups = [(g * 512, 512) for g in range(NT // 4)] + ([(NT // 4 * 512, (NT % 4) * 128)] if NT % 4 else [])
    NFB = 4096 // FFB
    NC_ = FFB // 128
    steps = [(fb, gi) for fb in range(NFB) for gi in range(len(groups))]

    def load_w(fb):
        bi = fb % 2
        s.dma("pool", w1b[bi][:], w1_v[:, :, fb * FFB:(fb + 1) * FFB], writes=[("w1b", bi)])
        s.dma("pool", w2b[bi][:], w2_v[:, fb * NC_:(fb + 1) * NC_, :], writes=[("w2b", bi)])

    def stage_A(i):
        fb, gi = steps[i]
        t0, n = groups[gi]
        bi = fb % 2
        gp = i % 2
        for c in range(NC_):
            pa = 4 + (c % 2)
            for k in range(8):
                s.op("pe", lambda e, k=k, c=c, pa=pa, bi=bi, t0=t0, n=n: e.matmul(pb[pa][:, 0:n], w1b[bi][:, k, c * 128:(c + 1) * 128], h1T[:, k, t0:t0 + n],
                                                                           start=(k == 0), stop=(k == 7)),
                     reads=[("w1b", bi)] + [("h1T", t, hb) for t in range(t0 // 128, (t0 + n) // 128) for hb in range(2)], writes=[("pb", pa)])
            ap_ = c % 2
            s.op("act", lambda e, pa=pa, ap_=ap_, n=n: e.activation(aT[ap_][:, 0:n], pb[pa][:, 0:n], AF.Relu), writes=[("pb", pa), ("aT", ap_)])
            s.op("pool", lambda e, gp=gp, c=c, ap_=ap_, n=n: e.tensor_tensor(gT[gp][:, c, 0:n], aT[ap_][:, 0:n], aT[ap_][:, 0:n], op=ALU.mult),
                 reads=[("aT", ap_)], writes=[("gT", gp, c)])

    zrot = [0]

    def stage_Z(i):
        fb, gi = steps[i]
        t0, n = groups[gi]
        bi = fb % 2
        gp = i % 2
        for tt in range(n // 128):
            t = t0 // 128 + tt
            for hb in range(2):
                zb_ = [0, 1, 6, 7][zrot[0] % 4]
                zrot[0] += 1
                for c in range(NC_):
                    s.op("pe", lambda e, c=c, gp=gp, tt=tt, hb=hb, bi=bi, zb_=zb_: e.matmul(pb[zb_][:], gT[gp][:, c, tt * 128:(tt + 1) * 128], w2b[bi][:, c, hb * 512:(hb + 1) * 512],
                                                                                     start=(c == 0), stop=(c == NC_ - 1)),
                         reads=[("gT", gp, c), ("w2b", bi)], writes=[("pb", zb_)])
                s.op("dve", lambda e, t=t, hb=hb, zb_=zb_: e.tensor_tensor(acc[:, t, hb * 512:(hb + 1) * 512], acc[:, t, hb * 512:(hb + 1) * 512], pb[zb_][:], op=ALU.add),
                     reads=[("acc", t)], writes=[("pb", zb_), ("acc", t)])

    load_w(0)
    load_w(1)
    stage_A(0)
    for i in range(len(steps)):
        if i + 1 < len(steps):
            stage_A(i + 1)
        stage_Z(i)
        fb, gi = steps[i]
        if gi == len(groups) - 1 and fb + 2 < NFB:
            load_w(fb + 2)

    for t in range(NT):
        par = t % 2
        layer_norm(s, "c", acc[:, t, :], ("acc", t), lnp[:, 2, :], lnp[:, 3, :], ot[par][:], ("ot", par), sm, par)
        if t == 0:
            s.op("pool", lambda e, par=par: e.memset(ot[par][0:112, :], 0.0), reads=[("ot", par)], writes=[("ot", par)])
        if last:
            if t >= 1:
                s.dma("sp", D["out"][(t - 1) * 128:t * 128, :], ot[par][:], reads=[("ot", par)], is_out=True)
        else:
            s.dma("sp", Hs[t * 128:(t + 1) * 128, :], ot[par][:], reads=[("ot", par)], writes=[("Hs", t)])
            emit_hT(t, par)
    if not last:
        finish_T(s, D, BPR)
    s.end_phase()


def finish_T(s, D, BPR):
    HP, B0, G1, LBs, SPQ = D["HP"], D["B0"], D["G1"], D["LBs"], D["SPQ"]
    hpk = [("HP", k, t) for k in range(8) for t in range(1, BPR + 1)]
    b0k = [("B0", k) for k in range(8)]
    s.coll("AllGather", HP, G1, reads=hpk, writes=["G1"])
    W = BPR * 128
    for k in range(8):
        s.dma("sp", LBs[0:128, k * 16:(k + 1) * 16], B0[k * 128:(k + 1) * 128, 112:128], reads=b0k, writes=[("LBs", 0, k)])
        for r in range(8):
            s.dma("sp", LBs[(r + 1) * 128:(r + 2) * 128, k * 16:(k + 1) * 16], G1[r * 1024 + k * 128:r * 1024 + (k + 1) * 128, W - 16:W], reads=["G1"], writes=[("LBs", r + 1, k)])
    s.dma("sp", SPQ[0:1024, :], B0, reads=b0k, writes=["SPQ0"])

import math
I32 = mybir.dt.int32
_FPROG = {}


def build_fused(BPR):
    nc = bass.Bass("TRN2", target_bir_lowering=False)
    NT = BPR + 1
    nb = 8 * BPR + 1
    J = BPR // 2
    X = lambda n, sh, dt=F32: nc.dram_tensor(n, list(sh), dt, kind="ExternalInput").ap()
    I = lambda n, sh: nc.dram_tensor(n, list(sh), F32).ap()
    D = {"oh2": X("oh2", [128, 4]), "oh8": X("oh8", [128, 8]), "oh9": X("oh9", [128, 9]), "ident": X("ident", [128, 128]), "U": X("U", [128, 128]), "Ms": X("Ms", [128, 128]), "Mc": X("Mc", [128, 128]),
         "h0": X("h0", [NT * 128, 1024]),
         "out": nc.dram_tensor("out", [BPR * 128, 1024], F32, kind="ExternalOutput").ap(),
         "Hs": I("Hs", [NT * 128, 1024]), "HP": I("HP", [1024, BPR * 128]), "G1": I("G1", [8 * 1024, BPR * 128]), "B0": I("B0", [1024, 128]),
         "SPQ": I("SPQ", [2048, 128]), "LBs": I("LBs", [9 * 128, 128]),
         "MOe": I("MOe", [128, 9 * J * 128]), "GA": I("GA", [8 * 128, 9 * J * 128]),
         "YP": nc.dram_tensor("YP", [512, NT * 128], BF16).ap(), "MOo": I("MOo", [128, nb * 128]), "GO": I("GO", [8 * 128, nb * 128])}
    fixd = X("fix", [128, 4, 128])
    LW = []
    for i in range(4):
        j = i // 2
        d = {"w_out": X("w_out%d" % i, [1024, 1024]), "w1": X("w1_%d" % i, [1024, 4096]), "w2": X("w2_%d" % i, [4096, 1024]), "lnp": X("lnp%d" % i, [128, 4, 1024])}
        if i % 2 == 0:
            d["E"] = {n: X("%s_e%d" % (n, j), sh) for n, sh in
                      [("wq", [1024, 128]), ("wk", [1024, 128]), ("wv", [1024, 128]), ("near0", [128, 3, 128]), ("near", [128, 3, 128]), ("nearS", [128, 128]),
                       ("kbias", [128, 2]), ("lamrep", [128, 4, 64]), ("sublnw", [128, 128]), ("cst", [128, 4])]}
            d["pool"] = {"wu": X("wu_e%d" % j, [1024, 512]), "wp": X("wp_e%d" % j, [128, 512]), "pscale": X("pscale_e%d" % j, [128, 4]), "fix": fixd}
        else:
            d["O"] = {n: X("%s_o%d" % (n, j), sh) for n, sh in
                      [("wq", [1024, 128]), ("wk", [1024, 128]), ("wv", [1024, 128]), ("wz", [1024, 128]), ("wba", [1024, 2]), ("convw", [128, 3, 4]),
                       ("avec", [128, 2]), ("normw", [128, 128])]}
        LW.append(d)
    s = Sched(nc)
    pb = [s.ps("pb%d" % i, [128, 512]) for i in range(8)]
    s.begin_phase()
    z = s.sb("zt", [128, 128], F32)
    s.op("dve", lambda e: e.memset(z[:], 0.0), writes=["z"])
    for k in range(8):
        s.dma("sp", D["SPQ"][1024 + k * 128:1024 + (k + 1) * 128, :], z[:], reads=["z"], writes=[("SPQ1", k)])
    s.end_phase()
    import os
    STOP = 99
    ph = [0]
    def go():
        ph[0] += 1
        return ph[0] <= STOP
    if go():
        phase_T(s, pb, D, BPR, False, False, None, None, None, None, prologue=True)
    for i in range(4):
        d = LW[i]
        if i % 2 == 0:
            if go():
                phase_E(s, pb, D, BPR, d["E"])
        else:
            if go():
                phase_O(s, pb, D, BPR, d["O"])
        if i % 2 == 0 and go():
            phase_P(s, pb, D, BPR, d["pool"])
        if go():
            phase_T(s, pb, D, BPR, i % 2 == 0, i == 3, d["w_out"], d["w1"], d["w2"], d["lnp"], pool=d.get("pool"))
    s.finish(); s.emit()
    return nc


def fused_inputs(BPR, x, meta_tokens, rel_bias, ev_w_in, ev_lambda, ev_subln_w, ev_pool_w, ev_pool_scale, ev_w_out,
                 od_w_in, od_conv_w, od_a_log, od_dt_bias, od_norm_w, od_w_out, mlp_w1, mlp_w2, ln_mix_g, ln_mix_b, ln_mlp_g, ln_mlp_b):
    f32 = np.float32
    A = lambda a: np.ascontiguousarray(np.asarray(a, dtype=f32))
    x = A(x)[0]
    idx = np.arange(128)
    common = {"ident": np.eye(128, dtype=f32), "U": (idx[:, None] <= idx[None, :]).astype(f32), "Ms": (idx[None, :] > idx[:, None]).astype(f32),
              "Mc": (idx[None, :] >= idx[:, None]).astype(f32)}
    fix = np.ones((128, 4, 128), f32)
    p = idx - 112
    for g, win in enumerate(POOL_WINDOWS):
        fix[:, g, :] = np.where(p >= 0, win / np.minimum(np.maximum(p, 0) + 1, win), 1.0).astype(f32)[None, :]
    common["fix"] = fix
    for i in range(4):
        common["w_out%d" % i] = A(ev_w_out[i // 2] if i % 2 == 0 else od_w_out[i // 2])
        common["w1_%d" % i] = A(mlp_w1[i]); common["w2_%d" % i] = A(mlp_w2[i])
        common["lnp%d" % i] = A(np.broadcast_to(np.stack([A(ln_mix_g[i]), A(ln_mix_b[i]), A(ln_mlp_g[i]), A(ln_mlp_b[i])])[None], (128, 4, 1024)))
    for j in range(2):
        common["wu_e%d" % j] = A(A(ev_w_in[j])[:, 1536:2048])
        common["wp_e%d" % j] = A(A(ev_pool_w[j]).transpose(1, 0, 2).reshape(128, 512))
        common["pscale_e%d" % j] = A(A(ev_pool_scale[j]).reshape(4, 128).T)
    rb_all = A(rel_bias)
    ims = []
    allneg = np.full((128, 128), NEG, f32)
    for c in range(8):
        hd, half = c // 2, c % 2
        m = dict(common)
        oh2 = np.zeros((128, 4), f32); oh2[:, 0] = float(half == 1); oh2[:, 1] = float(half == 0)
        oh8 = np.zeros((128, 8), f32); oh8[:, c] = 1.0
        oh9 = np.zeros((128, 9), f32); oh9[:, c] = 1.0
        m["oh2"] = oh2; m["oh8"] = oh8; m["oh9"] = oh9
        h0 = np.zeros(((BPR + 1) * 128, 1024), f32)
        h0[112:128] = A(meta_tokens)
        h0[128:] = x[c * BPR * 128:(c + 1) * BPR * 128]
        m["h0"] = h0
        rb = rb_all[:, hd]
        if half == 0:
            near = np.stack([bias_tile(rb, 6, 4), bias_tile(rb, 6, 5), bias_tile(rb, 6, 6)], axis=1)
            near0 = np.stack([bias_tile(rb, 2, 0), bias_tile(rb, 2, 1), bias_tile(rb, 2, 2)], axis=1)
            nearS = bias_tile(rb, 0, 0)
        else:
            near = np.stack([bias_tile(rb, 5, 4), bias_tile(rb, 5, 5), allneg], axis=1)
            near0 = np.stack([bias_tile(rb, 1, 0), bias_tile(rb, 1, 1), allneg], axis=1)
            nearS = np.zeros((128, 128), f32)
        kbias = np.empty((128, 2), f32)
        kbias[:, 0] = rb[31]
        kbias[:, 1] = np.where(idx < 112, f32(NEG), rb[31])
        for j in range(2):
            w_in = A(ev_w_in[j])
            lambda_init = 0.8 - 0.6 * math.exp(-0.3 * (2 * j))
            m["wq_e%d" % j] = A(w_in[:, hd * 128:(hd + 1) * 128]); m["wk_e%d" % j] = A(w_in[:, 512 + hd * 128:512 + (hd + 1) * 128])
            m["wv_e%d" % j] = A(w_in[:, 1024 + hd * 128:1024 + (hd + 1) * 128])
            m["near0_e%d" % j] = A(near0); m["near_e%d" % j] = A(near); m["nearS_e%d" % j] = A(nearS); m["kbias_e%d" % j] = kbias
            m["lamrep_e%d" % j] = A(np.broadcast_to(A(ev_lambda[j])[None], (128, 4, 64)))
            m["sublnw_e%d" % j] = A(np.broadcast_to(A(ev_subln_w[j])[None], (128, 128)))
            m["cst_e%d" % j] = A(np.broadcast_to(np.array([lambda_init, 1.0 - lambda_init, 1e-6, 0.0], f32)[None], (128, 4)))
            po = prep_O(c, None, A(od_w_in[j]), A(od_conv_w[j]), A(od_a_log[j]), A(od_dt_bias[j]), A(od_norm_w[j]))
            for n in ("wq", "wk", "wv", "wz", "wba", "convw", "avec", "normw"):
                m["%s_o%d" % (n, j)] = po[n]
        ims.append(m)
    return ims


def kernel_fused(BPR, **inp):
    if BPR not in _FPROG:
        _FPROG[BPR] = build_fused(BPR)
    ims = fused_inputs(BPR, **inp)
    res = run_bass_kernel_spmd(_FPROG[BPR], ims, core_ids=list(range(8)))
    return np.ascontiguousarray(np.concatenate([res.results[c]["out"] for c in range(8)], 0)[None])

def kernel(**inputs):
    return kernel_fused(16, **inputs)
```

```python
import numpy as np
import contextlib
import concourse.bass as bass
import concourse.mybir as mybir
from concourse.bass_utils import run_bass_kernel_spmd

F32 = mybir.dt.float32
BF16 = mybir.dt.bfloat16
ALU = mybir.AluOpType
AF = mybir.ActivationFunctionType
AX = mybir.AxisListType


class Sched:
    COMPUTE = ("pe", "act", "dve", "pool")
    RING = 24
    COLL_INC = 16

    def __init__(self, nc, same_engine_sync=True):
        self.nc = nc
        self.stack = contextlib.ExitStack()
        self.ops = {e: [] for e in ("pe", "act", "dve", "pool", "sp")}
        self.cnt = {e: 0 for e in self.COMPUTE}
        self.seen = {e: {} for e in self.ops}
        self.res = {}
        self.dma_n = 0
        self.same = same_engine_sync
        self.sem = {e: self.stack.enter_context(nc.semaphore("c_" + e)) for e in self.COMPUTE}
        self.dsem = [self.stack.enter_context(nc.semaphore("d%d" % i)) for i in range(self.RING)]
        self.out_dmas = []
        self.regs = {}
        self.phase = 0
        self.pstack = None
        self.pclose = []
        self.coll_n = 0
        self.csem = self.stack.enter_context(nc.semaphore("coll"))

    def sb(self, name, shape, dt):
        st = self.pstack if self.pstack is not None else self.stack
        return st.enter_context(self.nc.sbuf_tensor("s_%s_p%d" % (name, self.phase), list(shape), dt))

    def begin_phase(self):
        self.phase += 1
        self.pstack = contextlib.ExitStack()

    def end_phase(self):
        targets = {("c", e): self.cnt[e] for e in self.COMPUTE if self.cnt[e] > 0}
        for n in range(max(0, self.dma_n - self.RING), self.dma_n):
            sk, v = self._semval(("dma", n))
            targets[sk] = max(targets.get(sk, 0), v)
        if self.coll_n:
            targets[("k", 0)] = self.coll_n
        for e in self.ops:
            waits = []
            for sk, v in targets.items():
                if sk == ("c", e):
                    continue
                if self.seen[e].get(sk, 0) >= v:
                    continue
                self.seen[e][sk] = v
                waits.append((sk, v))
            if waits:
                self.ops[e].append((None, waits, None, 0))
        self.res = {}
        self.pstack.close()
        self.pstack = None

    def ps(self, name, shape, dt=F32):
        return self.stack.enter_context(self.nc.psum_tensor("p_" + name, list(shape), dt))

    def _semval(self, ident):
        if ident[0] == "dma":
            n = ident[1]
            return ("d", n % self.RING), 16 * (n // self.RING + 1)
        if ident[0] == "coll":
            return ("k", 0), ident[1]
        return ("c", ident[1]), ident[2]

    def _deps(self, eng, reads, writes, me):
        need = {}
        for k in reads:
            r = self.res.get(k)
            if r and r["w"] is not None:
                s, v = self._semval(r["w"])
                need[s] = max(need.get(s, 0), v)
        for k in writes:
            r = self.res.get(k)
            if r:
                if r["w"] is not None:
                    s, v = self._semval(r["w"])
                    need[s] = max(need.get(s, 0), v)
                for s, v in r["r"].items():
                    need[s] = max(need.get(s, 0), v)
        waits = []
        for s, v in need.items():
            if s == ("c", eng) and (eng == "pe" or not self.same):
                continue
            if self.seen[eng].get(s, 0) >= v:
                continue
            self.seen[eng][s] = v
            waits.append((s, v))
        for k in reads:
            r = self.res.setdefault(k, {"w": None, "r": {}})
            s, v = self._semval(me)
            r["r"][s] = max(r["r"].get(s, 0), v)
        for k in writes:
            self.res[k] = {"w": me, "r": {}}
        return waits

    def _sem(self, s):
        if s[0] == "k":
            return self.csem
        return self.dsem[s[1]] if s[0] == "d" else self.sem[s[1]]

    def op(self, eng, fn, reads=(), writes=()):
        idx = self.cnt[eng] + 1
        self.cnt[eng] = idx
        me = ("eng", eng, idx)
        waits = self._deps(eng, reads, writes, me)
        self.ops[eng].append((fn, waits, self.sem[eng], 1))

    def dma(self, q, out, in_, reads=(), writes=(), is_out=False):
        n = self.dma_n
        self.dma_n += 1
        me = ("dma", n)
        waits = self._deps(q, reads, writes, me)
        if n >= self.RING:
            s, v = self._semval(("dma", n - self.RING))
            if self.seen[q].get(s, 0) < v:
                self.seen[q][s] = v
                waits.append((s, v))
        fn = lambda e, out=out, in_=in_: e.dma_start(out=out, in_=in_)
        self.ops[q].append((fn, waits, self.dsem[n % self.RING], 16))
        if is_out:
            self.out_dmas.append(me)

    def dma_fn(self, q, fn, reads=(), writes=(), is_out=False):
        n = self.dma_n
        self.dma_n += 1
        me = ("dma", n)
        waits = self._deps(q, reads, writes, me)
        if n >= self.RING:
            s, v = self._semval(("dma", n - self.RING))
            if self.seen[q].get(s, 0) < v:
                self.seen[q][s] = v
                waits.append((s, v))
        self.ops[q].append((fn, waits, self.dsem[n % self.RING], 16))
        if is_out:
            self.out_dmas.append(me)

    def coll(self, kind, in_ap, out_ap, reads=(), writes=()):
        self.coll_n += 1
        me = ("coll", self.coll_n)
        waits = self._deps("pool", reads, writes, me)
        fn = lambda e: e.collective_compute(kind, ALU.bypass, replica_groups=[list(range(8))], ins=[in_ap.opt()], outs=[out_ap.opt()])
        self.ops["pool"].append((fn, waits, self.csem, 1))

    def reg(self, e, eng, name, ap, max_val=1 << 20):
        key = (eng, name)
        if key not in self.regs:
            r = e.alloc_register("r_%s_%s" % (eng, name))
            e.reg_load(r, ap)
            self.regs[key] = e.snap(r, min_val=0, max_val=max_val)
        return self.regs[key]

    def finish(self):
        need = {}
        for ident in self.out_dmas:
            s, v = self._semval(ident)
            need[s] = max(need.get(s, 0), v)
        self.final_waits = list(need.items())

    def emit(self):
        nc = self.nc
        names = {"pe": "tensor", "act": "scalar", "dve": "vector", "pool": "gpsimd", "sp": "sync"}
        with nc.Block() as block:
            for e, bn in names.items():
                lst = self.ops[e]
                extra = self.final_waits if e == "sp" else []

                def body(engobj, lst=lst, extra=extra):
                    for fn, waits, sem, inc in lst:
                        for s, v in waits:
                            engobj.wait_ge(self._sem(s), v)
                        if fn is not None:
                            fn(engobj).then_inc(sem, inc)
                    for s, v in extra:
                        engobj.wait_ge(self._sem(s), v)
                if lst or extra:
                    getattr(block, bn)(body)
        self.stack.close()

import math
import numpy as np

NEG = -30000.0
POOL_WINDOWS = (2, 4, 8, 16)


def t5_bucket(n):
    n = np.maximum(n, 0)
    nf = np.maximum(n, 1).astype(np.float32)
    large = 16 + (np.log(nf / np.float32(16)) / np.float32(math.log(8.0)) * np.float32(16)).astype(np.int32)
    large = np.minimum(large, 31)
    return np.where(n < 16, n, large)


def bias_tile(rb_h, qb, kb):
    kl = np.arange(128)[:, None]
    ql = np.arange(128)[None, :]
    qp = qb * 128 + ql
    kp = kb * 128 + kl
    val = rb_h[t5_bucket(qp - kp)].astype(np.float32)
    allowed = kp <= qp
    if kb == 0:
        padk = kl < 112
        if qb == 0:
            allowed = allowed & (~padk | (ql < 112))
        else:
            allowed = allowed & ~padk
    return np.where(allowed, val, np.float32(NEG)).astype(np.float32)


def prep_E(core, hT_full, w_in, lam_vecs, subln_w, pool_w, pool_scale, rel_bias, lambda_init, nq, nb, npool):
    hd, half = core // 2, core % 2
    lp = nb * 128
    f32 = np.float32
    hTq = np.zeros((1024, nq * 128), f32)
    for i in range(nq):
        qb = 2 * i + half
        if qb < nb:
            hTq[:, i * 128:(i + 1) * 128] = hT_full[:, qb * 128:(qb + 1) * 128]
    hTp = np.zeros((1024, npool + 16), f32)
    p0 = half * npool - 16
    lo = max(p0, 0)
    hTp[:, lo - p0:] = hT_full[:, lo:half * npool + npool]
    rb = rel_bias[:, hd]
    allneg = np.full((128, 128), NEG, f32)
    if half == 0:
        near = np.stack([bias_tile(rb, 4, 3), bias_tile(rb, 4, 4), allneg], axis=1)
        near0 = np.stack([bias_tile(rb, 0, 0), allneg], axis=1)
    else:
        near = np.stack([bias_tile(rb, 5, 3), bias_tile(rb, 5, 4), bias_tile(rb, 5, 5)], axis=1)
        near0 = np.stack([bias_tile(rb, 1, 0), bias_tile(rb, 1, 1)], axis=1)
    kbias = np.empty((128, 2), f32)
    kbias[:, 0] = rb[31]
    kbias[:, 1] = np.where(np.arange(128) < 112, f32(NEG), rb[31])
    win = POOL_WINDOWS[hd]
    pcoef = np.zeros((128, 4), f32)
    pcoef[:, hd] = 1.0 / win
    invfix = np.ones((128, 128), f32)
    if half == 0:
        p = np.arange(128) - 112
        invfix[:, :] = np.where(p >= 0, win / np.minimum(p + 1, win), 1.0).astype(f32)[None, :]
    cst = np.broadcast_to(np.array([lambda_init, 1.0 - lambda_init, 1e-6, 0.0], f32)[None], (128, 4))
    c = np.ascontiguousarray
    return {
        "hT": hT_full, "hTq": hTq, "hTp": hTp,
        "wq": c(w_in[:, hd * 128:(hd + 1) * 128]), "wk": c(w_in[:, 512 + hd * 128:512 + (hd + 1) * 128]),
        "wv": c(w_in[:, 1024 + hd * 128:1024 + (hd + 1) * 128]), "wu": c(w_in[:, 1536 + hd * 128:1536 + (hd + 1) * 128]),
        "wp": c(pool_w[hd]), "near0": c(near0), "near": c(near), "kbias": kbias,
        "lamrep": c(np.broadcast_to(lam_vecs[None], (128, 4, 64))), "sublnw": c(np.broadcast_to(subln_w[None], (128, 128))),
        "cst": c(cst), "pcoef": pcoef, "pscale": c(pool_scale[hd * 128:(hd + 1) * 128, None]), "invfix": invfix,
        "ident": np.eye(128, dtype=f32),
    }


def scatter_E(catT, core, res, nq, nb, npool):
    hd, half = core // 2, core % 2
    for i in range(nq):
        qb = 2 * i + half
        if qb < nb:
            catT[hd * 128:(hd + 1) * 128, qb * 128:(qb + 1) * 128] = res["oT_out"][:, i * 128:(i + 1) * 128]
    catT[512 + hd * 128:512 + (hd + 1) * 128, half * npool:(half + 1) * npool] = res["yT_out"]


def prep_O(core, hT_full, w_in, conv_w, a_log, dt_bias, norm_w):
    hd = core
    f32 = np.float32
    c = np.ascontiguousarray
    idx = np.arange(128)
    convw = np.stack([conv_w[hd * 128:(hd + 1) * 128], conv_w[1024 + hd * 128:1024 + (hd + 1) * 128],
                      conv_w[2048 + hd * 128:2048 + (hd + 1) * 128]], axis=1)
    return {
        "hT": hT_full,
        "wq": c(w_in[:, hd * 128:(hd + 1) * 128]), "wk": c(w_in[:, 1024 + hd * 128:1024 + (hd + 1) * 128]),
        "wv": c(w_in[:, 2048 + hd * 128:2048 + (hd + 1) * 128]), "wz": c(w_in[:, 3072 + hd * 128:3072 + (hd + 1) * 128]),
        "wba": c(np.stack([w_in[:, 4096 + hd], w_in[:, 4104 + hd]], axis=1)),
        "convw": c(convw.astype(f32)),
        "avec": c(np.broadcast_to(np.array([a_log[hd], dt_bias[hd]], f32)[None], (128, 2))),
        "normw": c(np.broadcast_to(norm_w[None], (128, 128))),
        "ident": np.eye(128, dtype=f32),
        "U": (idx[:, None] <= idx[None, :]).astype(f32),
        "Ms": (idx[None, :] > idx[:, None]).astype(f32),
        "Mc": (idx[None, :] >= idx[:, None]).astype(f32),
    }


ALPHA = 8.0 ** 0.25
LN_EPS = 1e-5
NT = 17
TOK = NT * 128
FFB = 256


def layer_norm(s, tag, src, src_key, g, b, dst, dst_key, sm, par):
    st, mv, rstd, nmr, xn, eps = sm["st"][par], sm["mv"][par], sm["rstd"][par], sm["nmr"][par], sm["xn"][par], sm["eps"]
    k = lambda n: (n, par)
    src_keys = src_key if isinstance(src_key, list) else [src_key]
    s.op("dve", lambda e: e.bn_stats(st[:, 0:6], src[:, 0:512]), reads=src_keys, writes=[k("st0")])
    s.op("dve", lambda e: e.bn_stats(st[:, 6:12], src[:, 512:1024]), reads=src_keys, writes=[k("st1")])
    s.op("dve", lambda e: e.bn_aggr(mv[:, 0:2], st[:, 0:12]), reads=[k("st0"), k("st1")], writes=[k("mv")])
    s.op("act", lambda e: e.activation(rstd[:, 0:1], mv[:, 1:2], AF.Sqrt, bias=eps[:, 0:1], scale=1.0),
         reads=[k("mv"), "eps"], writes=[k("rstd")])
    s.op("dve", lambda e: e.reciprocal(rstd[:, 0:1], rstd[:, 0:1]), reads=[k("rstd")], writes=[k("rstd")])
    s.op("dve", lambda e: e.scalar_tensor_tensor(nmr[:, 0:1], mv[:, 0:1], -1.0, rstd[:, 0:1], op0=ALU.mult, op1=ALU.mult),
         reads=[k("mv"), k("rstd")], writes=[k("nmr")])
    s.op("act", lambda e: e.activation(xn[:], src, AF.Identity, bias=nmr[:, 0:1], scale=rstd[:, 0:1]),
         reads=src_keys + [k("nmr"), k("rstd")], writes=[k("xn")])
    s.op("dve", lambda e: e.tensor_tensor(xn[:], xn[:], g, op=ALU.mult), reads=[k("xn"), "lnp"], writes=[k("xn")])
    s.op("pool", lambda e: e.tensor_tensor(dst, xn[:], b, op=ALU.add), reads=[k("xn"), "lnp"], writes=[dst_key])


def build_T(nc, s, h_in, catT, w_out, w1, w2, lnp_d, ident_d, h_out, hT_out):
    wo = s.sb("wo", [128, 8, 1024], BF16)
    lnp = s.sb("lnp", [128, 4, 1024], F32)
    ident = s.sb("ident", [128, 128], F32)
    acc = s.sb("acc", [128, NT, 1024], F32)
    h1T = s.sb("h1T", [128, 8, TOK], BF16)
    w1b = [s.sb("w1b%d" % i, [128, 8, FFB], BF16) for i in range(2)]
    w2b = [s.sb("w2b%d" % i, [128, FFB // 128, 1024], BF16) for i in range(2)]
    gT = [s.sb("gT%d" % i, [128, FFB // 128, 512], BF16) for i in range(2)]
    aT = [s.sb("aT%d" % i, [128, 512], BF16) for i in range(2)]
    ht = [s.sb("ht%d" % i, [128, 1024], F32) for i in range(2)]
    ct = [s.sb("ct%d" % i, [128, 8, 128], BF16) for i in range(2)]
    rt = [s.sb("rt%d" % i, [128, 1024], F32) for i in range(2)]
    ot = [s.sb("ot%d" % i, [128, 1024], F32) for i in range(2)]
    sm = {"st": [s.sb("st%d" % i, [128, 12], F32) for i in range(2)],
          "mv": [s.sb("mv%d" % i, [128, 2], F32) for i in range(2)],
          "rstd": [s.sb("rstd%d" % i, [128, 1], F32) for i in range(2)],
          "nmr": [s.sb("nmr%d" % i, [128, 1], F32) for i in range(2)],
          "xn": [s.sb("xn%d" % i, [128, 1024], F32) for i in range(2)],
          "eps": s.sb("eps", [128, 1], F32)}
    pb = [s.ps("pb%d" % i, [128, 512]) for i in range(8)]

    s.op("dve", lambda e: e.memset(sm["eps"][:], LN_EPS), writes=["eps"])
    s.dma("sp", lnp[:], lnp_d, writes=["lnp"])
    s.dma("sp", ident[:], ident_d, writes=["ident"])
    s.dma("pool", wo[:], w_out.rearrange("(k p) f -> p k f", p=128), writes=["wo"])
    catT_v = catT.rearrange("(k p) t -> p k t", p=128)
    hT_v = hT_out.rearrange("(k p) t -> p k t", p=128)
    w1_v = w1.rearrange("(k p) f -> p k f", p=128)
    w2_v = w2.rearrange("(c p) f -> p c f", p=128)

    def transposes(src, src_key, t, par, dst_fn, dst_key_fn, evac_engs):
        for hb in range(2):
            bank = pb[2 + hb]
            bk = ("pb", 2 + hb)
            for j in range(4):
                k = hb * 4 + j
                s.op("pe", lambda e, k=k, j=j, bank=bank: e.transpose(bank[:, j * 128:(j + 1) * 128], src[:, k * 128:(k + 1) * 128], ident[:]),
                     reads=[src_key, "ident"], writes=[bk])
            dst_fn(hb, bank, bk)

    for t in range(NT):
        par = t % 2
        s.dma("sp", ht[par][:], h_in[t * 128:(t + 1) * 128, :], writes=[("ht", par)])
        s.dma("pool", ct[par][:], catT_v[:, :, t * 128:(t + 1) * 128], writes=[("ct", par)])
        for hb in range(2):
            for k in range(8):
                s.op("pe", lambda e, k=k, hb=hb, par=par: e.matmul(pb[hb][:], ct[par][:, k, :], wo[:, k, hb * 512:(hb + 1) * 512],
                                                              start=(k == 0), stop=(k == 7)),
                     reads=[("ct", par), "wo"], writes=[("pb", hb)])
            s.op("dve", lambda e, hb=hb, par=par: e.scalar_tensor_tensor(rt[par][:, hb * 512:(hb + 1) * 512], ht[par][:, hb * 512:(hb + 1) * 512],
                                                                     ALPHA, pb[hb][:], op0=ALU.mult, op1=ALU.add),
                 reads=[("ht", par), ("pb", hb)], writes=[("rt", par, hb)])
        layer_norm(s, "a", rt[par][:], [("rt", par, 0), ("rt", par, 1)], lnp[:, 0, :], lnp[:, 1, :], ot[par][:], ("ot", par), sm, par)
        s.op("act", lambda e, t=t, par=par: e.mul(acc[:, t, :], ot[par][:], ALPHA), reads=[("ot", par)], writes=[("acc", t)])

        def dst_fn(hb, bank, bk, t=t, par=par):
            eng = "act" if hb == 0 else "dve"
            if eng == "act":
                f = lambda e: e.copy(h1T[:, hb * 4:(hb + 1) * 4, t * 128:(t + 1) * 128], bank[:].rearrange("p (j t) -> p j t", j=4))
            else:
                f = lambda e: e.tensor_copy(h1T[:, hb * 4:(hb + 1) * 4, t * 128:(t + 1) * 128], bank[:].rearrange("p (j t) -> p j t", j=4))
            s.op(eng, f, reads=[bk], writes=[("h1T", t, hb)])
        transposes(ot[par], ("ot", par), t, par, dst_fn, None, None)

    groups = [(g * 512, 512) for g in range(NT // 4)] + ([(NT // 4 * 512, (NT % 4) * 128)] if NT % 4 else [])
    NFB = 4096 // FFB
    NC_ = FFB // 128
    steps = [(fb, gi) for fb in range(NFB) for gi in range(len(groups))]

    def load_w(fb):
        bi = fb % 2
        s.dma("pool", w1b[bi][:], w1_v[:, :, fb * FFB:(fb + 1) * FFB], writes=[("w1b", bi)])
        s.dma("pool", w2b[bi][:], w2_v[:, fb * NC_:(fb + 1) * NC_, :], writes=[("w2b", bi)])

    def stage_A(i):
        fb, gi = steps[i]
        t0, n = groups[gi]
        bi = fb % 2
        gp = i % 2
        for c in range(NC_):
            pa = 4 + (c % 2)
            for k in range(8):
                s.op("pe", lambda e, k=k, c=c, pa=pa, bi=bi, t0=t0, n=n: e.matmul(pb[pa][:, 0:n], w1b[bi][:, k, c * 128:(c + 1) * 128], h1T[:, k, t0:t0 + n],
                                                                           start=(k == 0), stop=(k == 7)),
                     reads=[("w1b", bi)] + [("h1T", t, hb) for t in range(t0 // 128, (t0 + n) // 128) for hb in range(2)], writes=[("pb", pa)])
            ap_ = c % 2
            s.op("act", lambda e, pa=pa, ap_=ap_, n=n: e.activation(aT[ap_][:, 0:n], pb[pa][:, 0:n], AF.Relu), reads=[("pb", pa)], writes=[("aT", ap_)])
            s.op("pool", lambda e, gp=gp, c=c, ap_=ap_, n=n: e.tensor_tensor(gT[gp][:, c, 0:n], aT[ap_][:, 0:n], aT[ap_][:, 0:n], op=ALU.mult),
                 reads=[("aT", ap_)], writes=[("gT", gp, c)])

    zrot = [0]

    def stage_Z(i):
        fb, gi = steps[i]
        t0, n = groups[gi]
        bi = fb % 2
        gp = i % 2
        for tt in range(n // 128):
            t = t0 // 128 + tt
            for hb in range(2):
                zb = [0, 1, 6, 7][zrot[0] % 4]
                zrot[0] += 1
                for c in range(NC_):
                    s.op("pe", lambda e, c=c, gp=gp, tt=tt, hb=hb, bi=bi, zb=zb: e.matmul(pb[zb][:], gT[gp][:, c, tt * 128:(tt + 1) * 128], w2b[bi][:, c, hb * 512:(hb + 1) * 512],
                                                                                   start=(c == 0), stop=(c == NC_ - 1)),
                         reads=[("gT", gp, c), ("w2b", bi)], writes=[("pb", zb)])
                s.op("dve", lambda e, t=t, hb=hb, zb=zb: e.tensor_tensor(acc[:, t, hb * 512:(hb + 1) * 512], acc[:, t, hb * 512:(hb + 1) * 512], pb[zb][:], op=ALU.add),
                     reads=[("pb", zb), ("acc", t)], writes=[("acc", t)])

    load_w(0)
    load_w(1)
    stage_A(0)
    for i in range(len(steps)):
        if i + 1 < len(steps):
            stage_A(i + 1)
        stage_Z(i)
        fb, gi = steps[i]
        if gi == len(groups) - 1 and fb + 2 < NFB:
            load_w(fb + 2)

    for t in range(NT):
        par = t % 2
        layer_norm(s, "c", acc[:, t, :], ("acc", t), lnp[:, 2, :], lnp[:, 3, :], ot[par][:], ("ot", par), sm, par)
        if t == 0:
            s.op("pool", lambda e, par=par: e.memset(ot[par][0:112, :], 0.0), reads=[("ot", par)], writes=[("ot", par)])
        s.dma("sp", h_out[t * 128:(t + 1) * 128, :], ot[par][:], reads=[("ot", par)], is_out=True)

        def dst_fn(hb, bank, bk, t=t, par=par):
            if hb == 0:
                f = lambda e: e.copy(rt[par][:, 0:512].rearrange("p (j t) -> p j t", j=4), bank[:].rearrange("p (j t) -> p j t", j=4))
                s.op("act", f, reads=[bk], writes=[("rt", par, 0)])
            else:
                f = lambda e: e.tensor_copy(rt[par][:, 512:1024].rearrange("p (j t) -> p j t", j=4), bank[:].rearrange("p (j t) -> p j t", j=4))
                s.op("dve", f, reads=[bk], writes=[("rt", par, 1)])
        transposes(ot[par], ("ot", par), t, par, dst_fn, None, None)
        s.dma("sp", hT_v[:, :, t * 128:(t + 1) * 128], rt[par][:].rearrange("p (k t) -> p k t", k=8), reads=[("rt", par, 0), ("rt", par, 1)], is_out=True)


def hT_chunks(D, BPR):
    B0, G1 = D["B0"], D["G1"]
    CW = min(512, BPR * 128)
    out = [((lambda k: B0[k * 128:(k + 1) * 128, :]), 128, 0)]
    for r in range(8):
        for m in range(BPR * 128 // CW):
            out.append(((lambda k, r=r, m=m: G1[r * 1024 + k * 128:r * 1024 + (k + 1) * 128, m * CW:(m + 1) * CW]), CW, 1 + BPR * r + m * (CW // 128)))
    return out


def phase_E(s, pb, D, BPR, W):
    NXB = 8 * BPR
    nb = NXB + 1
    NQL = NXB // 2
    J = BPR // 2
    lp = nb * 128
    s.begin_phase()
    KT = [s.sb("KT%d" % c, [64, lp], BF16) for c in range(2)]
    QT = [s.sb("QT%d" % c, [64, (NQL + 1) * 128], BF16) for c in range(2)]
    Vp = s.sb("Vp", [128, nb, 136], BF16)
    wqs = s.sb("wqs", [128, 8, 128], BF16)
    wks = s.sb("wks", [128, 8, 128], BF16)
    wvs = s.sb("wvs", [128, 8, 128], BF16)
    hb_ = [s.sb("hblk%d" % i, [128, 8, 512], BF16) for i in range(2)]
    near0s = s.sb("near0s", [128, 3, 128], F32)
    nears = s.sb("nears", [128, 3, 128], F32)
    nearSs = s.sb("nearSs", [128, 128], F32)
    kb = s.sb("kb", [128, 2], F32)
    lam = s.sb("lam", [128, 4, 64], F32)
    lamw = s.sb("lamw", [128, 2, 64], F32)
    lams = s.sb("lams", [128, 4], F32)
    subw = s.sb("subw", [128, 128], F32)
    cs = s.sb("cs", [128, 4], F32)
    ident = s.sb("ident", [128, 128], F32)
    PT = [[s.sb("PT%d_%d" % (i, c), [128, 512], BF16) for c in range(2)] for i in range(2)]
    tmpn = [s.sb("tmpn%d" % i, [128, 128], F32) for i in range(4)]
    ep = {n: [s.sb("%s%d" % (n, i), sh, F32) for i in range(2)] for n, sh in
          [("rl", [128, 2]), ("nl", [128, 1]), ("o0", [128, 128]), ("aa", [128, 128]), ("sq", [128, 128]), ("ss", [128, 1]), ("rs", [128, 1]),
           ("on", [128, 128]), ("e0", [128, 129]), ("e1", [128, 129])]}
    ostg = [s.sb("ostg%d" % i, [128, 512], F32) for i in range(2)]

    for (dst, src, key) in [(near0s, W["near0"], "near0"), (nears, W["near"], "near"), (nearSs, W["nearS"], "nearS"), (kb, W["kbias"], "kb"),
                            (lam, W["lamrep"], "lam"), (subw, W["sublnw"], "subw"), (cs, W["cst"], "cs"), (ident, D["ident"], "ident")]:
        s.dma("sp", dst[:], src, writes=[key])
    for (dst, src, key) in [(wqs, W["wq"], "wq"), (wks, W["wk"], "wk"), (wvs, W["wv"], "wv")]:
        s.dma("pool", dst[:], src.rearrange("(k p) f -> p k f", p=128), writes=[key])
    s.op("dve", lambda e: e.memset(Vp[:, :, 128:129], 1.0), writes=["Vones"])
    s.op("dve", lambda e: e.tensor_tensor(lamw[:, 0, :], lam[:, 0, :], lam[:, 1, :], op=ALU.mult), reads=["lam"], writes=["lamw0"])
    s.op("dve", lambda e: e.tensor_tensor(lamw[:, 1, :], lam[:, 2, :], lam[:, 3, :], op=ALU.mult), reads=["lam"], writes=["lamw1"])
    s.op("dve", lambda e: e.reduce_sum(lams[:, 0:2], lamw[:], axis=AX.X), reads=["lamw0", "lamw1"], writes=["lams"])
    s.op("act", lambda e: e.activation(lams[:, 0:2], lams[:, 0:2], AF.Exp), reads=["lams"], writes=["lams"])
    s.op("dve", lambda e: e.tensor_tensor(lams[:, 2:3], lams[:, 0:1], lams[:, 1:2], op=ALU.subtract), reads=["lams"], writes=["lams2"])
    s.op("dve", lambda e: e.tensor_tensor(lams[:, 3:4], lams[:, 2:3], cs[:, 0:1], op=ALU.add), reads=["lams2", "cs"], writes=["lamv"])
    s.op("dve", lambda e: e.tensor_scalar_mul(lams[:, 3:4], lams[:, 3:4], -1.0), reads=["lamv"], writes=["lamv"])
    s.op("dve", lambda e: e.tensor_scalar_mul(subw[:], subw[:], cs[:, 1:2]), reads=["subw", "cs"], writes=["subw"])

    rot = [0]

    def bank():
        b = rot[0] % 4
        rot[0] += 1
        return b

    ld = [0]

    def projT(i, n, w, wkey, m0, dst, dkey):
        b = bank()
        for k in range(8):
            s.op("pe", lambda e, k=k, b=b: e.matmul(pb[b][0:64, 0:n], w[:, k, m0:m0 + 64], hb_[i][:, k, 0:n], start=(k == 0), stop=(k == 7)),
                 reads=[wkey] + [("hblk", i, kk) for kk in range(8)], writes=[("pb", b)])
        s.op("act", lambda e, b=b: e.copy(dst, pb[b][0:64, 0:n]), writes=[("pb", b), dkey])

    for (rowfn, n, blk0) in hT_chunks(D, BPR):
        i = ld[0] % 2
        ld[0] += 1
        for k in range(8):
            s.dma("pool", hb_[i][:, k, 0:n], rowfn(k), writes=[("hblk", i, k)])
        c0 = blk0 * 128
        for c in range(2):
            projT(i, n, wks, "wk", c * 64, KT[c][:, c0:c0 + n], ("KT", c, blk0))
        for tt in range(n // 128):
            b = bank()
            blk = blk0 + tt
            for k in range(8):
                s.op("pe", lambda e, k=k, b=b, tt=tt, i=i: e.matmul(pb[b][:, 0:128], hb_[i][:, k, tt * 128:(tt + 1) * 128], wvs[:, k, :], start=(k == 0), stop=(k == 7)),
                     reads=["wv"] + [("hblk", i, kk) for kk in range(8)], writes=[("pb", b)])
            s.op("dve", lambda e, b=b, blk=blk: e.tensor_copy(Vp[:, blk, 0:128], pb[b][:, 0:128]), writes=[("pb", b), ("V", blk)])
    CWB = min(512, BPR * 128) // 128
    ktb = lambda c, j: ("KT", c, 0 if j == 0 else 1 + ((j - 1) // CWB) * CWB)

    ngrp = (NQL + 3) // 4
    grp_list = [(g * 4, min(4, NQL - g * 4), False) for g in range(ngrp)] + [(NQL, 1, True)]
    oh2 = s.sb("oh2", [128, 4], F32)
    s.dma("sp", oh2[:], D["oh2"], writes=["oh2"])
    qpair = s.sb("qpair", [128, 8, 4, 256], BF16)
    for (i0, nbk, special) in grp_list:
        for il in range(nbk):
            i = i0 + il
            for k in range(8):
                if special:
                    s.dma("pool", qpair[:, k, il, 0:128], D["B0"][k * 128:(k + 1) * 128, :], writes=[("qp", k, il)])
                else:
                    r, pr = i // J, i % J
                    s.dma("pool", qpair[:, k, il, :], D["G1"][r * 1024 + k * 128:r * 1024 + (k + 1) * 128, pr * 256:(pr + 1) * 256], writes=[("qp", k, il)])
        i_ = ld[0] % 2
        ld[0] += 1
        n = nbk * 128
        qv = hb_[i_][:].rearrange("p k (b c) -> p k b c", c=128)
        for k in range(8):
            rk_ = [("qp", k, il) for il in range(nbk)] + ["oh2"]
            if special:
                s.op("dve", lambda e, k=k, nbk=nbk, qv=qv: e.tensor_scalar_mul(qv[:, k, 0:nbk, :], qpair[:, k, 0:nbk, 0:128], oh2[:, 1:2]), reads=rk_, writes=[("hblk", i_, k)])
            else:
                s.op("dve", lambda e, k=k, nbk=nbk, qv=qv: e.tensor_scalar_mul(qv[:, k, 0:nbk, :], qpair[:, k, 0:nbk, 0:128], oh2[:, 0:1]), reads=rk_, writes=[("hblk", i_, k)])
                s.op("dve", lambda e, k=k, nbk=nbk, qv=qv: e.scalar_tensor_tensor(qv[:, k, 0:nbk, :], qpair[:, k, 0:nbk, 128:256], oh2[:, 1:2], qv[:, k, 0:nbk, :], op0=ALU.mult, op1=ALU.add),
                     reads=rk_ + [("hblk", i_, k)], writes=[("hblk", i_, k)])
        for c in range(2):
            projT(i_, n, wqs, "wq", c * 64, QT[c][:, i0 * 128:i0 * 128 + n], ("QT", c, i0))
    qtb = lambda c, i0: [("QT", c, i0)]

    steps = []
    for (i0, nbk, special) in grp_list:
        if special:
            steps.append((i0, nbk, 0, [0], True))
            continue
        jmax = 2 * (i0 + nbk - 1) + 2
        for j in range(0, jmax + 1):
            act = [il for il in range(nbk) if 2 * (i0 + il) + 2 >= j]
            steps.append((i0, nbk, j, act, False))

    def acc_ap(il, c):
        a = il * 2 + c
        return pb[4 + a // 3][:, (a % 3) * 160:(a % 3) * 160 + 129], ("pb", 4 + a // 3)

    def emit_qk(si):
        i0, nbk, j, act, special = steps[si]
        sp = si % 2
        lo, hi = act[0], act[-1] + 1
        for c in range(2):
            s.op("pe", lambda e, c=c, sp=sp, lo=lo, hi=hi, j=j, i0=i0: e.matmul(pb[sp * 2 + c][:, lo * 128:hi * 128], KT[c][:, j * 128:(j + 1) * 128],
                                                                      QT[c][:, (i0 + lo) * 128:(i0 + hi) * 128], start=True, stop=True),
                 reads=[ktb(c, j)] + qtb(c, i0), writes=[("pb", sp * 2 + c)])

    tn = [0]

    def emit_sm(si):
        i0, nbk, j, act, special = steps[si]
        sp = si % 2
        far = [] if special else [il for il in act if j <= 2 * (i0 + il) - 1]
        nearl = [il for il in act if il not in far]
        for c in range(2):
            for il in nearl:
                i = i0 + il
                if special:
                    btile, bkey = nearSs[:], "nearS"
                elif i == 0:
                    btile, bkey = near0s[:, j, :], "near0"
                else:
                    btile, bkey = nears[:, j - 2 * i, :], "near"
                ti = tn[0] % 4
                tn[0] += 1
                s.op("dve", lambda e, c=c, sp=sp, il=il, ti=ti, btile=btile: e.scalar_tensor_tensor(tmpn[ti][:], pb[sp * 2 + c][:, il * 128:(il + 1) * 128], 0.125, btile,
                                                                                              op0=ALU.mult, op1=ALU.add),
                     reads=[bkey], writes=[("pb", sp * 2 + c), ("tmpn", ti)])
                s.op("act", lambda e, c=c, sp=sp, il=il, ti=ti: e.activation(PT[sp][c][:, il * 128:(il + 1) * 128], tmpn[ti][:], AF.Exp),
                     reads=[("tmpn", ti)], writes=[("PT", sp, c, il)])
            if far:
                lo, hi = far[0], far[-1] + 1
                kcol = 1 if j == 0 else 0
                s.op("act", lambda e, c=c, sp=sp, lo=lo, hi=hi, kcol=kcol: e.activation(PT[sp][c][:, lo * 128:hi * 128], pb[sp * 2 + c][:, lo * 128:hi * 128], AF.Exp,
                                                                                  bias=kb[:, kcol:kcol + 1], scale=0.125),
                     reads=["kb"], writes=[("pb", sp * 2 + c)] + [("PT", sp, c, x) for x in far])

    def emit_pv(si):
        i0, nbk, j, act, special = steps[si]
        sp = si % 2
        for il in act:
            i = i0 + il
            last = True if special else (j == 2 * i + 2)
            for c in range(2):
                ap, akey = acc_ap(il, c)
                if j == 0:
                    s.op("dve", lambda e, ap=ap: e.memset(ap, 0.0), writes=[akey])
                s.op("pe", lambda e, ap=ap, c=c, sp=sp, il=il, j=j, last=last: e.matmul(ap, PT[sp][c][:, il * 128:(il + 1) * 128], Vp[:, j, 0:129], start=False, stop=last,
                                                                                   skip_group_check=True),
                     reads=[("PT", sp, c, il), ("V", j), "Vones"], writes=[akey])
            if last:
                emit_epi(i0, il, nbk)

    gcount = [0]

    def emit_epi(i0, il, nbk):
        i = i0 + il
        par = i % 2
        a0, k0 = acc_ap(il, 0)
        a1, k1 = acc_ap(il, 1)
        rl, nl, o0, aa, sq, ss, rs, on, e0, e1 = (ep[n][par] for n in ("rl", "nl", "o0", "aa", "sq", "ss", "rs", "on", "e0", "e1"))
        K = lambda n: (n, par)
        s.op("dve", lambda e: e.tensor_copy(e0[:], a0), writes=[k0, K("e0")])
        s.op("dve", lambda e: e.tensor_copy(e1[:], a1), writes=[k1, K("e1")])
        s.op("dve", lambda e: e.reciprocal(rl[:, 0:1], e0[:, 128:129]), reads=[K("e0")], writes=[K("rl0")])
        s.op("dve", lambda e: e.reciprocal(rl[:, 1:2], e1[:, 128:129]), reads=[K("e1")], writes=[K("rl1")])
        s.op("dve", lambda e: e.tensor_tensor(nl[:], rl[:, 1:2], lams[:, 3:4], op=ALU.mult), reads=[K("rl1"), "lamv"], writes=[K("nl")])
        s.op("dve", lambda e: e.tensor_scalar_mul(o0[:], e0[:, 0:128], rl[:, 0:1]), reads=[K("e0"), K("rl0")], writes=[K("o0")])
        s.op("dve", lambda e: e.scalar_tensor_tensor(aa[:], e1[:, 0:128], nl[:, 0:1], o0[:], op0=ALU.mult, op1=ALU.add), reads=[K("e1"), K("nl"), K("o0")], writes=[K("aa")])
        s.op("pool", lambda e: e.tensor_tensor(sq[:], aa[:], aa[:], op=ALU.mult), reads=[K("aa")], writes=[K("sq")])
        s.op("dve", lambda e: e.reduce_sum(ss[:], sq[:], axis=AX.X), reads=[K("sq")], writes=[K("ss")])
        s.op("act", lambda e: e.activation(rs[:], ss[:], AF.Ln, bias=cs[:, 2:3], scale=1.0 / 128), reads=[K("ss"), "cs"], writes=[K("rs")])
        s.op("act", lambda e: e.activation(rs[:], rs[:], AF.Exp, scale=-0.5), reads=[K("rs")], writes=[K("rs")])
        s.op("dve", lambda e: e.scalar_tensor_tensor(on[:], aa[:], rs[:, 0:1], subw[:], op0=ALU.mult, op1=ALU.mult), reads=[K("aa"), K("rs"), "subw"], writes=[K("on")])
        s.op("pe", lambda e: e.transpose(pb[7][:, 0:128], on[:], ident[:]), reads=[K("on"), "ident"], writes=[("pb", 7)])
        og = (i0 // 4) % 2
        s.op("act", lambda e: e.copy(ostg[og][:, il * 128:(il + 1) * 128], pb[7][:, 0:128]), writes=[("pb", 7), ("ostg", og, il)])
        col = 8 * J * 128 if i >= NQL else ((i % J) * 8 + i // J) * 128
        s.dma("sp", D["MOe"][:, col:col + 128], ostg[og][:, il * 128:(il + 1) * 128], reads=[("ostg", og, il)], writes=[("MOe", i)])

    emit_qk(0)
    for si in range(len(steps)):
        emit_sm(si)
        if si + 1 < len(steps):
            emit_qk(si + 1)
        emit_pv(si)
    s.coll("AllGather", D["MOe"], D["GA"], reads=[("MOe", i) for i in range(NQL + 1)], writes=["GA"])
    s.end_phase()

import math

NB = 129
RMS_EPS = 1e-6


def phase_O(s, pb, D, BPR, W):
    wq, wk, wv, wz, wba, convw, avec, normw = (W[n] for n in ("wq", "wk", "wv", "wz", "wba", "convw", "avec", "normw"))
    ident_d, U_d, Ms_d, Mc_d = D["ident"], D["U"], D["Ms"], D["Mc"]
    oT_out = D["MOo"]
    s.begin_phase()
    sb = s.sb
    wqs, wks, wvs, wzs = (sb(n, [128, 8, 128], BF16) for n in ("wqs", "wks", "wvs", "wzs"))
    wbas = sb("wbas", [128, 8, 2], BF16)
    cw = sb("cw", [128, 3, 4], F32)
    av = sb("av", [128, 2], F32)
    nw = sb("nw", [128, 128], F32)
    ident = sb("ident", [128, 128], F32)
    U = sb("U", [128, 128], F32)
    Ms = sb("Ms", [128, 128], F32)
    Mc = sb("Mc", [128, 128], F32)
    ones = sb("ones", [128, 128], F32)
    cst = sb("cst", [128, 4], F32)
    negA = sb("negA", [128, 1], F32)
    S = sb("S", [128, 128], F32)
    hb_ = [sb("hblk%d" % i, [128, 8, 512], BF16) for i in range(2)]
    X = [[sb("X%d_%d" % (p, i), [128, 515], F32) for i in range(3)] for p in range(2)]
    Y = [sb("Y%d" % i, [128, 512], F32) for i in range(3)]
    ST = [[sb("ST%d_%d" % (p, i), [128, 512], F32) for i in range(3)] for p in range(2)]
    QTb = [sb("QTb%d" % p, [128, 512], BF16) for p in range(2)]
    Qsq = [sb("Qsq%d" % p, [128, 512], F32) for p in range(2)]
    zs = [sb("zs%d" % p, [128, 4, 128], F32) for p in range(2)]
    ostg = [sb("ostg%d" % p, [128, 512], F32) for p in range(2)]

    def two(name, shape, dt=F32):
        return [sb("%s%d" % (name, p), shape, dt) for p in range(3)]
    bas, ebt, beta, gcol, gam, rq, ssk, rk, small = (two(n, [128, w]) for n, w in
                                                     [("bas", 2), ("ebt", 2), ("beta", 1), ("gcol", 1), ("gam", 2), ("rq", 1), ("ssk", 1), ("rk", 1), ("small", 8)])
    Kraw, Ksq, Kn, Vt, dg, dE, ET, ReG, Bm, t1, B32, P32, t2, qkT, Vb, Kbg, Kd, Qt, usb, wT, vn, osb, osq, og = (
        two(n, [128, 256 if n == "dg" else 128]) for n in
        ("Kraw", "Ksq", "Kn", "Vt", "dg", "dE", "ET", "ReG", "Bm", "t1", "B32", "P32", "t2", "qkT", "Vb", "Kbg", "Kd", "Qt", "usb", "wT", "vn", "osb", "osq", "og"))
    KTb = two("KTb", [128, 128], F32)
    Pb = P32
    Ab = [two("Ab%d" % i, [128, 128], F32) for i in range(2)]
    Bb = [two("Bb%d" % i, [128, 128], F32) for i in range(2)]
    sso, ro = two("sso", [128, 1]), two("ro", [128, 1])
    PK = lambda b: ("pb", b)
    for (dst, src, key) in [(cw, convw, "cw"), (av, avec, "av"), (nw, normw, "nw"), (ident, ident_d, "ident"), (U, U_d, "U"), (Ms, Ms_d, "Ms"), (Mc, Mc_d, "Mc")]:
        s.dma("sp", dst[:], src, writes=[key])
    for (dst, src, key) in [(wqs, wq, "wq"), (wks, wk, "wk"), (wvs, wv, "wv"), (wzs, wz, "wz"), (wbas, wba, "wba")]:
        s.dma("pool", dst[:], src.rearrange("(k p) f -> p k f", p=128), writes=[key])
    s.op("dve", lambda e: e.memset(ones[:], 1.0), writes=["ones"])
    s.op("dve", lambda e: e.memset(S[:], 0.0), writes=["S"])
    s.op("dve", lambda e: e.memset(cst[:, 0:1], RMS_EPS), writes=["cst0"])
    s.op("dve", lambda e: e.memset(cst[:, 1:2], 1.0), writes=["cst1"])
    s.op("dve", lambda e: e.memset(cst[:, 2:3], math.log(128.0 ** -0.5)), writes=["cst2"])
    CK = ["cst0", "cst1", "cst2"]
    s.op("act", lambda e: e.activation(negA[:], av[:, 0:1], AF.Exp), reads=["av"], writes=["negA"])
    s.op("dve", lambda e: e.tensor_scalar_mul(negA[:], negA[:], -1.0), reads=["negA"], writes=["negA"])
    for i in range(3):
        s.op("pool", lambda e, i=i: e.memset(X[1][i][:, 0:515], 0.0), writes=[("X", 1, i)])

    def mm(out, lhsT, rhs, reads, bank, start=True, stop=True):
        s.op("pe", lambda e: e.matmul(out, lhsT, rhs, start=start, stop=stop), reads=reads, writes=[PK(bank)])

    def tr(out, in_, reads, bank):
        s.op("pe", lambda e: e.transpose(out, in_, ident[:]), reads=reads + ["ident"], writes=[PK(bank)])

    chunks = hT_chunks(D, BPR)
    nblk = len(chunks)
    wlist = [(wqs, "wq"), (wks, "wk"), (wvs, "wv")]
    nprev = [0]
    cbase = [0]
    def do_block(tb):
        rowfn, n, blk0 = chunks[tb]
        c0 = blk0 * 128
        p = tb % 2
        npv = nprev[0]
        nprev[0] = n
        cb = cbase[0]
        cbase[0] += n // 128
        for k in range(8):
            s.dma("pool", hb_[p][:, k, 0:n], rowfn(k), writes=[("hblk", p, k)])
        for i, (w, wkey) in enumerate(wlist):
            b = i % 2
            for k in range(8):
                mm(pb[b][:, 0:n], w[:, k, :], hb_[p][:, k, 0:n], [wkey, ("hblk", p)] + [("hblk", p, kk) for kk in range(8)], b, start=(k == 0), stop=(k == 7))
            s.op("act", lambda e, b=b, i=i: e.copy(X[p][i][:, 3:3 + n], pb[b][:, 0:n]), writes=[PK(b), ("X", p, i)])
            s.op("dve", lambda e, i=i: e.tensor_copy(X[p][i][:, 0:3], X[1 - p][i][:, npv:npv + 3]), reads=[("X", 1 - p, i)], writes=[("Xc", p, i)])
            eng = "dve"
            xr = [("X", p, i), ("Xc", p, i), "cw"]
            s.op(eng, lambda e, i=i: e.tensor_scalar_mul(Y[i][:, 0:n], X[p][i][:, 0:n], cw[:, i, 0:1]), reads=xr, writes=[("Y", i)])
            for jj in range(1, 4):
                s.op(eng, lambda e, i=i, jj=jj: e.scalar_tensor_tensor(Y[i][:, 0:n], X[p][i][:, jj:jj + n], cw[:, i, jj:jj + 1], Y[i][:, 0:n], op0=ALU.mult, op1=ALU.add),
                     reads=xr + [("Y", i)], writes=[("Y", i)])
            s.op("act", lambda e, i=i: e.activation(ST[p][i][:, 0:n], Y[i][:, 0:n], AF.Silu), reads=[("Y", i)], writes=[("ST", p, i)])
        s.op("pool", lambda e: e.tensor_tensor(Qsq[p][:, 0:n], ST[p][0][:, 0:n], ST[p][0][:, 0:n], op=ALU.mult), reads=[("ST", p, 0)], writes=[("Qsq", p)])
        for tt in range(n // 128):
            cols = slice(tt * 128, (tt + 1) * 128)
            for k in range(8):
                mm(pb[2][:, 0:128], hb_[p][:, k, cols], wzs[:, k, :], ["wz", ("hblk", p)] + [("hblk", p, kk) for kk in range(8)], 2, start=(k == 0), stop=(k == 7))
            s.op("act", lambda e, tt=tt: e.activation(zs[p][:, tt, :], pb[2][:, 0:128], AF.Silu), writes=[PK(2), ("zs", p, tt)])

        def do_chunk(tt):
            cols = slice(tt * 128, (tt + 1) * 128)
            ci = cb + tt
            q = ci % 3
            K_ = lambda name: (name, q)
            for k in range(8):
                mm(pb[2][:, 128:130], hb_[p][:, k, cols], wbas[:, k, :], ["wba", ("hblk", p)] + [("hblk", p, kk) for kk in range(8)], 2, start=(k == 0), stop=(k == 7))
                yield
            s.op("dve", lambda e, q=q: e.tensor_copy(bas[q][:], pb[2][:, 128:130]), writes=[PK(2), K_("bas")])
            yield
            s.op("act", lambda e, q=q: e.activation(ebt[q][:, 0:1], bas[q][:, 0:1], AF.Exp, scale=-1.0), reads=[K_("bas")], writes=[K_("eb")])
            yield
            s.op("act", lambda e, q=q: e.activation(ebt[q][:, 1:2], bas[q][:, 1:2], AF.Exp, bias=av[:, 1:2]), reads=[K_("bas"), "av"], writes=[K_("ea")])
            yield
            s.op("act", lambda e, q=q: e.activation(ebt[q][:, 1:2], ebt[q][:, 1:2], AF.Ln, bias=cst[:, 1:2]), reads=[K_("ea")] + CK, writes=[K_("ea")])
            yield
            s.op("dve", lambda e, q=q: e.tensor_scalar_add(beta[q][:], ebt[q][:, 0:1], 1.0), reads=[K_("eb")], writes=[K_("beta")])
            yield
            s.op("dve", lambda e, q=q: e.reciprocal(beta[q][:], beta[q][:]), reads=[K_("beta")], writes=[K_("beta")])
            yield
            s.op("dve", lambda e, q=q: e.tensor_tensor(gcol[q][:], ebt[q][:, 1:2], negA[:], op=ALU.mult), reads=[K_("ea"), "negA"], writes=[K_("g")])
            yield
            mm(pb[2][:, 136:137], U[:], gcol[q][:], ["U", K_("g")], 2)
            yield
            mm(pb[2][:, 137:138], ones[:], gcol[q][:], ["ones", K_("g")], 2)
            yield
            s.op("dve", lambda e, q=q: e.tensor_copy(gam[q][:], pb[2][:, 136:138]), writes=[PK(2), K_("gam")])
            yield
            mm(pb[2][:, 132:133], Qsq[p][:, cols], ones[:, 0:1], [("Qsq", p), "ones"], 2)
            yield
            s.op("act", lambda e, q=q: e.activation(rq[q][:], pb[2][:, 132:133], AF.Ln, bias=cst[:, 0:1]), reads=CK, writes=[PK(2), K_("rq")])
            yield
            s.op("act", lambda e, q=q: e.activation(rq[q][:], rq[q][:], AF.Exp, scale=-0.5, bias=cst[:, 2:3]), reads=[K_("rq")] + CK, writes=[K_("rq")])
            yield
            tr(pb[3][:, 0:128], ST[p][1][:, cols], [("ST", p, 1)], 3)
            yield
            s.op("act", lambda e, q=q: e.copy(Kraw[q][:], pb[3][:, 0:128]), writes=[PK(3), K_("Kraw")])
            yield
            s.op("pool", lambda e, q=q: e.tensor_tensor(Ksq[q][:], Kraw[q][:], Kraw[q][:], op=ALU.mult), reads=[K_("Kraw")], writes=[K_("Ksq")])
            yield
            s.op("dve", lambda e, q=q: e.reduce_sum(ssk[q][:], Ksq[q][:], axis=AX.X), reads=[K_("Ksq")], writes=[K_("ssk")])
            yield
            s.op("act", lambda e, q=q: e.activation(rk[q][:], ssk[q][:], AF.Ln, bias=cst[:, 0:1]), reads=[K_("ssk")] + CK, writes=[K_("rk")])
            yield
            s.op("act", lambda e, q=q: e.activation(rk[q][:], rk[q][:], AF.Exp, scale=-0.5), reads=[K_("rk")], writes=[K_("rk")])
            yield
            s.op("dve", lambda e, q=q: e.tensor_scalar_mul(Kn[q][:], Kraw[q][:], rk[q][:, 0:1]), reads=[K_("Kraw"), K_("rk")], writes=[K_("Kn")])
            yield
            tr(pb[3][:, 256:384], Kn[q][:], [K_("Kn")], 3)
            yield
            s.op("act", lambda e, q=q: e.copy(KTb[q][:], pb[3][:, 256:384]), writes=[PK(3), K_("KTb")])
            yield
            tr(pb[3][:, 128:256], ST[p][2][:, cols], [("ST", p, 2)], 3)
            yield
            s.op("dve", lambda e, q=q: e.tensor_copy(Vt[q][:], pb[3][:, 128:256]), writes=[PK(3), K_("Vt")])
            yield
            s.op("pool", lambda e, q=q: e.tensor_scalar_mul(dg[q][:, 0:128], ident[:], gam[q][:, 0:1]), reads=["ident", K_("gam")], writes=[K_("dg0")])
            yield
            s.op("pool", lambda e, q=q: e.tensor_scalar_mul(dg[q][:, 128:256], ident[:], beta[q][:, 0:1]), reads=["ident", K_("beta")], writes=[K_("dg1")])
            yield
            mm(pb[4][:, 0:256], ones[:], dg[q][:], ["ones", K_("dg0"), K_("dg1")], 4)
            yield
            s.op("dve", lambda e, q=q: e.tensor_scalar(dE[q][:], pb[4][:, 0:128], gam[q][:, 0:1], 0.0, op0=ALU.subtract, op1=ALU.min), reads=[K_("gam")], writes=[PK(4), K_("dE")])
            yield
            s.op("act", lambda e, q=q: e.activation(ReG[q][:], pb[4][:, 0:128], AF.Exp), writes=[PK(4), K_("ReG")])
            yield
            s.op("dve", lambda e, q=q: e.tensor_tensor(Bm[q][:], pb[4][:, 128:256], Ms[:], op=ALU.mult), reads=["Ms"], writes=[PK(4), K_("Bm")])
            yield
            s.op("act", lambda e, q=q: e.activation(ET[q][:], dE[q][:], AF.Exp), reads=[K_("dE")], writes=[K_("ET")])
            yield
            mm(pb[4][:, 256:384], KTb[q][:], KTb[q][:], [K_("KTb")], 4)
            yield
            s.op("dve", lambda e, q=q: e.tensor_tensor(t1[q][:], pb[4][:, 256:384], ET[q][:], op=ALU.mult), reads=[K_("ET")], writes=[PK(4), K_("t1")])
            yield
            mm(pb[4][:, 384:512], KTb[q][:], ST[p][0][:, cols], [K_("KTb"), ("ST", p, 0)], 4)
            yield
            s.op("dve", lambda e, q=q: e.tensor_tensor(t2[q][:], pb[4][:, 384:512], ET[q][:], op=ALU.mult), reads=[K_("ET")], writes=[PK(4), K_("t2")])
            yield
            s.op("pool", lambda e, q=q: e.tensor_tensor(B32[q][:], t1[q][:], Bm[q][:], op=ALU.mult), reads=[K_("t1"), K_("Bm")], writes=[K_("B32")])
            yield
            s.op("pool", lambda e, q=q: e.tensor_tensor(qkT[q][:], t2[q][:], Mc[:], op=ALU.mult), reads=[K_("t2"), "Mc"], writes=[K_("qkT")])
            yield
            s.op("act", lambda e, q=q: e.copy(Bb[0][q][:], B32[q][:]), reads=[K_("B32")], writes=[K_("Bb0")])
            yield
            s.op("dve", lambda e, q=q: e.tensor_tensor(P32[q][:], ident[:], B32[q][:], op=ALU.subtract), reads=["ident", K_("B32")], writes=[K_("P32")])
            yield
            tr(pb[3][:, 384:512], B32[q][:], [K_("B32")], 3)
            yield
            s.op("act", lambda e, q=q: e.copy(Ab[0][q][:], pb[3][:, 384:512]), writes=[PK(3), K_("Ab0")])
            yield
            yield "SPLIT"
            for lv in range(1, 7):
                a_old, a_new = (lv - 1) % 2, lv % 2
                mm(pb[5][:, 0:128], Bb[a_old][q][:], Ab[a_old][q][:], [K_("Bb%d" % a_old), K_("Ab%d" % a_old)], 5)
                yield
                if lv < 6:
                    mm(pb[6][:, 384:512], Ab[a_old][q][:], Bb[a_old][q][:], [K_("Bb%d" % a_old), K_("Ab%d" % a_old)], 6)
                s.op("act", lambda e, q=q, a_new=a_new: e.copy(Ab[a_new][q][:], pb[5][:, 0:128]), writes=[PK(5), K_("Ab%d" % a_new)])
                yield
                if lv < 6:
                    s.op("dve", lambda e, q=q, a_new=a_new: e.tensor_copy(Bb[a_new][q][:], pb[6][:, 384:512]), writes=[PK(6), K_("Bb%d" % a_new)])
                mm(pb[5][:, 256:384], Ab[a_new][q][:], Pb[q][:], [K_("Ab%d" % a_new), K_("P32")], 5)
                yield
                s.op("dve", lambda e, q=q: e.tensor_tensor(P32[q][:], P32[q][:], pb[5][:, 256:384], op=ALU.add), reads=[K_("P32")], writes=[PK(5), K_("P32")])
                yield
            s.op("act", lambda e, q=q: e.activation(small[q][:, 0:1], gam[q][:, 0:1], AF.Exp), reads=[K_("gam")], writes=[K_("eg")])
            yield
            s.op("act", lambda e, q=q: e.activation(small[q][:, 2:3], gam[q][:, 0:1], AF.Exp, scale=-1.0, bias=gam[q][:, 1:2]), reads=[K_("gam")], writes=[K_("kd")])
            yield
            s.op("act", lambda e, q=q: e.activation(small[q][:, 3:4], gam[q][:, 1:2], AF.Exp), reads=[K_("gam")], writes=[K_("dec")])
            yield
            s.op("dve", lambda e, q=q: e.tensor_tensor(small[q][:, 1:2], small[q][:, 0:1], beta[q][:], op=ALU.mult), reads=[K_("eg"), K_("beta")], writes=[K_("bg")])
            yield
            s.op("pool", lambda e, q=q: e.tensor_scalar_mul(Vb[q][:], Vt[q][:], beta[q][:, 0:1]), reads=[K_("Vt"), K_("beta")], writes=[K_("Vb")])
            yield
            s.op("pool", lambda e, q=q: e.tensor_scalar_mul(Kbg[q][:], Kn[q][:], small[q][:, 1:2]), reads=[K_("Kn"), K_("bg")], writes=[K_("Kbg")])
            yield
            s.op("pool", lambda e, q=q: e.tensor_scalar_mul(Kd[q][:], Kn[q][:], small[q][:, 2:3]), reads=[K_("Kn"), K_("kd")], writes=[K_("Kd")])
            yield
            s.op("dve", lambda e, q=q: e.tensor_tensor(Qt[q][:], ST[p][0][:, cols], ReG[q][:], op=ALU.mult), reads=[("ST", p, 0), K_("ReG")], writes=[K_("Qt")])
            yield
            mm(pb[6][:, 0:128], P32[q][:], Vb[q][:], [K_("P32"), K_("Vb")], 6)
            yield
            s.op("act", lambda e, q=q: e.copy(usb[q][:], pb[6][:, 0:128]), writes=[PK(6), K_("usb")])
            yield
            mm(pb[6][:, 128:256], Kbg[q][:], P32[q][:], [K_("P32"), K_("Kbg")], 6)
            yield
            s.op("act", lambda e, q=q: e.copy(wT[q][:], pb[6][:, 128:256]), writes=[PK(6), K_("wT")])
            yield
            yield "SPLIT2"
            mm(pb[7][:, 0:128], wT[q][:], S[:], [K_("wT"), "S"], 7)
            yield
            s.op("dve", lambda e, q=q: e.tensor_tensor(vn[q][:], usb[q][:], pb[7][:, 0:128], op=ALU.subtract), reads=[K_("usb")], writes=[PK(7), K_("vn")])
            yield
            mm(pb[7][:, 128:256], Kd[q][:], vn[q][:], [K_("Kd"), K_("vn")], 7)
            yield
            mm(pb[7][:, 256:384], Qt[q][:], S[:], [K_("Qt"), "S"], 7, start=True, stop=False)
            yield
            mm(pb[7][:, 256:384], qkT[q][:], vn[q][:], [K_("qkT"), K_("vn")], 7, start=False, stop=True)
            yield
            s.op("dve", lambda e, q=q: e.scalar_tensor_tensor(S[:], S[:], small[q][:, 3:4], pb[7][:, 128:256], op0=ALU.mult, op1=ALU.add), reads=["S", K_("dec")], writes=[PK(7), "S"])
            yield
            s.op("dve", lambda e, q=q: e.tensor_scalar_mul(osb[q][:], pb[7][:, 256:384], rq[q][:, 0:1]), reads=[K_("rq")], writes=[PK(7), K_("osb")])
            yield
            s.op("pool", lambda e, q=q: e.tensor_tensor(osq[q][:], osb[q][:], osb[q][:], op=ALU.mult), reads=[K_("osb")], writes=[K_("osq")])
            yield
            s.op("dve", lambda e, q=q: e.reduce_sum(sso[q][:], osq[q][:], axis=AX.X), reads=[K_("osq")], writes=[K_("sso")])
            yield
            s.op("act", lambda e, q=q: e.activation(ro[q][:], sso[q][:], AF.Ln, bias=cst[:, 0:1], scale=1.0 / 128), reads=[K_("sso")] + CK, writes=[K_("ro")])
            yield
            s.op("act", lambda e, q=q: e.activation(ro[q][:], ro[q][:], AF.Exp, scale=-0.5), reads=[K_("ro")], writes=[K_("ro")])
            yield
            s.op("dve", lambda e, q=q: e.scalar_tensor_tensor(og[q][:], osb[q][:], ro[q][:, 0:1], nw[:], op0=ALU.mult, op1=ALU.mult), reads=[K_("osb"), K_("ro"), "nw"], writes=[K_("og")])
            yield
            s.op("pool", lambda e, q=q, tt=tt: e.tensor_tensor(og[q][:], og[q][:], zs[p][:, tt, :], op=ALU.mult), reads=[K_("og"), ("zs", p, tt)], writes=[K_("og")])
            yield
            tr(pb[6][:, 256:384], og[q][:], [K_("og")], 6)
            yield
            s.op("act", lambda e, tt=tt: e.copy(ostg[p][:, tt * 128:(tt + 1) * 128], pb[6][:, 256:384]), writes=[PK(6), ("ostg", p, tt)])
            pbk = blk0 + tt
            col = 0 if pbk == 0 else 128 + (((pbk - 1) % BPR) * 8 + (pbk - 1) // BPR) * 128
            s.dma("sp", oT_out[:, col:col + 128], ostg[p][:, tt * 128:(tt + 1) * 128], reads=[("ostg", p, tt)], writes=[("MOo", tb, tt)])
            yield
        for tt in range(n // 128):
            drive(do_chunk(tt))

    pend = [None, None]

    def drive(g):
        a, b = pend[0], pend[1]
        done_g = False
        done_b = b is None
        while True:
            if a is not None:
                try:
                    next(a)
                except StopIteration:
                    a = None
            if not done_b:
                if next(b) == "SPLIT2":
                    done_b = True
            if not done_g:
                if next(g) == "SPLIT":
                    done_g = True
            if a is None and done_b and done_g:
                break
        pend[0], pend[1] = b, g

    for tb in range(nblk):
        do_block(tb)
    a, b = pend
    if b is not None:
        while next(b) != "SPLIT2":
            pass
    for g_ in (a, b):
        if g_ is not None:
            for _ in g_:
                pass
    s.coll("AllGather", D["MOo"], D["GO"], reads=[("MOo", tb, x) for tb in range(nblk) for x in range(4)], writes=["GO"])
    s.end_phase()


POOL_WINDOWS = (2, 4, 8, 16)


def phase_P(s, pb, D, BPR, pool):
    NT = BPR + 1
    TOK = NT * 128
    HP, B0 = D["HP"], D["B0"]
    s.begin_phase()
    ypT = s.sb("ypT", [128, 4, TOK], BF16)
    wus = s.sb("wus", [128, 8, 512], BF16)
    wps = s.sb("wps", [128, 4, 128], BF16)
    psc = s.sb("psc", [128, 4], F32)
    fix = s.sb("fix", [128, 4, 128], F32)
    hbuf = s.sb("hbuf", [128, 8, 528], BF16)
    lbf = s.sb("lbf", [128, 8, 16], F32)
    lbc = s.sb("lbc", [128, 9, 8, 16], F32)
    oh9 = s.sb("oh9", [128, 9], F32)
    s.dma("sp", oh9[:], D["oh9"], writes=["oh9"])
    uc = s.sb("uc", [128, 528], F32)
    sA = s.sb("sA", [128, 528], F32)
    sB = s.sb("sB", [128, 528], F32)
    pl = s.sb("pl", [128, 512], BF16)
    s.dma("pool", wus[:], pool["wu"].rearrange("(k p) f -> p k f", p=128), writes=["wu"])
    s.dma("pool", wps[:].rearrange("c g d -> c (g d)"), pool["wp"], writes=["wp"])
    s.dma("sp", psc[:], pool["pscale"], writes=["psc"])
    s.dma("sp", fix[:], pool["fix"], writes=["fix"])
    CW = min(512, BPR * 128)
    parts = [(0, 128, True, None)] + [(128 + m * CW, CW, False, m) for m in range(BPR * 128 // CW)]
    zb = [0]
    for (tok0, n, isb0, m) in parts:
        W = n + 16
        if isb0:
            s.op("pool", lambda e: e.memset(hbuf[:, :, 0:16], 0.0), writes=["hbufL"])
            for k in range(8):
                s.dma("pool", hbuf[:, k, 16:16 + n], B0[k * 128:(k + 1) * 128, :], writes=[("hbuf", k)])
        else:
            if m > 0:
                s.op("act", lambda e: e.copy(hbuf[:, :, 0:16], hbuf[:, :, 512:528]), reads=[("hbuf", k) for k in range(8)] + ["hbufL"], writes=["hbufL"])
            for k in range(8):
                s.dma("pool", hbuf[:, k, 16:16 + n], HP[k * 128:(k + 1) * 128, m * CW:m * CW + n], writes=[("hbuf", k)])
            if m == 0:
                for r in range(9):
                    s.dma("sp", lbc[:, r, :, :].rearrange("p k f -> p (k f)"), D["LBs"][r * 128:(r + 1) * 128, :], writes=[("lbc", r)])
                s.op("dve", lambda e: e.tensor_scalar_mul(lbf[:], lbc[:, 0, :, :], oh9[:, 0:1]), reads=[("lbc", 0), "oh9"], writes=["lbf"])
                for r in range(1, 9):
                    s.op("dve", lambda e, r=r: e.scalar_tensor_tensor(lbf[:], lbc[:, r, :, :], oh9[:, r:r + 1], lbf[:], op0=ALU.mult, op1=ALU.add),
                         reads=[("lbc", r), "oh9", "lbf"], writes=["lbf"])
                s.op("act", lambda e: e.copy(hbuf[:, :, 0:16], lbf[:]), reads=["lbf"], writes=["hbufL"])
        hk = ["hbufL"] + [("hbuf", k) for k in range(8)] + [("hbufL", k) for k in range(8)]
        for g in range(4):
            win = POOL_WINDOWS[g]
            bA, bB = zb[0] % 2, 6 + zb[0] % 2
            zb[0] += 1
            for k in range(8):
                s.op("pe", lambda e, k=k, bA=bA, g=g: e.matmul(pb[bA][:, 0:16], wus[:, k, g * 128:(g + 1) * 128], hbuf[:, k, 0:16], start=(k == 0), stop=(k == 7)),
                     reads=["wu"] + hk, writes=[("pb", bA)])
            for k in range(8):
                s.op("pe", lambda e, k=k, bB=bB, g=g, n=n: e.matmul(pb[bB][:, 0:n], wus[:, k, g * 128:(g + 1) * 128], hbuf[:, k, 16:16 + n], start=(k == 0), stop=(k == 7)),
                     reads=["wu"] + hk, writes=[("pb", bB)])
            s.op("act", lambda e, bA=bA: e.copy(uc[:, 0:16], pb[bA][:, 0:16]), writes=[("pb", bA), "ucA"])
            s.op("dve", lambda e, bB=bB, n=n: e.tensor_copy(uc[:, 16:16 + n], pb[bB][:, 0:n]), writes=[("pb", bB), "ucB"])
            src, skey = uc, ["ucA", "ucB"]
            bufs = [(sA, "sA"), (sB, "sB")]
            st = 0
            while (1 << st) < win:
                sh = 1 << st
                lo = (1 << (st + 1)) - 1
                dst, dkey = bufs[st % 2]
                s.op("dve", lambda e, dst=dst, src=src, lo=lo, sh=sh, W=W: e.tensor_tensor(dst[:, lo:W], src[:, lo:W], src[:, lo - sh:W - sh], op=ALU.add),
                     reads=skey, writes=[dkey])
                src, skey = dst, [dkey]
                st += 1
            if isb0:
                s.op("dve", lambda e, src=src, g=g: e.tensor_tensor(src[:, 16:144], src[:, 16:144], fix[:, g, :], op=ALU.mult), reads=skey + ["fix"], writes=skey)
            s.op("dve", lambda e, src=src, W=W, n=n, win=win: e.scalar_tensor_tensor(pl[:, 0:n], src[:, 16:W], 1.0 / win, uc[:, 16:W], op0=ALU.mult, op1=ALU.subtract),
                 reads=skey + ["ucB"], writes=["pl"])
            b = 4 + g % 2
            s.op("pe", lambda e, b=b, n=n, g=g: e.matmul(pb[b][:, 0:n], wps[:, g, :], pl[:, 0:n], start=True, stop=True), reads=["wp", "pl"], writes=[("pb", b)])
            s.op("act", lambda e, b=b, n=n, g=g, tok0=tok0: e.activation(ypT[:, g, tok0:tok0 + n], pb[b][:, 0:n], AF.Identity, scale=psc[:, g:g + 1]),
                 reads=["psc"], writes=[("pb", b), ("ypT", g, tok0)])
    for g in range(4):
        s.dma("sp", D["YP"][g * 128:(g + 1) * 128, :], ypT[:, g, :], reads=[("ypT", g, tok0) for (tok0, n, _, _) in parts], writes=[("YP", g)])
    s.end_phase()

def phase_T(s, pb, D, BPR, even, last, w_out, w1, w2, lnp_d, pool=None, prologue=False):
    NT = BPR + 1
    TOK = NT * 128
    s.begin_phase()
    ident = s.sb("ident", [128, 128], F32)
    s.dma("sp", ident[:], D["ident"], writes=["ident"])
    rt = [s.sb("rt%d" % i, [128, 1024], F32) for i in range(2)]
    ot = [s.sb("ot%d" % i, [128, 1024], F32) for i in range(2)]
    Hs, HP, B0 = D["Hs"], D["HP"], D["B0"]

    def transposes(src, src_key, dst_fn):
        for hb in range(2):
            bank = pb[2 + hb]
            bk = ("pb", 2 + hb)
            for j in range(4):
                k = hb * 4 + j
                s.op("pe", lambda e, k=k, j=j, bank=bank: e.transpose(bank[:, j * 128:(j + 1) * 128], src[:, k * 128:(k + 1) * 128], ident[:]),
                     reads=[src_key, "ident"], writes=[bk])
            dst_fn(hb, bank, bk)

    def emit_hT(t, par):
        def dst_fn(hb, bank, bk):
            if hb == 0:
                f = lambda e: e.copy(rt[par][:, 0:512].rearrange("p (j t) -> p j t", j=4), bank[:].rearrange("p (j t) -> p j t", j=4))
                s.op("act", f, writes=[bk, ("rt", par, 0)])
            else:
                f = lambda e: e.tensor_copy(rt[par][:, 512:1024].rearrange("p (j t) -> p j t", j=4), bank[:].rearrange("p (j t) -> p j t", j=4))
                s.op("dve", f, writes=[bk, ("rt", par, 1)])
        transposes(ot[par], ("ot", par), dst_fn)
        for k in range(8):
            if t == 0:
                dst = B0[k * 128:(k + 1) * 128, :]
                wk_ = ("B0", k)
            else:
                dst = HP[k * 128:(k + 1) * 128, (t - 1) * 128:t * 128]
                wk_ = ("HP", k, t)
            s.dma("sp", dst, rt[par][:, k * 128:(k + 1) * 128], reads=[("rt", par, k // 4)], writes=[wk_])

    if prologue:
        for t in range(NT):
            par = t % 2
            s.dma("sp", ot[par][:], D["h0"][t * 128:(t + 1) * 128, :], writes=[("ot", par)])
            s.dma("sp", Hs[t * 128:(t + 1) * 128, :], ot[par][:], reads=[("ot", par)], writes=[("Hs", t)])
            emit_hT(t, par)
        finish_T(s, D, BPR)
        s.end_phase()
        return

    wo = s.sb("wo", [128, 8, 1024], BF16)
    lnp = s.sb("lnp", [128, 4, 1024], F32)
    acc = s.sb("acc", [128, NT, 1024], F32)
    h1T = s.sb("h1T", [128, 8, TOK], BF16)
    w1b = [s.sb("w1b%d" % i, [128, 8, FFB], BF16) for i in range(2)]
    w2b = [s.sb("w2b%d" % i, [128, FFB // 128, 1024], BF16) for i in range(2)]
    gT = [s.sb("gT%d" % i, [128, FFB // 128, 512], BF16) for i in range(2)]
    aT = [s.sb("aT%d" % i, [128, 512], BF16) for i in range(2)]
    ht = [s.sb("ht0", [128, 1024], F32)] * 2
    ct = [s.sb("ct%d" % i, [128, 8, 128], BF16) for i in range(2)]
    ctf = [s.sb("ctf0", [128, 8, 128], F32)] * 2
    cand = [s.sb("cand0", [128, 8, 128], F32)] * 2
    oh8 = s.sb("oh8", [128, 8], F32)
    s.dma("sp", oh8[:], D["oh8"], writes=["oh8"])
    sm = {"st": [s.sb("st%d" % i, [128, 12], F32) for i in range(2)],
          "mv": [s.sb("mv%d" % i, [128, 2], F32) for i in range(2)],
          "rstd": [s.sb("rstd%d" % i, [128, 1], F32) for i in range(2)],
          "nmr": [s.sb("nmr%d" % i, [128, 1], F32) for i in range(2)],
          "xn": [s.sb("xn%d" % i, [128, 1024], F32) for i in range(2)],
          "eps": s.sb("eps", [128, 1], F32)}
    s.op("dve", lambda e: e.memset(sm["eps"][:], LN_EPS), writes=["eps"])
    s.dma("sp", lnp[:], lnp_d, writes=["lnp"])
    s.dma("pool", wo[:], w_out.rearrange("(k p) f -> p k f", p=128), writes=["wo"])
    w1_v = w1.rearrange("(k p) f -> p k f", p=128)
    w2_v = w2.rearrange("(c p) f -> p c f", p=128)


    nh = 4 if even else 8
    for t in range(NT):
        par = t % 2
        s.dma("sp", ht[par][:], Hs[t * 128:(t + 1) * 128, :], writes=[("ht", 0)])
        for hd in range(nh):
            if even:
                J = BPR // 2
                if t == 0:
                    rows = D["GA"][(2 * hd) * 128:(2 * hd + 1) * 128, :]
                    src1 = rows[:, 8 * J * 128:(8 * J + 1) * 128]
                else:
                    rk = 2 * hd + (t % 2)
                    jst = (t // 2 - 1) if t % 2 == 0 else (t - 1) // 2
                    srcc = D["GA"][rk * 128:(rk + 1) * 128, jst * 1024:(jst + 1) * 1024]
            else:
                go = D["GO"][hd * 128:(hd + 1) * 128, :]
                if t == 0:
                    src1 = go[:, 0:128]
                else:
                    srcc = go[:, 128 + (t - 1) * 1024:128 + t * 1024]
            if t == 0:
                s.dma("sp", ctf[par][:, hd, :], src1, writes=[("ctf", 0, hd)])
            else:
                cp = 0
                s.dma("sp", cand[cp][:].rearrange("p c q -> p (c q)"), srcc, writes=[("cand", cp)])
                s.op("dve", lambda e, cp=cp, par=par, hd=hd: e.tensor_scalar_mul(ctf[par][:, hd, :], cand[cp][:, 0, :], oh8[:, 0:1]), reads=[("cand", cp), "oh8"], writes=[("ctf", 0, hd)])
                for cc in range(1, 8):
                    s.op("dve", lambda e, cp=cp, par=par, hd=hd, cc=cc: e.scalar_tensor_tensor(ctf[par][:, hd, :], cand[cp][:, cc, :], oh8[:, cc:cc + 1], ctf[par][:, hd, :],
                                                                                        op0=ALU.mult, op1=ALU.add),
                         reads=[("cand", cp), "oh8", ("ctf", 0, hd)], writes=[("ctf", 0, hd)])
        s.op("act", lambda e, par=par: e.copy(ct[par][:, 0:nh, :], ctf[par][:, 0:nh, :]), reads=[("ctf", 0, hd) for hd in range(nh)], writes=[("ct", par)])
        ckeys = [("ct", par)]
        if even:
            for g in range(4):
                s.dma("sp", ct[par][:, 4 + g, :], D["YP"][g * 128:(g + 1) * 128, t * 128:(t + 1) * 128], writes=[("ctp", par, g)])
                ckeys.append(("ctp", par, g))
        for hb in range(2):
            for k in range(8):
                s.op("pe", lambda e, k=k, hb=hb, par=par: e.matmul(pb[hb][:], ct[par][:, k, :], wo[:, k, hb * 512:(hb + 1) * 512], start=(k == 0), stop=(k == 7)),
                     reads=ckeys + ["wo"], writes=[("pb", hb)])
            s.op("dve", lambda e, hb=hb, par=par: e.scalar_tensor_tensor(rt[par][:, hb * 512:(hb + 1) * 512], ht[par][:, hb * 512:(hb + 1) * 512],
                                                                     ALPHA, pb[hb][:], op0=ALU.mult, op1=ALU.add),
                 reads=[("ht", 0)], writes=[("pb", hb), ("rt", par, hb)])
        layer_norm(s, "a", rt[par][:], [("rt", par, 0), ("rt", par, 1)], lnp[:, 0, :], lnp[:, 1, :], ot[par][:], ("ot", par), sm, par)
        s.op("act", lambda e, t=t, par=par: e.mul(acc[:, t, :], ot[par][:], ALPHA), reads=[("ot", par)], writes=[("acc", t)])

        def dst_fn(hb, bank, bk, t=t, par=par):
            if hb == 0:
                f = lambda e: e.copy(h1T[:, 0:4, t * 128:(t + 1) * 128], bank[:].rearrange("p (j t) -> p j t", j=4))
                s.op("act", f, writes=[bk, ("h1T", t, hb)])
            else:
                f = lambda e: e.tensor_copy(h1T[:, 4:8, t * 128:(t + 1) * 128], bank[:].rearrange("p (j t) -> p j t", j=4))
                s.op("dve", f, writes=[bk, ("h1T", t, hb)])
        transposes(ot[par], ("ot", par), dst_fn)

    groups = [(g * 512, 512) for g in range(NT // 4)] + ([(NT // 4 * 512, (NT % 4) * 128)] if NT % 4 else [])
    NFB = 4096 // FFB
    NC_ = FFB // 128
    steps = [(fb, gi) for fb in range(NFB) for gi in range(len(groups))]

    def load_w(fb):
        bi = fb % 2
        s.dma("pool", w1b[bi][:], w1_v[:, :, fb * FFB:(fb + 1) * FFB], writes=[("w1b", bi)])
        s.dma("pool", w2b[bi][:], w2_v[:, fb * NC_:(fb + 1) * NC_, :], writes=[("w2b", bi)])

    def stage_A(i):
        fb, gi = steps[i]
        t0, n = groups[gi]
        bi = fb % 2
        gp = i % 2
        for c in range(NC_):
            pa = 4 + (c % 2)
            for k in range(8):
                s.op("pe", lambda e, k=k, c=c, pa=pa, bi=bi, t0=t0, n=n: e.matmul(pb[pa][:, 0:n], w1b[bi][:, k, c * 128:(c + 1) * 128], h1T[:, k, t0:t0 + n],
                                                                           start=(k == 0), stop=(k == 7)),
                     reads=[("w1b", bi)] + [("h1T", t, hb) for t in range(t0 // 128, (t0 + n) // 128) for hb in range(2)], writes=[("pb", pa)])
            ap_ = c % 2
            s.op("act", lambda e, pa=pa, ap_=ap_, n=n: e.activation(aT[ap_][:, 0:n], pb[pa][:, 0:n], AF.Relu), writes=[("pb", pa), ("aT", ap_)])
            s.op("pool", lambda e, gp=gp, c=c, ap_=ap_, n=n: e.tensor_tensor(gT[gp][:, c, 0:n], aT[ap_][:, 0:n], aT[ap_][:, 0:n], op=ALU.mult),
                 reads=[("aT", ap_)], writes=[("gT", gp, c)])

    zrot = [0]

    def stage_Z(i):
        fb, gi = steps[i]
        t0, n = groups[gi]
        bi = fb % 2
        gp = i % 2
        for tt in range(n // 128):
            t = t0 // 128 + tt
            for hb in range(2):
                zb_ = [0, 1, 6, 7][zrot[0] % 4]
                zrot[0] += 1
                for c in range(NC_):
                    s.op("pe", lambda e, c=c, gp=gp, tt=tt, hb=hb, bi=bi, zb_=zb_: e.matmul(pb[zb_][:], gT[gp][:, c, tt * 128:(tt + 1) * 128], w2b[bi][:, c, hb * 512:(hb + 1) * 512],
                                                                                     start=(c == 0), stop=(c == NC_ - 1)),
                         reads=[("gT", gp, c), ("w2b", bi)], writes=[("pb", zb_)])
                s.op("dve", lambda e, t=t, hb=hb, zb_=zb_: e.tensor_tensor(acc[:, t, hb * 512:(hb + 1) * 512], acc[:, t, hb * 512:(hb + 1) * 512], pb[zb_][:], op=ALU.add),
                     reads=[("acc", t)], writes=[("pb", zb_), ("acc", t)])

    load_w(0)
    load_w(1)
    stage_A(0)
    for i in range(len(steps)):
        if i + 1 < len(steps):
            stage_A(i + 1)
        stage_Z(i)
        fb, gi = steps[i]
        if gi == len(groups) - 1 and fb + 2 < NFB:
            load_w(fb + 2)

    for t in range(NT):
        par = t % 2
        layer_norm(s, "c", acc[:, t, :], ("acc", t), lnp[:, 2, :], lnp[:, 3, :], ot[par][:], ("ot", par), sm, par)
        if t == 0:
            s.op("pool", lambda e, par=par: e.memset(ot[par][0:112, :], 0.0), reads=[("ot", par)], writes=[("ot", par)])
        if last:
            if t >= 1:
                s.dma("sp", D["out"][(t - 1) * 128:t * 128, :], ot[par][:], reads=[("ot", par)], is_out=True)
        else:
            s.dma("sp", Hs[t * 128:(t + 1) * 128, :], ot[par][:], reads=[("ot", par)], writes=[("Hs", t)])
            emit_hT(t, par)
    if not last:
        finish_T(s, D, BPR)
    s.end_phase()


def finish_T(s, D, BPR):
    HP, B0, G1, LBs, SPQ = D["HP"], D["B0"], D["G1"], D["LBs"], D["SPQ"]
    hpk = [("HP", k, t) for k in range(8) for t in range(1, BPR + 1)]
    b0k = [("B0", k) for k in range(8)]
    s.coll("AllGather", HP, G1, reads=hpk, writes=["G1"])
    W = BPR * 128
    for k in range(8):
        s.dma("sp", LBs[0:128, k * 16:(k + 1) * 16], B0[k * 128:(k + 1) * 128, 112:128], reads=b0k, writes=[("LBs", 0, k)])
        for r in range(8):
            s.dma("sp", LBs[(r + 1) * 128:(r + 2) * 128, k * 16:(k + 1) * 16], G1[r * 1024 + k * 128:r * 1024 + (k + 1) * 128, W - 16:W], reads=["G1"], writes=[("LBs", r + 1, k)])
    s.dma("sp", SPQ[0:1024, :], B0, reads=b0k, writes=["SPQ0"])

import math
I32 = mybir.dt.int32
_FPROG = {}


def build_fused(BPR):
    nc = bass.Bass("TRN2", target_bir_lowering=False)
    NT = BPR + 1
    nb = 8 * BPR + 1
    J = BPR // 2
    X = lambda n, sh, dt=F32: nc.dram_tensor(n, list(sh), dt, kind="ExternalInput").ap()
    I = lambda n, sh: nc.dram_tensor(n, list(sh), F32).ap()
    D = {"oh2": X("oh2", [128, 4]), "oh8": X("oh8", [128, 8]), "oh9": X("oh9", [128, 9]), "ident": X("ident", [128, 128]), "U": X("U", [128, 128]), "Ms": X("Ms", [128, 128]), "Mc": X("Mc", [128, 128]),
         "h0": X("h0", [NT * 128, 1024]),
         "out": nc.dram_tensor("out", [BPR * 128, 1024], F32, kind="ExternalOutput").ap(),
         "Hs": I("Hs", [NT * 128, 1024]), "HP": I("HP", [1024, BPR * 128]), "G1": I("G1", [8 * 1024, BPR * 128]), "B0": I("B0", [1024, 128]),
         "SPQ": I("SPQ", [2048, 128]), "LBs": I("LBs", [9 * 128, 128]),
         "MOe": I("MOe", [128, 9 * J * 128]), "GA": I("GA", [8 * 128, 9 * J * 128]),
         "YP": nc.dram_tensor("YP", [512, NT * 128], BF16).ap(), "MOo": I("MOo", [128, nb * 128]), "GO": I("GO", [8 * 128, nb * 128])}
    fixd = X("fix", [128, 4, 128])
    LW = []
    for i in range(4):
        j = i // 2
        d = {"w_out": X("w_out%d" % i, [1024, 1024]), "w1": X("w1_%d" % i, [1024, 4096]), "w2": X("w2_%d" % i, [4096, 1024]), "lnp": X("lnp%d" % i, [128, 4, 1024])}
        if i % 2 == 0:
            d["E"] = {n: X("%s_e%d" % (n, j), sh) for n, sh in
                      [("wq", [1024, 128]), ("wk", [1024, 128]), ("wv", [1024, 128]), ("near0", [128, 3, 128]), ("near", [128, 3, 128]), ("nearS", [128, 128]),
                       ("kbias", [128, 2]), ("lamrep", [128, 4, 64]), ("sublnw", [128, 128]), ("cst", [128, 4])]}
            d["pool"] = {"wu": X("wu_e%d" % j, [1024, 512]), "wp": X("wp_e%d" % j, [128, 512]), "pscale": X("pscale_e%d" % j, [128, 4]), "fix": fixd}
        else:
            d["O"] = {n: X("%s_o%d" % (n, j), sh) for n, sh in
                      [("wq", [1024, 128]), ("wk", [1024, 128]), ("wv", [1024, 128]), ("wz", [1024, 128]), ("wba", [1024, 2]), ("convw", [128, 3, 4]),
                       ("avec", [128, 2]), ("normw", [128, 128])]}
        LW.append(d)
    s = Sched(nc)
    pb = [s.ps("pb%d" % i, [128, 512]) for i in range(8)]
    s.begin_phase()
    z = s.sb("zt", [128, 128], F32)
    s.op("dve", lambda e: e.memset(z[:], 0.0), writes=["z"])
    for k in range(8):
        s.dma("sp", D["SPQ"][1024 + k * 128:1024 + (k + 1) * 128, :], z[:], reads=["z"], writes=[("SPQ1", k)])
    s.end_phase()
    import os
    STOP = 99
    ph = [0]
    def go():
        ph[0] += 1
        return ph[0] <= STOP
    if go():
        phase_T(s, pb, D, BPR, False, False, None, None, None, None, prologue=True)
    for i in range(4):
        d = LW[i]
        if i % 2 == 0:
            if go():
                phase_E(s, pb, D, BPR, d["E"])
        else:
            if go():
                phase_O(s, pb, D, BPR, d["O"])
        if i % 2 == 0 and go():
            phase_P(s, pb, D, BPR, d["pool"])
        if go():
            phase_T(s, pb, D, BPR, i % 2 == 0, i == 3, d["w_out"], d["w1"], d["w2"], d["lnp"], pool=d.get("pool"))
    s.finish(); s.emit()
    return nc


def fused_inputs(BPR, x, meta_tokens, rel_bias, ev_w_in, ev_lambda, ev_subln_w, ev_pool_w, ev_pool_scale, ev_w_out,
                 od_w_in, od_conv_w, od_a_log, od_dt_bias, od_norm_w, od_w_out, mlp_w1, mlp_w2, ln_mix_g, ln_mix_b, ln_mlp_g, ln_mlp_b):
    f32 = np.float32
    A = lambda a: np.ascontiguousarray(np.asarray(a, dtype=f32))
    x = A(x)[0]
    idx = np.arange(128)
    common = {"ident": np.eye(128, dtype=f32), "U": (idx[:, None] <= idx[None, :]).astype(f32), "Ms": (idx[None, :] > idx[:, None]).astype(f32),
              "Mc": (idx[None, :] >= idx[:, None]).astype(f32)}
    fix = np.ones((128, 4, 128), f32)
    p = idx - 112
    for g, win in enumerate(POOL_WINDOWS):
        fix[:, g, :] = np.where(p >= 0, win / np.minimum(np.maximum(p, 0) + 1, win), 1.0).astype(f32)[None, :]
    common["fix"] = fix
    for i in range(4):
        common["w_out%d" % i] = A(ev_w_out[i // 2] if i % 2 == 0 else od_w_out[i // 2])
        common["w1_%d" % i] = A(mlp_w1[i]); common["w2_%d" % i] = A(mlp_w2[i])
        common["lnp%d" % i] = A(np.broadcast_to(np.stack([A(ln_mix_g[i]), A(ln_mix_b[i]), A(ln_mlp_g[i]), A(ln_mlp_b[i])])[None], (128, 4, 1024)))
    for j in range(2):
        common["wu_e%d" % j] = A(A(ev_w_in[j])[:, 1536:2048])
        common["wp_e%d" % j] = A(A(ev_pool_w[j]).transpose(1, 0, 2).reshape(128, 512))
        common["pscale_e%d" % j] = A(A(ev_pool_scale[j]).reshape(4, 128).T)
    rb_all = A(rel_bias)
    ims = []
    allneg = np.full((128, 128), NEG, f32)
    for c in range(8):
        hd, half = c // 2, c % 2
        m = dict(common)
        oh2 = np.zeros((128, 4), f32); oh2[:, 0] = float(half == 1); oh2[:, 1] = float(half == 0)
        oh8 = np.zeros((128, 8), f32); oh8[:, c] = 1.0
        oh9 = np.zeros((128, 9), f32); oh9[:, c] = 1.0
        m["oh2"] = oh2; m["oh8"] = oh8; m["oh9"] = oh9
        h0 = np.zeros(((BPR + 1) * 128, 1024), f32)
        h0[112:128] = A(meta_tokens)
        h0[128:] = x[c * BPR * 128:(c + 1) * BPR * 128]
        m["h0"] = h0
        rb = rb_all[:, hd]
        if half == 0:
            near = np.stack([bias_tile(rb, 6, 4), bias_tile(rb, 6, 5), bias_tile(rb, 6, 6)], axis=1)
            near0 = np.stack([bias_tile(rb, 2, 0), bias_tile(rb, 2, 1), bias_tile(rb, 2, 2)], axis=1)
            nearS = bias_tile(rb, 0, 0)
        else:
            near = np.stack([bias_tile(rb, 5, 4), bias_tile(rb, 5, 5), allneg], axis=1)
            near0 = np.stack([bias_tile(rb, 1, 0), bias_tile(rb, 1, 1), allneg], axis=1)
            nearS = np.zeros((128, 128), f32)
        kbias = np.empty((128, 2), f32)
        kbias[:, 0] = rb[31]
        kbias[:, 1] = np.where(idx < 112, f32(NEG), rb[31])
        for j in range(2):
            w_in = A(ev_w_in[j])
            lambda_init = 0.8 - 0.6 * math.exp(-0.3 * (2 * j))
            m["wq_e%d" % j] = A(w_in[:, hd * 128:(hd + 1) * 128]); m["wk_e%d" % j] = A(w_in[:, 512 + hd * 128:512 + (hd + 1) * 128])
            m["wv_e%d" % j] = A(w_in[:, 1024 + hd * 128:1024 + (hd + 1) * 128])
            m["near0_e%d" % j] = A(near0); m["near_e%d" % j] = A(near); m["nearS_e%d" % j] = A(nearS); m["kbias_e%d" % j] = kbias
            m["lamrep_e%d" % j] = A(np.broadcast_to(A(ev_lambda[j])[None], (128, 4, 64)))
            m["sublnw_e%d" % j] = A(np.broadcast_to(A(ev_subln_w[j])[None], (128, 128)))
            m["cst_e%d" % j] = A(np.broadcast_to(np.array([lambda_init, 1.0 - lambda_init, 1e-6, 0.0], f32)[None], (128, 4)))
            po = prep_O(c, None, A(od_w_in[j]), A(od_conv_w[j]), A(od_a_log[j]), A(od_dt_bias[j]), A(od_norm_w[j]))
            for n in ("wq", "wk", "wv", "wz", "wba", "convw", "avec", "normw"):
                m["%s_o%d" % (n, j)] = po[n]
        ims.append(m)
    return ims


def kernel_fused(BPR, **inp):
    if BPR not in _FPROG:
        _FPROG[BPR] = build_fused(BPR)
    ims = fused_inputs(BPR, **inp)
    res = run_bass_kernel_spmd(_FPROG[BPR], ims, core_ids=list(range(8)))
    return np.ascontiguousarray(np.concatenate([res.results[c]["out"] for c in range(8)], 0)[None])

def kernel(**inputs):
    return kernel_fused(16, **inputs)
```
